# Optimizing a Trainium2 kernel written in Bass

```python
import math
import jax, jax.numpy as jnp
from jax import lax
import numpy as np

D_MODEL = 2048
BATCH = 8
SEQ = 2048
DEPTH = 4
DEC_BATCH = 8
DEC_SEQ = 4096
PAST_LEN = 128

D_MIX = D_MODEL
SSD_WIDTH = D_MIX // 2
ATT_WIDTH = D_MIX // 4
HY_WIDTH = D_MIX - SSD_WIDTH - ATT_WIDTH

SSD_HEAD_DIM = 64
SSD_HEADS = SSD_WIDTH // SSD_HEAD_DIM
SSD_GROUPS = 2
SSD_STATE = 128
SSD_CONV = 5
SSD_CHUNK = 128
SSD_XBC = SSD_WIDTH + 2 * SSD_GROUPS * SSD_STATE

ATT_HEAD_DIM = 64
ATT_HEADS = ATT_WIDTH // ATT_HEAD_DIM
ATT_KV_HEADS = 2
ATT_WINDOW = 128
ATT_BLOCK = 128
REL_BUCKETS = 32
REL_MAX_DIST = 128

HY_CONV = 3
HY_EMB_BANDS = 16
HY_EMB = 1 + 2 * HY_EMB_BANDS
HY_FF = 64
HY_FAST_DECAY = 0.3
HY_SLOW_DECAY = 1.5
HY_DECAY_TARGET = 1e-2

EPS = 1e-6
IN_SIZES = [SSD_WIDTH, SSD_XBC, 2 * SSD_HEADS,
            ATT_WIDTH, ATT_KV_HEADS * ATT_HEAD_DIM, ATT_KV_HEADS * ATT_HEAD_DIM, ATT_WIDTH,
            3 * HY_WIDTH, HY_WIDTH]
IN_COLS = sum(IN_SIZES)
IN_SPLITS = [int(v) for v in np.cumsum(IN_SIZES)[:-1]]

kernel_name = 'hymba_ssd_window_hyena_encoder'

F32 = jnp.float32


def rms_norm(x, g):
    xf = x.astype(F32)
    y = xf * lax.rsqrt(jnp.mean(xf * xf, axis=-1, keepdims=True) + EPS)
    return (y * g.astype(F32)).astype(x.dtype)


def centred_dwconv(x, w, b):
    width = w.shape[0]
    y = lax.conv_general_dilated(x, w[:, None, :].astype(x.dtype), window_strides=(1,),
                                 padding=[(width // 2, width // 2)],
                                 dimension_numbers=('NWC', 'WIO', 'NWC'),
                                 feature_group_count=x.shape[-1])
    return y + b.astype(x.dtype)


def ssd_scan(x, dt, a, bm, cm):
    bsz, L, H, P = x.shape
    G, N = bm.shape[2], bm.shape[3]
    E = H // G
    nc = L // SSD_CHUNK
    x = x.reshape(bsz, nc, SSD_CHUNK, G, E, P)
    dt = dt.reshape(bsz, nc, SSD_CHUNK, G, E)
    bm = bm.reshape(bsz, nc, SSD_CHUNK, G, N)
    cm = cm.reshape(bsz, nc, SSD_CHUNK, G, N)
    xdt = x * dt[..., None]
    a_cs = jnp.cumsum(jnp.transpose(dt * a.reshape(G, E), (0, 1, 3, 4, 2)), axis=-1)
    seg = a_cs[..., :, None] - a_cs[..., None, :]
    tril = np.tril(np.ones((SSD_CHUNK, SSD_CHUNK), dtype=bool))
    lmat = jnp.exp(jnp.where(tril, seg, -jnp.inf))
    cb = jnp.einsum('bclgn,bcsgn->bcgls', cm, bm)
    y_diag = jnp.einsum('bcgels,bcsgep->bclgep', cb[:, :, :, None] * lmat, xdt)
    decay_states = jnp.exp(a_cs[..., -1:] - a_cs)
    states = jnp.einsum('bclgn,bcgel,bclgep->bcgepn', bm, decay_states, xdt)
    chunk_decay = jnp.exp(a_cs[..., -1])

    def step(h, inp):
        s, d = inp
        return h * d[..., None, None] + s, h

    h0 = jnp.zeros((bsz, G, E, P, N), F32)
    _, prev = lax.scan(step, h0, (jnp.moveaxis(states, 1, 0), jnp.moveaxis(chunk_decay, 1, 0)))
    prev = jnp.moveaxis(prev, 0, 1)
    y_off = jnp.einsum('bclgn,bcgepn,bcgel->bclgep', cm, prev, jnp.exp(a_cs))
    return (y_diag + y_off).reshape(bsz, L, H, P)


def ssd_branch(z, xbc, dt_raw, conv_w, conv_b, dt_bias, a_log, d_skip, norm_g):
    bsz, L, _ = xbc.shape
    xbc = jax.nn.silu(centred_dwconv(xbc, conv_w, conv_b)).astype(F32)
    xs, bm, cm = jnp.split(xbc, [SSD_WIDTH, SSD_WIDTH + SSD_GROUPS * SSD_STATE], axis=-1)
    xs = xs.reshape(bsz, L, SSD_HEADS, SSD_HEAD_DIM)
    bm = bm.reshape(bsz, L, SSD_GROUPS, SSD_STATE)
    cm = cm.reshape(bsz, L, SSD_GROUPS, SSD_STATE)
    dt = jax.nn.softplus(dt_raw.astype(F32).reshape(bsz, L, 2, SSD_HEADS) + dt_bias.astype(F32))
    a = -jnp.exp(a_log.astype(F32))
    flip = lambda t: jnp.flip(t, axis=1)
    y_f = ssd_scan(xs, dt[:, :, 0], a[0], bm, cm)
    y_b = flip(ssd_scan(flip(xs), flip(dt[:, :, 1]), a[1], flip(bm), flip(cm)))
    y = (y_f + y_b + xs * d_skip.astype(F32)[:, None]).reshape(bsz, L, SSD_WIDTH)
    return rms_norm(y * jax.nn.silu(z.astype(F32)), norm_g).astype(z.dtype)


def t5_buckets(rel):
    nb = REL_BUCKETS // 2
    max_exact = nb // 2
    ret = (rel > 0).astype(np.int32) * nb
    n = np.abs(rel)
    large = max_exact + (np.log(np.maximum(n, 1) / max_exact) / math.log(REL_MAX_DIST / max_exact)
                         * (nb - max_exact)).astype(np.int32)
    large = np.minimum(large, nb - 1)
    return ret + np.where(n < max_exact, n, large)


def attention_branch(q, k, v, gate, rel_bias, sink, norm_g):
    bsz, L, _ = q.shape
    nb = L // ATT_BLOCK
    R = ATT_HEADS // ATT_KV_HEADS
    qb = q.reshape(bsz, nb, ATT_BLOCK, ATT_KV_HEADS, R, ATT_HEAD_DIM)

    def band(t):
        t = t.reshape(bsz, L, ATT_KV_HEADS, ATT_HEAD_DIM)
        t = jnp.pad(t, ((0, 0), (ATT_BLOCK, ATT_BLOCK), (0, 0), (0, 0)))
        t = t.reshape(bsz, nb + 2, ATT_BLOCK, ATT_KV_HEADS, ATT_HEAD_DIM)
        return jnp.concatenate([t[:, :-2], t[:, 1:-1], t[:, 2:]], axis=2)

    kb, vb = band(k), band(v)
    qi = np.arange(ATT_BLOCK)[:, None]
    kj = np.arange(3 * ATT_BLOCK)[None, :]
    rel = kj - ATT_BLOCK - qi
    bias = rel_bias.astype(F32)[t5_buckets(rel)]
    bias = jnp.transpose(bias, (2, 0, 1)).reshape(ATT_KV_HEADS, R, ATT_BLOCK, 3 * ATT_BLOCK)
    kpos = (np.arange(nb)[:, None, None] - 1) * ATT_BLOCK + kj[None]
    mask = (np.abs(rel) <= ATT_WINDOW)[None] & (kpos >= 0) & (kpos < L)
    s = jnp.einsum('bnqgrd,bnkgd->bngrqk', qb, kb, preferred_element_type=F32) * (ATT_HEAD_DIM ** -0.5) + bias
    s = jnp.where(mask[None, :, None, None], s, -jnp.inf)
    sk = sink.astype(F32).reshape(ATT_KV_HEADS, R)[None, None, :, :, None, None]
    m = jnp.maximum(jnp.max(s, axis=-1, keepdims=True), sk)
    p = jnp.exp(s - m)
    den = jnp.sum(p, axis=-1, keepdims=True) + jnp.exp(sk - m)
    o = jnp.einsum('bngrqk,bnkgd->bnqgrd', (p / den).astype(vb.dtype), vb)
    o = o.reshape(bsz, L, ATT_WIDTH).astype(F32)
    return rms_norm(o * jax.nn.silu(gate.astype(F32)), norm_g).astype(gate.dtype)


def hyena_filters(L, w1, b1, w2, b2, w3, b3, w4, freq):
    t = jnp.linspace(0.0, 1.0, L, dtype=F32)[:, None]
    pos = jnp.arange(L, dtype=F32)[:, None]
    bands = jnp.linspace(1e-4, HY_EMB_BANDS - 1, HY_EMB_BANDS, dtype=F32)[None, :]
    ang = 2.0 * math.pi * pos * bands / L
    zemb = jnp.concatenate([t, jnp.cos(ang), -jnp.sin(ang)], axis=-1)
    fr = freq.astype(F32)
    h = jnp.sin(fr * (zemb @ w1.astype(F32) + b1.astype(F32)))
    h = jnp.sin(fr * (h @ w2.astype(F32) + b2.astype(F32)))
    h = jnp.sin(fr * (h @ w3.astype(F32) + b3.astype(F32)))
    h = h @ w4.astype(F32)
    max_decay = math.log(HY_DECAY_TARGET) / HY_FAST_DECAY
    min_decay = math.log(HY_DECAY_TARGET) / HY_SLOW_DECAY
    deltas = jnp.linspace(min_decay, max_decay, HY_WIDTH, dtype=F32)
    decay = jnp.exp(-t * jnp.abs(deltas))
    h = h.reshape(L, 2, HY_WIDTH) * decay[:, None, :]
    return h[:, 0], h[:, 1]


def hyena_branch(proj, gate, conv_w, conv_b, w1, b1, w2, b2, w3, b3, w4, freq, d_bias, norm_g):
    bsz, L, _ = proj.shape
    proj = centred_dwconv(proj, conv_w, conv_b).astype(F32)
    xa, xb, v = jnp.split(proj, 3, axis=-1)
    h_f, h_b = hyena_filters(L, w1, b1, w2, b2, w3, b3, w4, freq)
    kern = jnp.concatenate([h_f, jnp.zeros((1, HY_WIDTH), F32), h_b[1:][::-1]], axis=0)
    u = xb * v
    y = jnp.fft.irfft(jnp.fft.rfft(u, n=2 * L, axis=1) * jnp.fft.rfft(kern, axis=0)[None],
                      n=2 * L, axis=1)[:, :L]
    y = xa * (y + u * d_bias.astype(F32))
    return rms_norm(y * jax.nn.silu(gate.astype(F32)), norm_g).astype(gate.dtype)


def encoder_layer(x, rel_bias, norm_g, w_in, ssd_conv_w, ssd_conv_b, ssd_dt_bias, ssd_a_log, ssd_d,
                  ssd_norm_g, att_sink, att_norm_g, hy_conv_w, hy_conv_b, hy_w1, hy_b1, hy_w2, hy_b2,
                  hy_w3, hy_b3, hy_w4, hy_freq, hy_d, hy_norm_g, w_out):
    h = rms_norm(x, norm_g)
    u = h @ w_in.astype(h.dtype)
    z, xbc, dt_raw, q, k, v, g_att, hy_proj, g_hy = jnp.split(u, IN_SPLITS, axis=-1)
    y_ssd = ssd_branch(z, xbc, dt_raw, ssd_conv_w, ssd_conv_b, ssd_dt_bias, ssd_a_log, ssd_d, ssd_norm_g)
    y_att = attention_branch(q, k, v, g_att, rel_bias, att_sink, att_norm_g)
    y_hy = hyena_branch(hy_proj, g_hy, hy_conv_w, hy_conv_b, hy_w1, hy_b1, hy_w2, hy_b2, hy_w3, hy_b3,
                        hy_w4, hy_freq, hy_d, hy_norm_g)
    y = jnp.concatenate([y_ssd, y_att, y_hy], axis=-1) @ w_out.astype(h.dtype)
    return x + y.astype(x.dtype)


def setup_inputs(seed: int = 0) -> dict:
    key = jax.random.key(seed)
    ks = jax.random.split(key, 32)

    def nrm(k, shape, scale):
        return jax.random.normal(k, shape, F32) * scale

    dt0 = jnp.exp(jax.random.uniform(ks[7], (DEPTH, 2, SSD_HEADS), F32, math.log(1e-3), math.log(1e-1)))
    return {
        'x_prompt': nrm(ks[0], (BATCH, SEQ, D_MODEL), 1.0),
        'x_sample': nrm(ks[1], (DEC_BATCH, DEC_SEQ, D_MODEL), 1.0),
        'rel_bias': nrm(ks[2], (REL_BUCKETS, ATT_HEADS), 0.5),
        'norm_g': 1.0 + nrm(ks[3], (DEPTH, D_MODEL), 0.02),
        'w_in': nrm(ks[4], (DEPTH, D_MODEL, IN_COLS), D_MODEL ** -0.5),
        'ssd_conv_w': nrm(ks[5], (DEPTH, SSD_CONV, SSD_XBC), SSD_CONV ** -0.5),
        'ssd_conv_b': nrm(ks[6], (DEPTH, SSD_XBC), 0.02),
        'ssd_dt_bias': dt0 + jnp.log(-jnp.expm1(-dt0)),
        'ssd_a_log': jnp.log(jax.random.uniform(ks[8], (DEPTH, 2, SSD_HEADS), F32, 1.0, 16.0)),
        'ssd_d': 1.0 + nrm(ks[9], (DEPTH, SSD_HEADS), 0.1),
        'ssd_norm_g': 1.0 + nrm(ks[10], (DEPTH, SSD_WIDTH), 0.02),
        'att_sink': nrm(ks[11], (DEPTH, ATT_HEADS), 0.5),
        'att_norm_g': 1.0 + nrm(ks[12], (DEPTH, ATT_WIDTH), 0.02),
        'hy_conv_w': nrm(ks[13], (DEPTH, HY_CONV, 3 * HY_WIDTH), HY_CONV ** -0.5),
        'hy_conv_b': nrm(ks[14], (DEPTH, 3 * HY_WIDTH), 0.02),
        'hy_w1': nrm(ks[15], (DEPTH, HY_EMB, HY_FF), HY_EMB ** -0.5),
        'hy_b1': nrm(ks[16], (DEPTH, HY_FF), 0.02),
        'hy_w2': nrm(ks[17], (DEPTH, HY_FF, HY_FF), HY_FF ** -0.5),
        'hy_b2': nrm(ks[18], (DEPTH, HY_FF), 0.02),
        'hy_w3': nrm(ks[19], (DEPTH, HY_FF, HY_FF), HY_FF ** -0.5),
        'hy_b3': nrm(ks[20], (DEPTH, HY_FF), 0.02),
        'hy_w4': nrm(ks[21], (DEPTH, HY_FF, 2 * HY_WIDTH), HY_FF ** -0.5),
        'hy_freq': 1.0 + nrm(ks[22], (DEPTH, HY_FF), 0.1),
        'hy_d': nrm(ks[23], (DEPTH, HY_WIDTH), 1.0),
        'hy_norm_g': 1.0 + nrm(ks[24], (DEPTH, HY_WIDTH), 0.02),
        'w_out': nrm(ks[25], (DEPTH, D_MIX, D_MODEL), D_MIX ** -0.5),
        'final_norm_g': 1.0 + nrm(ks[26], (D_MODEL,), 0.02),
    }


def reference(x_prompt, x_sample, rel_bias, norm_g, w_in, ssd_conv_w, ssd_conv_b, ssd_dt_bias, ssd_a_log,
              ssd_d, ssd_norm_g, att_sink, att_norm_g, hy_conv_w, hy_conv_b, hy_w1, hy_b1, hy_w2, hy_b2,
              hy_w3, hy_b3, hy_w4, hy_freq, hy_d, hy_norm_g, w_out, final_norm_g):
    def trunk(x):
        for i in range(DEPTH):
            x = encoder_layer(x, rel_bias, norm_g[i], w_in[i], ssd_conv_w[i], ssd_conv_b[i], ssd_dt_bias[i],
                              ssd_a_log[i], ssd_d[i], ssd_norm_g[i], att_sink[i], att_norm_g[i],
                              hy_conv_w[i], hy_conv_b[i], hy_w1[i], hy_b1[i], hy_w2[i], hy_b2[i],
                              hy_w3[i], hy_b3[i], hy_w4[i], hy_freq[i], hy_d[i], hy_norm_g[i], w_out[i])
        return rms_norm(x, final_norm_g)

    y_prompt = trunk(x_prompt)
    y_sample = trunk(x_sample)
    return (y_prompt, y_sample)
```

```python
import math
import os
from contextlib import ExitStack
import numpy as np
import concourse.bass as bass
import concourse.mybir as mybir
from concourse.bass_utils import run_bass_kernel_spmd

F32 = mybir.dt.float32
BF16 = mybir.dt.bfloat16
I32 = mybir.dt.int32
AF = mybir.ActivationFunctionType
ALU = mybir.AluOpType
AX = mybir.AxisListType

D = 2048
DEPTH = 4
NCOL = 5920
EPS = 1e-6
C_Z, C_XBC, C_V, C_GA, C_HY, C_GH = 0, 1024, 2560, 2688, 3200, 4736
NTOK = 5248
C_DT = 5248
C_QK = 5280
UW = NTOK


class Buf:
    __slots__ = ("w", "r", "name")

    def __init__(self, name=""):
        self.w = {}
        self.r = {}
        self.name = name


class Eng:
    def __init__(self, ctx, name, e, sync_self=True):
        self.ctx, self.name, self.e = ctx, name, e
        self.sem = ctx.nc.alloc_semaphore("s_" + name)
        self.count = 0
        self.waited = {}
        self.pend_r, self.pend_w = [], []
        self.sync_self = sync_self

    def wait(self, sem, val):
        if sem is self.sem and not self.sync_self:
            return
        k = id(sem)
        if self.waited.get(k, 0) >= val:
            return
        self.waited[k] = val
        self.e.wait_ge(sem, val)


class Ctx:
    def __init__(self, nc, n_dma_sems=40):
        self.nc = nc
        self.pe = Eng(self, "pe", nc.tensor, sync_self=False)
        self.act = Eng(self, "act", nc.scalar)
        self.dve = Eng(self, "dve", nc.vector)
        self.pool = Eng(self, "pool", nc.gpsimd)
        self.sp = Eng(self, "sp", nc.sync)
        self.dsem = [[nc.alloc_semaphore(f"dma{i}"), 0] for i in range(n_dma_sems)]
        self.dnext = 0
        self.out_tokens = []

    def _deps(self, reads, writes):
        deps = {}
        for b in reads:
            for k, (s, v) in b.w.items():
                if deps.get(k, (None, 0))[1] < v:
                    deps[k] = (s, v)
        for b in writes:
            for dd in (b.w, b.r):
                for k, (s, v) in dd.items():
                    if deps.get(k, (None, 0))[1] < v:
                        deps[k] = (s, v)
        return deps

    @staticmethod
    def _publish(tok, reads, writes):
        k = id(tok[0])
        for b in reads:
            if b.r.get(k, (None, 0))[1] < tok[1]:
                b.r[k] = tok
        for b in writes:
            b.w = {k: tok}
            b.r = {}

    def op(self, eng, fn, reads=(), writes=(), inc=True):
        for s, v in self._deps(reads, writes).values():
            eng.wait(s, v)
        ins = fn(eng.e)
        if not inc:
            eng.pend_r += list(reads)
            eng.pend_w += list(writes)
            return ins
        eng.count += 1
        ins.then_inc(eng.sem, 1)
        tok = (eng.sem, eng.count)
        self._publish(tok, list(reads) + eng.pend_r, list(writes) + eng.pend_w)
        eng.pend_r, eng.pend_w = [], []
        return ins

    def dma(self, q, out, in_, reads=(), writes=(), final=False, **kw):
        slot = self.dsem[self.dnext]
        self.dnext = (self.dnext + 1) % len(self.dsem)
        sem = slot[0]
        for s, v in self._deps(reads, writes).values():
            q.wait(s, v)
        if slot[1]:
            q.wait(sem, slot[1])
        slot[1] += 16
        q.e.dma_start(out=out, in_=in_, **kw).then_inc(sem, 16)
        tok = (sem, slot[1])
        self._publish(tok, reads, writes)
        if final:
            self.out_tokens.append(tok)

    def barrier(self):
        import os
        if os.environ.get('NOBAR'):
            return
        engs = [self.pe, self.act, self.dve, self.pool, self.sp]
        toks = [(e.sem, e.count) for e in engs if e.count] + [(s[0], s[1]) for s in self.dsem if s[1]]
        for e in engs:
            for s, v in toks:
                if s is not e.sem:
                    e.wait(s, v)

    def finish(self):
        for s, v in self.out_tokens:
            self.sp.wait(s, v)
        for slot in self.dsem:
            if slot[1]:
                self.sp.wait(slot[0], slot[1])


class T:
    def __init__(self, nc, name, shape, dtype, psum=False):
        self.t = (nc.alloc_psum_tensor if psum else nc.alloc_sbuf_tensor)(name, shape, dtype)
        self.b = Buf(name)

    def __getitem__(self, k):
        return self.t[k]


def _t5_buckets(rel):
    nb = 16
    max_exact = 8
    ret = (rel > 0).astype(np.int32) * nb
    n = np.abs(rel)
    large = max_exact + (np.log(np.maximum(n, 1) / max_exact) / math.log(128 / max_exact)
                         * (nb - max_exact)).astype(np.int32)
    large = np.minimum(large, nb - 1)
    return ret + np.where(n < max_exact, n, large)


def host_consts():
    c = {}
    c["ident"] = np.eye(128, dtype=np.float32)
    m = np.arange(512)
    rel = 255 - m
    bk = _t5_buckets(rel)
    valid = (np.abs(rel) <= 128) & (m < 511)
    oh8 = np.zeros((128, 512), np.float32)
    oh8[bk[valid], m[valid]] = 8.0
    c["oh8"] = oh8
    c["mneg"] = np.ascontiguousarray(np.where(valid, 0.0, -30000.0).astype(np.float32)[None].repeat(8, 0))
    c["antid"] = np.ascontiguousarray(np.eye(128, dtype=np.float32)[::-1])
    j = np.arange(128)[:, None]
    l = np.arange(128)[None, :]
    c["ssd_masks"] = np.ascontiguousarray(np.stack([j <= l, j >= l, j > l, j < l, np.ones((128, 128), bool)], 1).astype(np.float32))
    return c


def hy_consts(L):
    import ml_dtypes
    bf = ml_dtypes.bfloat16
    A = L // 64
    N = 2 * L
    KA = 2 * A
    c = {}
    a = np.arange(KA)[:, None]
    ka = np.arange(KA)[None, :]
    th = 2 * np.pi * a * ka / KA
    FA = np.zeros((128, 2, 128))
    FA[:KA, 0, :KA] = np.cos(th)
    FA[:KA, 1, :KA] = -np.sin(th)
    b = np.arange(64)[:, None]
    kb = np.arange(64)[None, :]
    GB = np.zeros((KA, 128, 128))
    GBsw = np.zeros((KA, 128, 128))
    GP1 = np.zeros((KA, 128, 128))
    GP2 = np.zeros((KA, 128, 128))
    for k in range(KA):
        t = 2 * np.pi * (b * k / N + b * kb / 64)
        Gr, Gi = np.cos(t), -np.sin(t)
        GB[k] = np.block([[Gr, Gi], [-Gi, Gr]])
        GBsw[k] = np.block([[Gi, Gr], [Gr, -Gi]])
        Pr, Pi = np.cos(t.T), np.sin(t.T)
        GP1[k] = np.block([[Pr, Pi], [-Pr, -Pi]])
        GP2[k] = np.block([[-Pi, Pr], [-Pi, Pr]])
    th2 = 2 * np.pi * np.arange(A)[None, :] * np.arange(KA)[:, None] / KA
    wk = np.full((KA, 1), 2.0)
    wk[0] = 1.0
    wk[A] = 1.0
    wk[A + 1:] = 0.0
    FP = np.zeros((128, 2, 128))
    FP[:KA, 0, :A] = wk * np.cos(th2) / N
    FP[:KA, 1, :A] = -wk * np.sin(th2) / N
    for nm, v in (("FA", FA), ("FP", FP), ("GB", GB), ("GBsw", GBsw), ("GP1", GP1), ("GP2", GP2)):
        c[f"hy_{nm}{L}"] = np.ascontiguousarray(v.astype(np.float32).astype(bf))
    tt = np.linspace(0.0, 1.0, L, dtype=np.float32)[:, None]
    pos = np.arange(L, dtype=np.float32)[:, None]
    bands = np.linspace(1e-4, 15, 16, dtype=np.float32)[None, :]
    ang = (2.0 * math.pi * pos * bands / L).astype(np.float32)
    zemb = np.concatenate([tt, np.cos(ang), -np.sin(ang)], axis=-1)
    zp = np.zeros((128, L), np.float32)
    zp[:33] = zemb.T
    c[f"hy_zemb{L}"] = np.ascontiguousarray(zp.astype(bf))
    max_decay = math.log(1e-2) / 0.3
    min_decay = math.log(1e-2) / 1.5
    deltas = np.linspace(min_decay, max_decay, 512, dtype=np.float32)
    c[f"hy_decay{L}"] = np.ascontiguousarray(np.exp(-tt * np.abs(deltas)[None, :]).astype(np.float32))
    return c


def build(seqs, depth=DEPTH, dbg_out=(), dbg_in=(), phases=("inproj", "ssd", "att", "hy", "final"), stop=99):
    nc = bass.Bass("TRN2", target_bir_lowering=False)
    cx = Ctx(nc)
    pe, act, dve, pool, sp = cx.pe, cx.act, cx.dve, cx.pool, cx.sp

    def dram(name, shape, dt, kind="Internal"):
        if name in dbg_out:
            kind = "ExternalOutput"
        if name in dbg_in:
            kind = "ExternalInput"
        return nc.dram_tensor(name, list(shape), dt, kind=kind).ap()

    nseq = len(seqs)
    uid = [0]

    def sbt(name, shape, dt):
        uid[0] += 1
        return nc.sbuf_tensor(f"{name}_{uid[0]}", shape, dt)

    xin = [dram(f"x{s}", [L, D], F32, "ExternalInput") for s, L in enumerate(seqs)]
    yout = [dram(f"y{s}", [L, D], F32, "ExternalOutput") for s, L in enumerate(seqs)]
    w_in = dram("w_in", [depth, D, NCOL], F32, "ExternalInput")
    w_out = dram("w_out", [depth, D, D], F32, "ExternalInput")
    norm_g = dram("norm_g", [depth, D], F32, "ExternalInput")
    brg = dram("brg", [depth, D], F32, "ExternalInput")
    fin_g = dram("fin_g", [1, D], F32, "ExternalInput")
    ident_d = dram("ident", [128, 128], F32, "ExternalInput")

    PAD = 2
    xs_ = [[dram(f"xs{s}_{i}", [L, D], F32) for i in range(2)] for s, L in enumerate(seqs)]
    u_ = [dram(f"u{s}", [L + 2 * PAD, UW], BF16) for s, L in enumerate(seqs)]
    dtr_ = [dram(f"dtr{s}", [L, 32], F32) for s, L in enumerate(seqs)]
    qk_ = [dram(f"qk{s}", [640, L], BF16) for s, L in enumerate(seqs)]
    ybr_ = [dram(f"ybr{s}", [L, D], BF16) for s, L in enumerate(seqs)]
    xb_ = [[[Buf() for _ in range(L // 128)] for i in range(2)] for L in seqs]
    ub_ = [[Buf() for _ in range(L // 128 + 2)] for L in seqs]
    dtb_ = [[Buf() for _ in range(L // 128)] for L in seqs]
    qkb_ = [[Buf() for _ in range(L // 512)] for L in seqs]
    ybb_ = [[[Buf() for _ in range(3)] for _ in range(L // 128)] for L in seqs]

    dtall = [T(nc, f"dtall{s}", [128, L // 128, 32], F32) for s, L in enumerate(seqs)]
    ident_f = T(nc, "ident_f", [128, 128], F32)
    ident_b = T(nc, "ident_b", [128, 128], BF16)
    cx.dma(sp, ident_f[:], ident_d[:, :], writes=[ident_f.b])
    cx.op(dve, lambda e: e.tensor_copy(out=ident_b[:], in_=ident_f[:]), [ident_f.b], [ident_b.b])
    antid_d = dram("antid", [128, 128], F32, "ExternalInput")
    anti_f = T(nc, "anti_f", [128, 128], F32)
    anti_b = T(nc, "anti_b", [128, 128], BF16)
    cx.dma(sp, anti_f[:], antid_d[:, :], writes=[anti_f.b])
    cx.op(dve, lambda e: e.tensor_copy(out=anti_b[:], in_=anti_f[:]), [anti_f.b], [anti_b.b])
    zrow = T(nc, "zrow", [128, 82], BF16)
    cx.op(dve, lambda e: e.memset(zrow[:], 0.0), [], [zrow.b])
    for s, L in enumerate(seqs):
        cx.dma(sp, u_[s][0:PAD, :].rearrange("a (p f) -> (a p) f", f=82), zrow[:], reads=[zrow.b], writes=[ub_[s][0]])
        cx.dma(sp, u_[s][L + PAD:L + 2 * PAD, :].rearrange("a (p f) -> (a p) f", f=82), zrow[:], reads=[zrow.b],
               writes=[ub_[s][-1]])

    banks = [T(nc, f"bank{i}", [128, 512], F32, psum=True) for i in range(8)]

    def phase_inproj(layer, s, L, xsrc, xsrc_b):
        MG = 2048 if L % 2048 == 0 else L
        ntile = MG // 128
        with ExitStack() as es:
            hT_t = es.enter_context(sbt("hT", [128, 16, MG], BF16))
            wst0 = es.enter_context(sbt("wst0", [128, 8, 512], F32))
            wst1 = es.enter_context(sbt("wst1", [128, 8, 512], F32))
            wb0 = es.enter_context(sbt("wb0", [128, 16, 512], BF16))
            wb1 = es.enter_context(sbt("wb1", [128, 16, 512], BF16))
            xt0 = es.enter_context(sbt("xt0", [128, D], F32))
            xt1 = es.enter_context(sbt("xt1", [128, D], F32))
            hb_t = es.enter_context(sbt("hb", [128, D], BF16))
            gcol_t = es.enter_context(sbt("gcol", [128, 16], F32))
            st_t = es.enter_context(sbt("st4", [128, 4], F32))
            junk_t = es.enter_context(sbt("junk", [128, D], BF16))
            ot0 = es.enter_context(sbt("ot0", [128, 512], BF16))
            ot1 = es.enter_context(sbt("ot1", [128, 512], BF16))
            ot2 = es.enter_context(sbt("ot2", [128, 512], BF16))
            ot3 = es.enter_context(sbt("ot3", [128, 512], BF16))
            dto0 = es.enter_context(sbt("dto0", [128, 32], F32))
            dto1 = es.enter_context(sbt("dto1", [128, 32], F32))
            hT, gcol, st, hb, junk = hT_t, gcol_t, st_t, hb_t, junk_t
            bhT = [Buf() for _ in range(ntile)]
            bg, bst, bhb, bjunk = Buf(), Buf(), Buf(), Buf()
            wst = [(wst0, Buf()), (wst1, Buf())]
            wb = [(wb0, Buf()), (wb1, Buf())]
            xt = [(xt0, Buf()), (xt1, Buf())]
            ot = [(o, Buf()) for o in (ot0, ot1, ot2, ot3)]
            dto = [(dto0, Buf()), (dto1, Buf())]
            cx.dma(sp, gcol[:], norm_g[layer].rearrange("(c p) -> p c", p=128), writes=[bg],
                   allow_slow_non_contiguous=True)
            wv = w_in[layer].rearrange("(c p) n -> p c n", p=128)
            cnt = {"ot": 0, "bank": 0, "ev": 0, "dto": 0, "wst": 0}
            blocks = [(c0, 512) for c0 in range(0, 5120, 512)] + [(5120, 160)] + \
                     [(C_QK + 128 * i, 128) for i in range(5)]
            items = [(mg, bi) for mg in range(L // MG) for bi in range(len(blocks))]

            def prep_mg(mg):
                t0 = mg * MG
                for ti in range(ntile):
                    gt = (t0 // 128) + ti
                    xtile, xbuf = xt[ti % 2]
                    cx.dma(sp, xtile[:], xsrc[gt * 128:(gt + 1) * 128, :], reads=[xsrc_b[gt]], writes=[xbuf])
                    cx.op(act, lambda e: e.activation(out=junk[:], in_=xtile[:], func=AF.Square,
                                                      accum_out=st[:, 0:1]), [xbuf], [bjunk, bst])
                    cx.op(dve, lambda e: e.tensor_scalar(out=st[:, 1:2], in0=st[:, 0:1], scalar1=1.0 / D,
                                                         scalar2=EPS, op0=ALU.mult, op1=ALU.add), [bst], [bst])
                    cx.op(act, lambda e: e.activation(out=st[:, 2:3], in_=st[:, 1:2], func=AF.Sqrt), [bst], [bst])
                    cx.op(dve, lambda e: e.reciprocal(out=st[:, 3:4], in_=st[:, 2:3]), [bst], [bst])
                    cx.op(act, lambda e: e.activation(out=hb[:], in_=xtile[:], func=AF.Copy, scale=st[:, 3:4]),
                          [xbuf, bst], [bhb])
                    for q4 in range(4):
                        bk = banks[2 + cnt["bank"] % 6]
                        cnt["bank"] += 1
                        for j in range(4):
                            c = q4 * 4 + j
                            cx.op(pe, lambda e: e.matmul(bk[:, j * 128:(j + 1) * 128], hb[:, c * 128:(c + 1) * 128],
                                                         ident_b[:, :], start=(j == 0), stop=(j == 3)),
                                  [bhb, ident_b.b], [bk.b], inc=(j == 3))
                        src = bk[:, :].rearrange("p (c t) -> p c t", c=4)
                        dst = hT[:, q4 * 4:(q4 + 1) * 4, ti * 128:(ti + 1) * 128]
                        if q4 % 2 == 0:
                            cx.op(act, lambda e: e.activation(out=dst, in_=src, func=AF.Copy), [bk.b], [bhT[ti]])
                        else:
                            cx.op(dve, lambda e: e.tensor_copy(out=dst, in_=src), [bk.b], [bhT[ti]])

            def load_block(k):
                mg, bi = items[k]
                c0, cw = blocks[bi]
                wbt, wbb = wb[k % 2]
                for hlf in range(2):
                    wt, wtb = wst[cnt["wst"] % 2]
                    cnt["wst"] += 1
                    cx.dma(sp, wt[:, :, 0:cw], wv[:, hlf * 8:(hlf + 1) * 8, c0:c0 + cw], writes=[wtb])
                    for c in range(8):
                        cc = hlf * 8 + c
                        if c % 8 < 5:
                            cx.op(dve, lambda e: e.tensor_scalar(out=wbt[:, cc, 0:cw], in0=wt[:, c, 0:cw],
                                                                 scalar1=gcol[:, cc:cc + 1], scalar2=None, op0=ALU.mult),
                                  [wtb, bg], [wbb])
                        else:
                            cx.op(pool, lambda e: e.tensor_scalar(out=wbt[:, cc, 0:cw], in0=wt[:, c, 0:cw],
                                                                  scalar1=gcol[:, cc:cc + 1], scalar2=0.0, op0=ALU.mult,
                                                                  op1=ALU.add), [wtb, bg], [wbb])

            def compute_block(k):
                mg, bi = items[k]
                t0 = mg * MG
                c0, cw = blocks[bi]
                wbt, wbb = wb[k % 2]
                if c0 < C_QK:
                    for ti in range(ntile):
                        gt = (t0 // 128) + ti
                        bk = banks[2 + cnt["bank"] % 6]
                        cnt["bank"] += 1
                        for c in range(16):
                            cx.op(pe, lambda e: e.matmul(bk[:, 0:cw], hT[:, c, ti * 128:(ti + 1) * 128],
                                                         wbt[:, c, 0:cw], start=(c == 0), stop=(c == 15)),
                                  [bhT[ti], wbb], [bk.b], inc=(c == 15))
                        otile, ob = ot[cnt["ot"] % 4]
                        cnt["ot"] += 1
                        nb16 = min(cw, 512) if c0 < 5120 else 128
                        cx.op(act, lambda e: e.activation(out=otile[:, 0:nb16], in_=bk[:, 0:nb16], func=AF.Copy),
                              [bk.b], [ob])
                        cx.dma(sp, u_[s][PAD + gt * 128:PAD + (gt + 1) * 128, c0:c0 + nb16], otile[:, 0:nb16],
                               reads=[ob], writes=[ub_[s][gt + 1]])
                        if c0 == 5120:
                            cx.op(dve, lambda e: e.tensor_copy(out=dtall[s][:, gt, :], in_=bk[:, 128:160]),
                                  [bk.b], [dtall[s].b])
                else:
                    qi = (c0 - C_QK) // 128
                    for tg in range(MG // 512):
                        bk = banks[2 + cnt["bank"] % 6]
                        cnt["bank"] += 1
                        tiles = [bhT[tg * 4 + j] for j in range(4)]
                        for c in range(16):
                            cx.op(pe, lambda e: e.matmul(bk[:, :], wbt[:, c, 0:128], hT[:, c, tg * 512:(tg + 1) * 512],
                                                         start=(c == 0), stop=(c == 15)),
                                  tiles + [wbb], [bk.b], inc=(c == 15))
                        otile, ob = ot[cnt["ot"] % 4]
                        cnt["ot"] += 1
                        cx.op(act, lambda e: e.activation(out=otile[:, :], in_=bk[:, :], func=AF.Copy), [bk.b], [ob])
                        g512 = (t0 + tg * 512) // 512
                        cx.dma(sp, qk_[s][qi * 128:(qi + 1) * 128, t0 + tg * 512:t0 + (tg + 1) * 512], otile[:, :],
                               reads=[ob], writes=[qkb_[s][g512]])

            load_block(0)
            for k in range(len(items)):
                if items[k][1] == 0:
                    prep_mg(items[k][0])
                if k + 1 < len(items):
                    load_block(k + 1)
                compute_block(k)

    def phase_final(layer, s, L, xsrc, xsrc_b, xdst, xdst_b, last):
        with ExitStack() as es:
            wo = es.enter_context(sbt("wo", [128, 16, D], BF16))
            wos0 = es.enter_context(sbt("wos0", [128, 1, D], F32))
            wos1 = es.enter_context(sbt("wos1", [128, 1, D], F32))
            gbc = es.enter_context(sbt("gbc", [128, D], F32))
            fgbc = es.enter_context(sbt("fgbc", [128, D], F32))
            yb0 = es.enter_context(sbt("yb0", [128, D], BF16))
            yb1 = es.enter_context(sbt("yb1", [128, D], BF16))
            gt0 = es.enter_context(sbt("gt0", [128, D], BF16))
            gt1 = es.enter_context(sbt("gt1", [128, D], BF16))
            sg = es.enter_context(sbt("sg", [128, D], F32))
            yg = es.enter_context(sbt("yg", [128, D], F32))
            ynb = es.enter_context(sbt("ynb", [128, D], BF16))
            yT = es.enter_context(sbt("yT", [128, 16, 128], BF16))
            yT2 = es.enter_context(sbt("yT2", [128, 16, 128], BF16))
            ynb2 = es.enter_context(sbt("ynb2", [128, D], BF16))
            fst2 = es.enter_context(sbt("fst2", [128, 4], F32))
            bfst2 = Buf()
            xr0 = es.enter_context(sbt("xr0", [128, D], F32))
            xr1 = es.enter_context(sbt("xr1", [128, D], F32))
            xo0 = es.enter_context(sbt("xo0", [128, D], F32))
            fst = es.enter_context(sbt("fst", [128, 16], F32))
            junkf = es.enter_context(sbt("junkf", [128, D], BF16))
            bwo, bgbc, bfg, bsg, byg, bynb, byT, bfst, bjk = (Buf() for _ in range(9))
            wos = [(wos0, Buf()), (wos1, Buf())]
            ybt = [(yb0, Buf()), (yb1, Buf())]
            gtt = [(gt0, Buf()), (gt1, Buf())]
            xrt = [(xr0, Buf()), (xr1, Buf())]
            xot = [(xo0, Buf()), (xo0, None)]
            wv = w_out[layer].rearrange("(c p) n -> p c n", p=128)
            for c2 in range(16):
                wt, wtb = wos[c2 % 2]
                cx.dma(sp, wt[:], wv[:, c2:c2 + 1, :], writes=[wtb])
                if c2 % 2 == 0:
                    cx.op(dve, lambda e: e.tensor_copy(out=wo[:, c2:c2 + 1, :], in_=wt[:]), [wtb], [bwo])
                else:
                    cx.op(act, lambda e: e.activation(out=wo[:, c2:c2 + 1, :], in_=wt[:], func=AF.Copy), [wtb], [bwo])
            cx.dma(sp, gbc[:], brg[layer:layer + 1, :].broadcast_to([128, D]), writes=[bgbc], allow_slow_non_contiguous=True)
            if last:
                cx.dma(sp, fgbc[:], fin_g[0:1, :].broadcast_to([128, D]), writes=[bfg], allow_slow_non_contiguous=True)
            segs = [(0, 1024), (1024, 1536), (1536, 2048)]
            gcols = [(C_Z, 1024, 0), (C_GA, 512, 1024), (C_GH, 512, 1536)]
            bank_i = 0
            def fin_loads(ti):
                r0, r1 = ti * 128, (ti + 1) * 128
                yt, yb = ybt[ti % 2]
                gt, gb = gtt[ti % 2]
                xr, xrb = xrt[ti % 2]
                q2 = pool if os.environ.get("KQ2") else sp
                cx.dma(q2, yt[:], ybr_[s][r0:r1, :], reads=ybb_[s][ti], writes=[yb])
                for (cs, cw, do) in gcols:
                    cx.dma(sp, gt[:, do:do + cw], u_[s][PAD + r0:PAD + r1, cs:cs + cw], reads=[ub_[s][ti + 1]], writes=[gb])
                cx.dma(q2, xr[:], xsrc[r0:r1, :], reads=[xsrc_b[ti]], writes=[xrb])

            ynbs = [(ynb, bynb), (ynb2, Buf())]
            yTs = [(yT, byT), (yT2, Buf())]
            bcnt = [0]

            def nbk():
                bcnt[0] += 1
                return banks[bcnt[0] % 8]

            def fa_elem(ti):
                yt, yb = ybt[ti % 2]
                gt, gb = gtt[ti % 2]
                ynb_, bynb_ = ynbs[ti % 2]
                cx.op(act, lambda e: e.activation(out=sg[:], in_=gt[:], func=AF.Silu), [gb], [bsg])
                cx.op(dve, lambda e: e.tensor_tensor(out=yg[:], in0=yt[:], in1=sg[:], op=ALU.mult), [yb, bsg], [byg])
                for k, (a, b) in enumerate(segs):
                    cx.op(act, lambda e: e.activation(out=junkf[:, a:b], in_=yg[:, a:b], func=AF.Square,
                                                      accum_out=fst[:, k:k + 1]), [byg], [bjk, bfst])
                    cx.op(dve, lambda e: e.tensor_scalar(out=fst[:, 4 + k:5 + k], in0=fst[:, k:k + 1],
                                                         scalar1=1.0 / (b - a), scalar2=EPS, op0=ALU.mult, op1=ALU.add),
                          [bfst], [bfst])
                cx.op(act, lambda e: e.activation(out=fst[:, 8:11], in_=fst[:, 4:7], func=AF.Sqrt), [bfst], [bfst])
                cx.op(dve, lambda e: e.reciprocal(out=fst[:, 12:15], in_=fst[:, 8:11]), [bfst], [bfst])
                for k, (a, b) in enumerate(segs):
                    cx.op(dve, lambda e: e.scalar_tensor_tensor(out=ynb_[:, a:b], in0=yg[:, a:b], scalar=fst[:, 12 + k:13 + k],
                                                                in1=gbc[:, a:b], op0=ALU.mult, op1=ALU.mult),
                          [byg, bfst, bgbc], [bynb_])

            def fa_tr(ti):
                ynb_, bynb_ = ynbs[ti % 2]
                yT_, byT_ = yTs[ti % 2]
                for q4 in range(4):
                    bk = nbk()
                    for j in range(4):
                        c = q4 * 4 + j
                        cx.op(pe, lambda e: e.matmul(bk[:, j * 128:(j + 1) * 128], ynb_[:, c * 128:(c + 1) * 128],
                                                     ident_b[:, :], start=(j == 0), stop=(j == 3)),
                              [bynb_, ident_b.b], [bk.b], inc=(j == 3))
                    src = bk[:, :].rearrange("p (c t) -> p c t", c=4)
                    if q4 % 2 == 0:
                        cx.op(act, lambda e: e.activation(out=yT_[:, q4 * 4:(q4 + 1) * 4, :], in_=src, func=AF.Copy), [bk.b], [byT_])
                    else:
                        cx.op(dve, lambda e: e.tensor_copy(out=yT_[:, q4 * 4:(q4 + 1) * 4, :], in_=src), [bk.b], [byT_])

            def fb_mm(ti):
                yT_, byT_ = yTs[ti % 2]
                bks = []
                for nb in range(4):
                    bk = nbk()
                    bks.append(bk)
                    for c in range(16):
                        cx.op(pe, lambda e: e.matmul(bk[:, :], yT_[:, c, :], wo[:, c, nb * 512:(nb + 1) * 512],
                                                     start=(c == 0), stop=(c == 15)), [byT_, bwo], [bk.b], inc=(c == 15))
                return bks

            def fb_post(ti, bks):
                r0, r1 = ti * 128, (ti + 1) * 128
                xr, xrb = xrt[ti % 2]
                xo, xob = xot[0]
                for nb in range(4):
                    bk = bks[nb]
                    cx.op(dve, lambda e: e.tensor_tensor(out=xo[:, nb * 512:(nb + 1) * 512], in0=bk[:, :],
                                                         in1=xr[:, nb * 512:(nb + 1) * 512], op=ALU.add), [bk.b, xrb], [xob])
                if last:
                    cx.op(act, lambda e: e.activation(out=junkf[:], in_=xo[:], func=AF.Square, accum_out=fst2[:, 0:1]),
                          [xob], [bjk, bfst2])
                    cx.op(dve, lambda e: e.tensor_scalar(out=fst2[:, 1:2], in0=fst2[:, 0:1], scalar1=1.0 / D, scalar2=EPS,
                                                         op0=ALU.mult, op1=ALU.add), [bfst2], [bfst2])
                    cx.op(act, lambda e: e.activation(out=fst2[:, 2:3], in_=fst2[:, 1:2], func=AF.Sqrt), [bfst2], [bfst2])
                    cx.op(dve, lambda e: e.reciprocal(out=fst2[:, 3:4], in_=fst2[:, 2:3]), [bfst2], [bfst2])
                    cx.op(dve, lambda e: e.scalar_tensor_tensor(out=xo[:], in0=xo[:], scalar=fst2[:, 3:4], in1=fgbc[:],
                                                                op0=ALU.mult, op1=ALU.mult), [xob, bfst2, bfg], [xob])
                    cx.dma(stq, yout[s][r0:r1, :], xo[:], reads=[xob], writes=[], final=True)
                else:
                    cx.dma(stq, xdst[r0:r1, :], xo[:], reads=[xob], writes=[xdst_b[ti]])

            nt = L // 128
            stq = sp if os.environ.get("KNOSTQ") else pool
            fin_loads(0)
            fa_elem(0)
            fa_tr(0)
            for ti in range(nt):
                if ti + 1 < nt:
                    fin_loads(ti + 1)
                    fa_elem(ti + 1)
                bks = fb_mm(ti)
                if ti + 1 < nt:
                    fa_tr(ti + 1)
                fb_post(ti, bks)

    masks_d = dram("ssd_masks", [128, 5, 128], F32, "ExternalInput")
    ssd_cw = dram("ssd_conv_w", [depth, 5, 1536], F32, "ExternalInput")
    ssd_cb = dram("ssd_conv_b", [depth, 1, 1536], F32, "ExternalInput")
    ssd_dtb = dram("ssd_dt_bias", [depth, 1, 32], F32, "ExternalInput")
    ssd_al = dram("ssd_a_log", [depth, 1, 32], F32, "ExternalInput")
    ssd_dd = dram("ssd_d", [depth, 1, 16], F32, "ExternalInput")
    xact_ = [dram(f"xact{s}", [L, 1536], BF16) for s, L in enumerate(seqs)]
    hbs_ = [dram(f"hbs{s}", [L // 128, 128, 1024], BF16) for s, L in enumerate(seqs)]
    xab_ = [[Buf() for _ in range(L // 128)] for L in seqs]
    hbb_ = [[Buf() for _ in range(L // 128)] for L in seqs]
    mk_f = T(nc, "mk_f", [128, 5, 128], F32)
    mk_b = T(nc, "mk_b", [128, 5, 128], BF16)
    cx.dma(sp, mk_f[:], masks_d[:, :, :], writes=[mk_f.b])
    cx.op(dve, lambda e: e.tensor_copy(out=mk_b[:], in_=mk_f[:]), [mk_f.b], [mk_b.b])

    def phase_ssd(layer, s, L):
        nb = L // 128
        with ExitStack() as es:
            A = lambda name, shape, dt: es.enter_context(sbt(name, shape, dt))
            wbc = A("wbc", [128, 5, 1536], F32)
            bbc = A("bbc", [128, 1536], F32)
            dtbb = A("dtbb", [128, 32], F32)
            abc = A("abc", [128, 32], F32)
            d16 = A("d16", [128, 16], F32)
            dbc = A("dbc", [128, 16, 64], F32)
            bconst = Buf()
            for k in range(5):
                cx.dma(sp, wbc[:, k, :], ssd_cw[layer, k:k + 1, :].broadcast_to([128, 1536]), writes=[bconst],
                       allow_slow_non_contiguous=True)
            cx.dma(sp, bbc[:], ssd_cb[layer, 0:1, :].broadcast_to([128, 1536]), writes=[bconst], allow_slow_non_contiguous=True)
            cx.dma(sp, dtbb[:], ssd_dtb[layer, 0:1, :].broadcast_to([128, 32]), writes=[bconst], allow_slow_non_contiguous=True)
            cx.dma(sp, abc[:], ssd_al[layer, 0:1, :].broadcast_to([128, 32]), writes=[bconst], allow_slow_non_contiguous=True)
            cx.dma(sp, d16[:], ssd_dd[layer, 0:1, :].broadcast_to([128, 16]), writes=[bconst], allow_slow_non_contiguous=True)
            cx.op(act, lambda e: e.activation(out=abc[:], in_=abc[:], func=AF.Exp), [bconst], [bconst])
            cx.op(dve, lambda e: e.tensor_scalar(out=abc[:], in0=abc[:], scalar1=-1.0, scalar2=None, op0=ALU.mult),
                  [bconst], [bconst])
            cx.op(dve, lambda e: e.tensor_copy(out=dbc[:], in_=d16[:].unsqueeze(2).broadcast_to([128, 16, 64])),
                  [bconst], [bconst])
            xk = [[(A(f"xk{k}_{p}", [128, 1536], BF16), Buf()) for k in range(5)] for p in range(2)]
            accs_ = [A(f"acc{p}", [128, 8], F32) for p in range(2)]
            tmps = [A(f"tmpc{k}", [128, 1536], BF16) for k in range(5)]
            btmps = [Buf() for _ in range(5)]
            bias_bf = A("bias_bf", [1, 1536], BF16)
            cx.op(dve, lambda e: e.tensor_copy(out=bias_bf[:], in_=bbc[0:1, :]), [bconst], [bconst])
            xa2 = [(A(f"xa{p}", [128, 1536], BF16), Buf()) for p in range(2)]
            dtss = [A(f"dts{p}", [128, 4, 32], F32) for p in range(2)]
            dhls = [A(f"dhl{p}", [128, 2, 32], BF16) for p in range(2)]
            exs = [A(f"ex{p}", [128, 96], F32) for p in range(2)]
            bmcms = [A(f"bmcm{p}", [128, 4, 128], BF16) for p in range(2)]
            cbms = [A(f"cbm{p}", [128, 2, 2, 128], F32) for p in range(2)]
            rhl = [A(f"rhl{p}", [128, 512], F32) for p in range(2)]
            et = [A(f"et{p}", [128, 512], F32) for p in range(3)]
            mts = [A(f"mt{p}", [128, 2, 16, 128], BF16) for p in range(2)]
            xdts = [A(f"xdt{p}", [128, 2, 1024], BF16) for p in range(2)]
            xdds = [A(f"xdd{p}", [128, 2, 1024], BF16) for p in range(2)]
            hst = A("hst", [128, 2, 2, 512], F32)
            hbf = A("hbf", [128, 2, 2, 512], BF16)
            hld = [A(f"hld{p}", [128, 1024], BF16) for p in range(2)]
            yt1 = A("yt1", [128, 512], F32)
            yo = [A(f"yo{p}", [128, 1024], F32) for p in range(2)]
            yob = [A(f"yob{p}", [128, 1024], BF16) for p in range(2)]
            byob = [Buf(), Buf()]
            btmp, bhst, bhbf, byt1 = (Buf() for _ in range(4))
            pb = [{n: Buf() for n in ("acc", "dts", "dhl", "ex", "bmcm", "cbm", "mt", "xdt", "xdd")} for _ in range(2)]
            acc = dts = dhl = ex = bmcm = cbm = mt = xdt = xdd = None
            bacc = bdts = bdhl = bex = bbmcm = bcbm = bmt = bxdt = bxdd = None

            def bind(par):
                nonlocal acc, dts, dhl, ex, bmcm, cbm, mt, xdt, xdd, bacc, bdts, bdhl, bex, bbmcm, bcbm, bmt, bxdt, bxdd
                acc, dts, dhl, ex, bmcm, cbm, mt, xdt, xdd = (accs_[par], dtss[par], dhls[par], exs[par], bmcms[par],
                                                             cbms[par], mts[par], xdts[par], xdds[par])
                d = pb[par]
                bacc, bdts, bdhl, bex, bbmcm, bcbm, bmt, bxdt, bxdd = (d["acc"], d["dts"], d["dhl"], d["ex"], d["bmcm"],
                                                                      d["cbm"], d["mt"], d["xdt"], d["xdd"])

            brhl = [Buf(), Buf()]
            bet = [Buf(), Buf(), Buf()]
            bhld = [Buf(), Buf()]
            byo = [Buf(), Buf()]
            cx.op(dve, lambda e: e.memset(hst[:], 0.0), [], [bhst])
            cx.op(dve, lambda e: e.memset(hbf[:], 0.0), [], [bhbf])
            bank_n = [0]

            def nbank():
                bank_n[0] += 1
                return banks[2 + bank_n[0] % 6]

            def dt_part(i):
                cx.op(dve, lambda e: e.tensor_tensor(out=dts[:, 2, :], in0=dtall[s][:, i, :], in1=dtbb[:], op=ALU.add),
                      [dtall[s].b, bconst], [bdts])
                cx.op(act, lambda e: e.activation(out=dts[:, 2, :], in_=dts[:, 2, :], func=AF.Exp), [bdts], [bdts])
                cx.op(act, lambda e: e.activation(out=dts[:, 0, :], in_=dts[:, 2, :], func=AF.Ln, bias=1.0), [bdts], [bdts])
                cx.op(dve, lambda e: e.tensor_tensor(out=dts[:, 1, :], in0=dts[:, 0, :], in1=abc[:], op=ALU.mult),
                      [bdts, bconst], [bdts])
                cx.op(dve, lambda e: e.tensor_copy(out=dhl[:, 0, :], in_=dts[:, 1, :]), [bdts], [bdhl])
                cx.op(dve, lambda e: e.tensor_tensor(out=dts[:, 2, :], in0=dts[:, 1, :], in1=dhl[:, 0, :], op=ALU.subtract),
                      [bdts, bdhl], [bdts])
                cx.op(dve, lambda e: e.tensor_copy(out=dhl[:, 1, :], in_=dts[:, 2, :]), [bdts], [bdhl])
                bk = nbank()
                plan = [(0, 0, 0), (1, 16, 16), (2, 32, 0), (3, 48, 16), (4, 64, 0), (4, 80, 16)]
                first = True
                for (mi, oc, ic) in plan:
                    for hl in range(2):
                        last = (mi, oc) == (4, 80) and hl == 1
                        cx.op(pe, lambda e: e.matmul(bk[:, oc:oc + 16], mk_b[:, mi, :], dhl[:, hl, ic:ic + 16],
                                                     start=first, stop=last), [mk_b.b, bdhl], [bk.b], inc=last)
                        first = False
                cx.op(act, lambda e: e.activation(out=ex[:], in_=bk[:, 0:96], func=AF.Exp), [bk.b], [bex])
                cx.op(dve, lambda e: e.tensor_tensor(out=dts[:, 3, :], in0=dts[:, 0, :], in1=ex[:, 32:64], op=ALU.mult),
                      [bdts, bex], [bdts])

            def a_loads(i):
                par = i % 2
                for k in range(5):
                    t, b = xk[par][k]
                    r0 = PAD + i * 128 + k - 2
                    rd = [ub_[s][i + 1]] + ([ub_[s][i]] if k < 2 else []) + ([ub_[s][i + 2]] if k > 2 else [])
                    cx.dma(sp, t[:], u_[s][r0:r0 + 128, C_XBC:C_XBC + 1536], reads=rd, writes=[b])

            a_loads(nb - 1)
            for i in range(nb - 1, -1, -1):
                par = i % 2
                bind(par)
                if i - 1 >= 0:
                    a_loads(i - 1)
                t0, b0 = xk[par][0]
                for k in range(3, 5):
                    tk, bk_ = xk[par][k]
                    cx.op(pool, lambda e: e.tensor_tensor(out=tmps[k][:], in0=tk[:], in1=wbc[:, k, :], op=ALU.mult),
                          [bk_, bconst], [btmps[k]])
                for k in range(0, 3):
                    tk, bk_ = xk[par][k]
                    cx.op(dve, lambda e: e.tensor_tensor(out=tmps[k][:], in0=tk[:], in1=wbc[:, k, :], op=ALU.mult),
                          [bk_, bconst], [btmps[k]])
                xa, xab = xa2[par]
                for cb in range(3):
                    bk = nbank()
                    for k in (0, 1, 2, 3, 4):
                        cx.op(pe, lambda e: e.matmul(bk[:, :], ident_b[:, :], tmps[k][:, cb * 512:(cb + 1) * 512],
                                                     start=(k == 0), stop=False), [ident_b.b, btmps[k]], [bk.b], inc=False)
                    cx.op(pe, lambda e: e.matmul(bk[:, :], mk_b[0:1, 4, :], bias_bf[0:1, cb * 512:(cb + 1) * 512],
                                                 start=False, stop=True), [mk_b.b, bconst], [bk.b])
                    cx.op(act, lambda e: e.activation(out=xa[:, cb * 512:(cb + 1) * 512], in_=bk[:, :], func=AF.Silu),
                          [bk.b], [xab])
                cx.dma(sp, xact_[s][i * 128:(i + 1) * 128, :], xa[:], reads=[xab], writes=[xab_[s][i]])
                cx.dma(sp, hbs_[s][i].rearrange("p (g c) -> p g c", g=2), hbf[:, 1, :, :], reads=[bhbf], writes=[hbb_[s][i]])
                if i == 0:
                    break
                dt_part(i)
                cx.op(pool, lambda e: e.tensor_tensor(
                    out=xdd[:, 1, :].rearrange("p (h c) -> p h c", h=16), in0=xa[:, 0:1024].rearrange("p (h c) -> p h c", h=16),
                    in1=dts[:, 3, 16:32].unsqueeze(2).broadcast_to([128, 16, 64]), op=ALU.mult), [xab, bdts], [bxdd])
                for g in range(2):
                    bk = nbank()
                    cx.op(pe, lambda e: e.matmul(bk[:, :], xa[:, 1024 + g * 128:1024 + (g + 1) * 128],
                                                 xdd[:, 1, g * 512:(g + 1) * 512], start=True, stop=True), [xab, bxdd], [bk.b])
                    cx.op(dve, lambda e: e.tensor_tensor(
                        out=hst[:, 1, g, :].rearrange("p (h c) -> p h c", h=8), in0=hst[:, 1, g, :].rearrange("p (h c) -> p h c", h=8),
                        in1=ex[:, 80 + g * 8:88 + g * 8].unsqueeze(2).broadcast_to([128, 8, 64]), op=ALU.mult), [bhst, bex], [bhst])
                    cx.op(dve, lambda e: e.tensor_tensor(out=hst[:, 1, g, :], in0=hst[:, 1, g, :], in1=bk[:, :], op=ALU.add),
                          [bhst, bk.b], [bhst])
                    cx.op(act, lambda e: e.activation(out=hbf[:, 1, g, :], in_=hst[:, 1, g, :], func=AF.Copy), [bhst], [bhbf])

            def b_loads(i):
                par = i % 2
                xa, xab = xa2[par]
                cx.dma(sp, xa[:], xact_[s][i * 128:(i + 1) * 128, :], reads=[xab_[s][i]], writes=[xab])
                if i < nb - 1:
                    cx.dma(sp, hld[par][:], hbs_[s][i], reads=[hbb_[s][i]], writes=[bhld[par]])

            def st1(i):
                par = i % 2
                bind(par)
                xa, xab = xa2[par]
                dt_part(i)
                bk = nbank()
                for j in range(4):
                    cx.op(pe, lambda e: e.matmul(bk[:, j * 128:(j + 1) * 128], xa[:, 1024 + j * 128:1024 + (j + 1) * 128],
                                                 ident_b[:, :], start=(j == 0), stop=(j == 3)), [xab, ident_b.b], [bk.b], inc=(j == 3))
                cx.op(act, lambda e: e.activation(out=bmcm[:].rearrange("p a b -> p (a b)"), in_=bk[:, :], func=AF.Copy), [bk.b], [bbmcm])
                bk = nbank()
                for g in range(2):
                    cx.op(pe, lambda e: e.matmul(bk[:, g * 128:(g + 1) * 128], bmcm[:, g, :], bmcm[:, 2 + g, :],
                                                 start=(g == 0), stop=(g == 1)), [bbmcm], [bk.b], inc=(g == 1))
                for d in range(2):
                    cx.op(dve, lambda e: e.tensor_tensor(
                        out=cbm[:, d, :, :], in0=bk[:, 0:256].rearrange("p (g l) -> p g l", g=2),
                        in1=mk_f[:, d, :].unsqueeze(1).broadcast_to([128, 2, 128]), op=ALU.mult), [bk.b, mk_f.b], [bcbm])
                for d in range(2):
                    cx.op(pool, lambda e: e.tensor_tensor(
                        out=xdt[:, d, :].rearrange("p (h c) -> p h c", h=16), in0=xa[:, 0:1024].rearrange("p (h c) -> p h c", h=16),
                        in1=dts[:, 0, d * 16:(d + 1) * 16].unsqueeze(2).broadcast_to([128, 16, 64]), op=ALU.mult), [xab, bdts], [bxdt])
                cx.op(pool, lambda e: e.tensor_tensor(
                    out=xdd[:, 0, :].rearrange("p (h c) -> p h c", h=16), in0=xa[:, 0:1024].rearrange("p (h c) -> p h c", h=16),
                    in1=dts[:, 3, 0:16].unsqueeze(2).broadcast_to([128, 16, 64]), op=ALU.mult), [xab, bdts], [bxdd])
                specs = [(d, h4) for d in range(2) for h4 in range(4)]

                def seg_front(q):
                    d, h4 = specs[q]
                    lm, rm = (2, 0) if d == 0 else (3, 1)
                    r_t, r_b = rhl[q % 2], brhl[q % 2]
                    e_t, e_b = et[q % 3], bet[q % 3]
                    cx.op(dve, lambda e: e.tensor_tensor(
                        out=r_t[:, :].rearrange("p (h l) -> p h l", h=4),
                        in0=mk_f[:, rm, :].unsqueeze(1).broadcast_to([128, 4, 128]),
                        in1=dts[:, 1, d * 16 + h4 * 4:d * 16 + h4 * 4 + 4].unsqueeze(2).broadcast_to([128, 4, 128]),
                        op=ALU.mult), [mk_f.b, bdts], [r_b])
                    bk = nbank()
                    cx.op(pe, lambda e: e.matmul(bk[:, :], mk_f[:, lm, :], r_t[:, :], start=True, stop=True),
                          [mk_f.b, r_b], [bk.b])
                    cx.op(act, lambda e: e.activation(out=e_t[:], in_=bk[:, :], func=AF.Exp), [bk.b], [e_b])

                def seg_back(q):
                    d, h4 = specs[q]
                    e_t, e_b = et[q % 3], bet[q % 3]
                    g = h4 // 2
                    cx.op(dve, lambda e: e.tensor_tensor(
                        out=mt[:, d, h4 * 4:(h4 + 1) * 4, :], in0=e_t[:].rearrange("p (h l) -> p h l", h=4),
                        in1=cbm[:, d, g, :].unsqueeze(1).broadcast_to([128, 4, 128]), op=ALU.mult), [e_b, bcbm], [bmt])

                for q in range(8 + 2):
                    if q < 8:
                        seg_front(q)
                    if q >= 2:
                        seg_back(q - 2)

            def st2(i):
                par = i % 2
                bind(par)
                xa, xab = xa2[par]
                hl_t, hl_b = hld[par], bhld[par]
                ybk = [banks[0], banks[1]]
                for g in range(2):
                    first = True
                    for d in range(2):
                        for h in range(8):
                            hh = g * 8 + h
                            last = (d == 1 and h == 7)
                            cx.op(pe, lambda e: e.matmul(ybk[g][:, h * 64:(h + 1) * 64], mt[:, d, hh, :],
                                                         xdt[:, d, hh * 64:(hh + 1) * 64], start=first, stop=last),
                                  [bmt, bxdt], [ybk[g].b], inc=last)
                            first = False
                yo_t, yo_b = yo[par], byo[par]
                for g in range(2):
                    cx.op(dve, lambda e: e.tensor_tensor(out=yo_t[:, g * 512:(g + 1) * 512], in0=xa[:, g * 512:(g + 1) * 512],
                                                         in1=dbc[:, g * 8:(g + 1) * 8, :].rearrange("p h c -> p (h c)"), op=ALU.mult),
                          [xab, bconst], [yo_b])
                    cx.op(dve, lambda e: e.tensor_tensor(out=yo_t[:, g * 512:(g + 1) * 512], in0=yo_t[:, g * 512:(g + 1) * 512],
                                                         in1=ybk[g][:, :], op=ALU.add), [yo_b, ybk[g].b], [yo_b])
                    for d in range(2):
                        if (d == 0 and i == 0) or (d == 1 and i == nb - 1):
                            continue
                        bk = nbank()
                        rhs = hbf[:, 0, g, :] if d == 0 else hl_t[:, g * 512:(g + 1) * 512]
                        rb_ = bhbf if d == 0 else hl_b
                        cx.op(pe, lambda e: e.matmul(bk[:, :], bmcm[:, 2 + g, :], rhs, start=True, stop=True), [bbmcm, rb_], [bk.b])
                        cx.op(dve, lambda e: e.tensor_tensor(
                            out=yt1[:].rearrange("p (h c) -> p h c", h=8), in0=bk[:, :].rearrange("p (h c) -> p h c", h=8),
                            in1=ex[:, d * 16 + g * 8:d * 16 + g * 8 + 8].unsqueeze(2).broadcast_to([128, 8, 64]), op=ALU.mult),
                              [bk.b, bex], [byt1])
                        cx.op(dve, lambda e: e.tensor_tensor(out=yo_t[:, g * 512:(g + 1) * 512], in0=yo_t[:, g * 512:(g + 1) * 512],
                                                             in1=yt1[:], op=ALU.add), [yo_b, byt1], [yo_b])
                cx.op(act, lambda e: e.activation(out=yob[par][:], in_=yo_t[:], func=AF.Copy), [yo_b], [byob[par]])
                cx.dma(sp, ybr_[s][i * 128:(i + 1) * 128, 0:1024], yob[par][:], reads=[byob[par]], writes=[ybb_[s][i][0]])
                if i < nb - 1:
                    for g in range(2):
                        bk = nbank()
                        cx.op(pe, lambda e: e.matmul(bk[:, :], xa[:, 1024 + g * 128:1024 + (g + 1) * 128],
                                                     xdd[:, 0, g * 512:(g + 1) * 512], start=True, stop=True), [xab, bxdd], [bk.b])
                        cx.op(dve, lambda e: e.tensor_tensor(
                            out=hst[:, 0, g, :].rearrange("p (h c) -> p h c", h=8), in0=hst[:, 0, g, :].rearrange("p (h c) -> p h c", h=8),
                            in1=ex[:, 64 + g * 8:72 + g * 8].unsqueeze(2).broadcast_to([128, 8, 64]), op=ALU.mult), [bhst, bex], [bhst])
                        cx.op(dve, lambda e: e.tensor_tensor(out=hst[:, 0, g, :], in0=hst[:, 0, g, :], in1=bk[:, :], op=ALU.add),
                              [bhst, bk.b], [bhst])
                        cx.op(act, lambda e: e.activation(out=hbf[:, 0, g, :], in_=hst[:, 0, g, :], func=AF.Copy), [bhst], [bhbf])

            b_loads(0)
            st1(0)
            for i in range(nb):
                if i + 1 < nb:
                    b_loads(i + 1)
                    st1(i + 1)
                st2(i)

    hy_cw = dram("hy_conv_w", [depth, 3, 1536], F32, "ExternalInput")
    hy_cb = dram("hy_conv_b", [depth, 1, 1536], F32, "ExternalInput")
    hy_w123 = dram("hy_w123", [depth, 3, 128, 128], F32, "ExternalInput")
    hy_w4 = dram("hy_w4", [depth, 128, 1024], F32, "ExternalInput")
    hy_fb = dram("hy_fb", [depth, 128, 4], F32, "ExternalInput")
    hy_dd = dram("hy_d", [depth, 1, 512], F32, "ExternalInput")
    uniqL = sorted(set(seqs))
    hyc = {}
    for L in uniqL:
        KA = L // 32
        hyc[L] = dict(
            zemb=dram(f"hy_zemb{L}", [128, L], BF16, "ExternalInput"),
            decay=dram(f"hy_decay{L}", [L, 512], F32, "ExternalInput"),
            FA=dram(f"hy_FA{L}", [128, 2, 128], BF16, "ExternalInput"),
            FP=dram(f"hy_FP{L}", [128, 2, 128], BF16, "ExternalInput"),
            GB=dram(f"hy_GB{L}", [KA, 128, 128], BF16, "ExternalInput"),
            GBsw=dram(f"hy_GBsw{L}", [KA, 128, 128], BF16, "ExternalInput"),
            GP1=dram(f"hy_GP1{L}", [KA, 128, 128], BF16, "ExternalInput"),
            GP2=dram(f"hy_GP2{L}", [KA, 128, 128], BF16, "ExternalInput"),
        )
    hxa_ = [dram(f"hxa{s}", [L, 512], F32) for s, L in enumerate(seqs)]
    huu_ = [dram(f"huu{s}", [L, 512], F32) for s, L in enumerate(seqs)]
    huub_ = [dram(f"huub{s}", [L, 512], BF16) for s, L in enumerate(seqs)]
    kern_ = [dram(f"kern{s}", [2 * L, 512], BF16) for s, L in enumerate(seqs)]
    yu_ = [dram(f"hyu{s}", [2, 64, L // 32, 512], BF16) for s, L in enumerate(seqs)]
    yk_ = [dram(f"hyk{s}", [2, 64, L // 32, 512], BF16) for s, L in enumerate(seqs)]
    khd_ = [dram(f"hkh{s}", [128, L // 32, 512], BF16) for s, L in enumerate(seqs)]
    kswd_ = [dram(f"hksw{s}", [128, L // 32, 512], BF16) for s, L in enumerate(seqs)]
    zd_ = [dram(f"hzd{s}", [2, L // 32, 64, 512], BF16) for s, L in enumerate(seqs)]
    PI = math.pi

    def phase_hy(layer, s, L):
        nb = L // 128
        A_ = L // 64
        KA = 2 * A_
        tb = hyc[L]
        NK = A_ + 1
        stq = sp if os.environ.get("KNOSTQ") else pool
        bn = [0]

        def nbank():
            bn[0] += 1
            return banks[2 + bn[0] % 6]

        with ExitStack() as es:
            A = lambda name, shape, dt: es.enter_context(sbt(name, shape, dt))
            wbc = A("hwbc", [128, 3, 1536], F32)
            bbc = A("hbbc", [128, 1536], F32)
            bconst = Buf()
            for k in range(3):
                cx.dma(sp, wbc[:, k, :], hy_cw[layer, k:k + 1, :].broadcast_to([128, 1536]), writes=[bconst],
                       allow_slow_non_contiguous=True)
            cx.dma(sp, bbc[:], hy_cb[layer, 0:1, :].broadcast_to([128, 1536]), writes=[bconst], allow_slow_non_contiguous=True)
            xk = [[(A(f"hxk{k}_{p}", [128, 1536], BF16), Buf()) for k in range(3)] for p in range(2)]
            accs = [(A(f"hacc{p}", [128, 1024], F32), Buf()) for p in range(2)]
            tmps = [A(f"htmp{k}", [128, 1536], BF16) for k in range(3)]
            btmps = [Buf(), Buf(), Buf()]
            bias_bf = A("hbias_bf", [1, 1536], BF16)
            cx.op(dve, lambda e: e.tensor_copy(out=bias_bf[:], in_=bbc[0:1, :]), [bconst], [bconst])
            uus = [(A(f"huu32_{p}", [128, 512], F32), Buf()) for p in range(2)]
            uubs = [(A(f"huub_{p}", [128, 512], BF16), Buf()) for p in range(2)]
            def h1_loads(i):
                par = i % 2
                for k in range(3):
                    t, b = xk[par][k]
                    r0 = PAD + i * 128 + k - 1
                    rd = [ub_[s][i + 1]] + ([ub_[s][i]] if k < 1 else []) + ([ub_[s][i + 2]] if k > 1 else [])
                    cx.dma(sp, t[:], u_[s][r0:r0 + 128, C_HY:C_HY + 1536], reads=rd, writes=[b])

            h1_loads(0)
            for i in range(nb):
                par = i % 2
                if i + 1 < nb:
                    h1_loads(i + 1)
                acc, bacc = accs[par]
                for k in (2,):
                    tk, bk_ = xk[par][k]
                    cx.op(pool, lambda e: e.tensor_tensor(out=tmps[k][:], in0=tk[:], in1=wbc[:, k, :], op=ALU.mult),
                          [bk_, bconst], [btmps[k]])
                for k in (0, 1):
                    tk, bk_ = xk[par][k]
                    cx.op(dve, lambda e: e.tensor_tensor(out=tmps[k][:], in0=tk[:], in1=wbc[:, k, :], op=ALU.mult),
                          [bk_, bconst], [btmps[k]])
                cbk = []
                for cb in range(3):
                    bk = nbank()
                    cbk.append(bk)
                    for k in range(3):
                        cx.op(pe, lambda e: e.matmul(bk[:, :], ident_b[:, :], tmps[k][:, cb * 512:(cb + 1) * 512],
                                                     start=(k == 0), stop=False), [ident_b.b, btmps[k]], [bk.b], inc=False)
                    cx.op(pe, lambda e: e.matmul(bk[:, :], mk_b[0:1, 4, :], bias_bf[0:1, cb * 512:(cb + 1) * 512],
                                                 start=False, stop=True), [mk_b.b, bconst], [bk.b])
                cx.op(act, lambda e: e.activation(out=acc[:, 0:512], in_=cbk[0][:, :], func=AF.Copy), [cbk[0].b], [bacc])
                cx.op(act, lambda e: e.activation(out=acc[:, 512:1024], in_=cbk[2][:, :], func=AF.Copy), [cbk[2].b], [bacc])
                uu, buu = uus[par]
                uub, buub = uubs[par]
                cx.op(dve, lambda e: e.tensor_tensor(out=uu[:], in0=cbk[1][:, :], in1=acc[:, 512:1024], op=ALU.mult),
                      [cbk[1].b, bacc], [buu])
                cx.op(act, lambda e: e.activation(out=uub[:], in_=uu[:], func=AF.Copy), [buu], [buub])
                cx.dma(sp, hxa_[s][i * 128:(i + 1) * 128, :], acc[:, 0:512], reads=[bacc], writes=[])
                cx.dma(sp, huu_[s][i * 128:(i + 1) * 128, :], uu[:], reads=[buu], writes=[])
                cx.dma(sp, huub_[s][i * 128:(i + 1) * 128, :], uub[:], reads=[buub], writes=[])
        cx.barrier()

        with ExitStack() as es:
            A = lambda name, shape, dt: es.enter_context(sbt(name, shape, dt))
            wf = A("hwf", [128, 3, 128], F32)
            wbq = A("hwb", [128, 3, 128], BF16)
            w4f = A("hw4f", [128, 1024], F32)
            w4b = A("hw4b", [128, 1024], BF16)
            fb = A("hfb", [128, 8], F32)
            bw = Buf()
            for k in range(3):
                cx.dma(sp, wf[:, k, :], hy_w123[layer, k], writes=[bw])
            cx.dma(sp, w4f[:], hy_w4[layer], writes=[bw])
            cx.dma(sp, fb[:, 0:4], hy_fb[layer], writes=[bw])
            cx.op(dve, lambda e: e.tensor_copy(out=wbq[:], in_=wf[:]), [bw], [bw])
            cx.op(dve, lambda e: e.tensor_copy(out=w4b[:], in_=w4f[:]), [bw], [bw])
            cx.op(dve, lambda e: e.tensor_scalar(out=fb[:, 4:7], in0=fb[:, 1:4], scalar1=fb[:, 0:1], scalar2=None, op0=ALU.mult),
                  [bw], [bw])
            zts = [(A(f"hzt{p}", [128, 512], BF16), Buf()) for p in range(2)]
            arg = A("harg", [128, 512], F32)
            m1 = A("hm1", [128, 512], F32)
            hs = [(A(f"hh{p}", [128, 512], BF16), Buf()) for p in range(3)]
            dcs = [(A(f"hdc{p}", [128, 512], F32), Buf()) for p in range(2)]
            kfs = [(A(f"hkf{p}", [128, 512], BF16), Buf()) for p in range(2)]
            kbs = [(A(f"hkb{p}", [128, 512], BF16), Buf()) for p in range(2)]
            krs = [(A(f"hkr{p}", [128, 512], BF16), Buf()) for p in range(2)]
            barg, bm1 = Buf(), Buf()
            cx.dma(sp, kern_[s][L:L + 1, :].rearrange("a (p f) -> (a p) f", f=4), zrow[:, 0:4], reads=[zrow.b], writes=[])
            for tg in range(L // 512):
                zt, bzt = zts[tg % 2]
                cx.dma(sp, zt[:], tb["zemb"][:, tg * 512:(tg + 1) * 512], writes=[bzt])
                hin, bhin = zt, bzt
                for li in range(3):
                    bk = nbank()
                    cx.op(pe, lambda e: e.matmul(bk[:, :], wbq[:, li, :], hin[:, :], start=True, stop=True), [bw, bhin], [bk.b])
                    cx.op(act, lambda e: e.activation(out=arg[:], in_=bk[:, :], func=AF.Identity, scale=fb[:, 0:1],
                                                      bias=fb[:, 4 + li:5 + li]), [bk.b, bw], [barg])
                    cx.op(dve, lambda e: e.tensor_scalar(out=m1[:], in0=arg[:], scalar1=PI, scalar2=None, op0=ALU.is_gt),
                          [barg], [bm1])
                    cx.op(dve, lambda e: e.scalar_tensor_tensor(out=arg[:], in0=m1[:], scalar=-2.0 * PI, in1=arg[:],
                                                                op0=ALU.mult, op1=ALU.add), [bm1, barg], [barg])
                    cx.op(dve, lambda e: e.tensor_scalar(out=m1[:], in0=arg[:], scalar1=-PI, scalar2=None, op0=ALU.is_lt),
                          [barg], [bm1])
                    cx.op(dve, lambda e: e.scalar_tensor_tensor(out=arg[:], in0=m1[:], scalar=2.0 * PI, in1=arg[:],
                                                                op0=ALU.mult, op1=ALU.add), [bm1, barg], [barg])
                    ho, bho = hs[li]
                    cx.op(act, lambda e: e.activation(out=ho[:], in_=arg[:], func=AF.Sin), [barg], [bho])
                    hin, bhin = ho, bho
                for tt in range(4):
                    j = tg * 4 + tt
                    dc, bdc = dcs[j % 2]
                    cx.dma(sp, dc[:], tb["decay"][j * 128:(j + 1) * 128, :], writes=[bdc])
                    bk = nbank()
                    cx.op(pe, lambda e: e.matmul(bk[:, :], hin[:, tt * 128:(tt + 1) * 128], w4b[:, 0:512], start=True, stop=True),
                          [bhin, bw], [bk.b])
                    kf, bkf = kfs[j % 2]
                    cx.op(dve, lambda e: e.tensor_tensor(out=kf[:], in0=bk[:, :], in1=dc[:], op=ALU.mult), [bk.b, bdc], [bkf])
                    cx.dma(sp, kern_[s][j * 128:(j + 1) * 128, :], kf[:], reads=[bkf], writes=[])
                    bk = nbank()
                    cx.op(pe, lambda e: e.matmul(bk[:, :], hin[:, tt * 128:(tt + 1) * 128], w4b[:, 512:1024], start=True, stop=True),
                          [bhin, bw], [bk.b])
                    kb_, bkb = kbs[j % 2]
                    cx.op(dve, lambda e: e.tensor_tensor(out=kb_[:], in0=bk[:, :], in1=dc[:], op=ALU.mult), [bk.b, bdc], [bkb])
                    bk2 = nbank()
                    cx.op(pe, lambda e: e.matmul(bk2[:, :], anti_b[:, :], kb_[:, :], start=True, stop=True), [anti_b.b, bkb], [bk2.b])
                    kr, bkr = krs[j % 2]
                    cx.op(act, lambda e: e.activation(out=kr[:], in_=bk2[:, :], func=AF.Copy), [bk2.b], [bkr])
                    base = 2 * L - 128 * j - 127
                    nrow = 127 if j == 0 else 128
                    cx.dma(sp, kern_[s][base:base + nrow, :], kr[0:nrow, :], reads=[bkr], writes=[])
        cx.barrier()

        with ExitStack() as es:
            A = lambda name, shape, dt: es.enter_context(sbt(name, shape, dt))
            fa = A("hfa", [128, 2, 128], BF16)
            bfa = Buf()
            cx.dma(sp, fa[:], tb["FA"][:, :, :], writes=[bfa])
            xins = [(A(f"hxin{p}", [128, 8, 512], BF16), Buf()) for p in range(2)]
            yts = [(A(f"hyt{p}", [128, 2, 8, 512], BF16), Buf()) for p in range(2)]
            for p in range(2):
                cx.op(dve, lambda e: e.memset(xins[p][0][:], 0.0), [], [xins[p][1]])
            q = 0
            ev = 0
            for (srcd, kin, dst) in ((huub_[s], A_, yu_[s]), (kern_[s], KA, yk_[s])):
                sv = srcd.rearrange("(a b) c -> a b c", b=64)
                def h3_load(bc, q):
                    xin, bxin = xins[q % 2]
                    cx.dma(sp, xin[0:kin, :, :], sv[:, bc * 8:(bc + 1) * 8, :], writes=[bxin])

                h3_load(0, q)
                for bc in range(8):
                    xin, bxin = xins[q % 2]
                    yt, byt = yts[q % 2]
                    q += 1
                    if bc + 1 < 8:
                        h3_load(bc + 1, q)
                    for j in range(8):
                        for ri in range(2):
                            bk = nbank()
                            cx.op(pe, lambda e: e.matmul(bk[:, :], fa[:, ri, :], xin[:, j, :], start=True, stop=True),
                                  [bfa, bxin], [bk.b])
                            ev += 1
                            if ev % 2:
                                cx.op(act, lambda e: e.activation(out=yt[:, ri, j, :], in_=bk[:, :], func=AF.Copy), [bk.b], [byt])
                            else:
                                cx.op(dve, lambda e: e.tensor_copy(out=yt[:, ri, j, :], in_=bk[:, :]), [bk.b], [byt])
                    for ri in range(2):
                        cx.dma(stq, dst[ri, bc * 8:(bc + 1) * 8, 0:NK, :].rearrange("b k c -> k b c"), yt[0:NK, ri, :, :],
                               reads=[byt], writes=[])
        cx.barrier()

        nkc = (NK + 7) // 8
        nks = [min(8, NK - kc * 8) for kc in range(nkc)]
        with ExitStack() as es:
            A = lambda name, shape, dt: es.enter_context(sbt(name, shape, dt))
            ycs = [(A(f"hyc{p}", [128, 8, 512], BF16), Buf()) for p in range(2)]
            gbs = [(A(f"hgb{p}", [128, 2, 8, 128], BF16), Buf()) for p in range(2)]
            kts = [(A(f"hkt{p}", [128, 2, 8, 512], BF16), Buf()) for p in range(2)]
            ykv = yk_[s].rearrange("r b k c -> (r b) k c")
            ev = 0
            def h4_loads(kc):
                yc, byc = ycs[kc % 2]
                gb, bgb = gbs[kc % 2]
                nk = nks[kc]
                cx.dma(sp, yc[:, 0:nk, :], ykv[:, kc * 8:kc * 8 + nk, :], writes=[byc])
                cx.dma(sp, gb[:, 0, 0:nk, :], tb["GB"][kc * 8:kc * 8 + nk].rearrange("k p m -> p k m"), writes=[bgb])
                cx.dma(sp, gb[:, 1, 0:nk, :], tb["GBsw"][kc * 8:kc * 8 + nk].rearrange("k p m -> p k m"), writes=[bgb])

            h4_loads(0)
            for kc in range(nkc):
                yc, byc = ycs[kc % 2]
                gb, bgb = gbs[kc % 2]
                kt, bkt = kts[kc % 2]
                if kc + 1 < nkc:
                    h4_loads(kc + 1)
                nk = nks[kc]
                for j in range(nk):
                    for w in range(2):
                        bk = nbank()
                        cx.op(pe, lambda e: e.matmul(bk[:, :], gb[:, w, j, :], yc[:, j, :], start=True, stop=True), [bgb, byc], [bk.b])
                        ev += 1
                        if ev % 2:
                            cx.op(act, lambda e: e.activation(out=kt[:, w, j, :], in_=bk[:, :], func=AF.Copy), [bk.b], [bkt])
                        else:
                            cx.op(dve, lambda e: e.tensor_copy(out=kt[:, w, j, :], in_=bk[:, :]), [bk.b], [bkt])
                cx.dma(stq, khd_[s][:, kc * 8:kc * 8 + nk, :], kt[:, 0, 0:nk, :], reads=[bkt], writes=[])
                cx.dma(stq, kswd_[s][:, kc * 8:kc * 8 + nk, :], kt[:, 1, 0:nk, :], reads=[bkt], writes=[])
        cx.barrier()

        with ExitStack() as es:
            A = lambda name, shape, dt: es.enter_context(sbt(name, shape, dt))
            ycs = [(A(f"hyc5{p}", [128, 8, 512], BF16), Buf()) for p in range(2)]
            gbs = [(A(f"hgb5{p}", [128, 3, 8, 128], BF16), Buf()) for p in range(2)]
            khs = [(A(f"hkh5{p}", [128, 2, 8, 512], BF16), Buf()) for p in range(2)]
            zts = [(A(f"hzt5{p}", [128, 8, 512], BF16), Buf()) for p in range(2)]
            p1s = [(A(f"hp1{p}", [128, 512], BF16), Buf()) for p in range(4)]
            p2s = [(A(f"hp2{p}", [128, 512], BF16), Buf()) for p in range(4)]
            yuv = yu_[s].rearrange("r b k c -> (r b) k c")
            n = 0
            def h5_loads(kc):
                yc, byc = ycs[kc % 2]
                gb, bgb = gbs[kc % 2]
                kh, bkh = khs[kc % 2]
                nk = nks[kc]
                cx.dma(sp, yc[:, 0:nk, :], yuv[:, kc * 8:kc * 8 + nk, :], writes=[byc])
                for w, nm in enumerate(("GB", "GP1", "GP2")):
                    cx.dma(sp, gb[:, w, 0:nk, :], tb[nm][kc * 8:kc * 8 + nk].rearrange("k p m -> p k m"), writes=[bgb])
                cx.dma(sp, kh[:, 0, 0:nk, :], khd_[s][:, kc * 8:kc * 8 + nk, :], writes=[bkh])
                cx.dma(sp, kh[:, 1, 0:nk, :], kswd_[s][:, kc * 8:kc * 8 + nk, :], writes=[bkh])

            h5_loads(0)
            for kc in range(nkc):
                yc, byc = ycs[kc % 2]
                gb, bgb = gbs[kc % 2]
                kh, bkh = khs[kc % 2]
                zt, bzt = zts[kc % 2]
                if kc + 1 < nkc:
                    h5_loads(kc + 1)
                nk = nks[kc]
                pslots = {}

                def h5_front(j):
                    nonlocal n
                    bk = nbank()
                    cx.op(pe, lambda e: e.matmul(bk[:, :], gb[:, 0, j, :], yc[:, j, :], start=True, stop=True), [bgb, byc], [bk.b])
                    p1, bp1 = p1s[n % 4]
                    p2, bp2 = p2s[n % 4]
                    n += 1
                    cx.op(dve, lambda e: e.tensor_tensor(out=p1[:], in0=bk[:, :], in1=kh[:, 0, j, :], op=ALU.mult), [bk.b, bkh], [bp1])
                    cx.op(dve, lambda e: e.tensor_tensor(out=p2[:], in0=bk[:, :], in1=kh[:, 1, j, :], op=ALU.mult), [bk.b, bkh], [bp2])
                    pslots[j] = (p1, bp1, p2, bp2)

                def h5_back(j):
                    p1, bp1, p2, bp2 = pslots[j]
                    bz = nbank()
                    cx.op(pe, lambda e: e.matmul(bz[:, :], gb[:, 1, j, :], p1[:, :], start=True, stop=False), [bgb, bp1], [bz.b], inc=False)
                    cx.op(pe, lambda e: e.matmul(bz[:, :], gb[:, 2, j, :], p2[:, :], start=False, stop=True), [bgb, bp2], [bz.b])
                    cx.op(act, lambda e: e.activation(out=zt[:, j, :], in_=bz[:, :], func=AF.Copy), [bz.b], [bzt])

                for j in range(nk + 2):
                    if j < nk:
                        h5_front(j)
                    if j >= 2:
                        h5_back(j - 2)
                for ri in range(2):
                    cx.dma(stq, zd_[s][ri, kc * 8:kc * 8 + nk, :, :].rearrange("k b c -> b k c"), zt[ri * 64:(ri + 1) * 64, 0:nk, :],
                           reads=[bzt], writes=[])
        cx.barrier()

        with ExitStack() as es:
            A = lambda name, shape, dt: es.enter_context(sbt(name, shape, dt))
            fp = A("hfp", [128, 2, 128], BF16)
            dbc = A("hdbc", [128, 512], F32)
            bfp = Buf()
            cx.dma(sp, fp[:], tb["FP"][:, :, :], writes=[bfp])
            cx.dma(sp, dbc[:], hy_dd[layer, 0:1, :].broadcast_to([128, 512]), writes=[bfp], allow_slow_non_contiguous=True)
            zrs = [(A(f"hzr{p}", [128, 2, 8, 512], BF16), Buf()) for p in range(2)]
            xas = [(A(f"hxa6{p}", [128, 8, 512], F32), Buf()) for p in range(2)]
            uu6 = [(A(f"huu6{p}", [128, 8, 512], F32), Buf()) for p in range(2)]
            ots = [(A(f"hot{p}", [128, 8, 512], F32), Buf()) for p in range(2)]
            otbs = [(A(f"hotb{p}", [128, 8, 512], BF16), Buf()) for p in range(2)]
            for p in range(2):
                cx.op(dve, lambda e: e.memset(zrs[p][0][:], 0.0), [], [zrs[p][1]])
            xav = hxa_[s].rearrange("(a b) c -> a b c", b=64)
            uuv = huu_[s].rearrange("(a b) c -> a b c", b=64)
            ybv = ybr_[s].rearrange("(a b) c -> a b c", b=64)
            def h6_loads(bc):
                zr, bzr = zrs[bc % 2]
                xa, bxa = xas[bc % 2]
                uu, buu = uu6[bc % 2]
                for ri in range(2):
                    cx.dma(sp, zr[0:NK, ri, :, :], zd_[s][ri, 0:NK, bc * 8:(bc + 1) * 8, :], writes=[bzr])
                cx.dma(sp, xa[0:A_, :, :], xav[:, bc * 8:(bc + 1) * 8, :], writes=[bxa])
                cx.dma(sp, uu[0:A_, :, :], uuv[:, bc * 8:(bc + 1) * 8, :], writes=[buu])

            h6_loads(0)
            for bc in range(8):
                zr, bzr = zrs[bc % 2]
                xa, bxa = xas[bc % 2]
                uu, buu = uu6[bc % 2]
                ot, bot = ots[bc % 2]
                if bc + 1 < 8:
                    h6_loads(bc + 1)
                for j in range(8):
                    bk = nbank()
                    cx.op(pe, lambda e: e.matmul(bk[:, :], fp[:, 0, :], zr[:, 0, j, :], start=True, stop=False), [bfp, bzr], [bk.b], inc=False)
                    cx.op(pe, lambda e: e.matmul(bk[:, :], fp[:, 1, :], zr[:, 1, j, :], start=False, stop=True), [bfp, bzr], [bk.b])
                    cx.op(dve, lambda e: e.tensor_tensor(out=ot[0:A_, j, :], in0=uu[0:A_, j, :], in1=dbc[0:A_, :], op=ALU.mult),
                          [buu, bfp], [bot])
                    cx.op(dve, lambda e: e.tensor_tensor(out=ot[0:A_, j, :], in0=ot[0:A_, j, :], in1=bk[0:A_, :], op=ALU.add),
                          [bot, bk.b], [bot])
                    cx.op(dve, lambda e: e.tensor_tensor(out=ot[0:A_, j, :], in0=ot[0:A_, j, :], in1=xa[0:A_, j, :], op=ALU.mult),
                          [bot, bxa], [bot])
                blks = [ybb_[s][t][2] for t in range(nb)]
                otb, botb = otbs[bc % 2]
                cx.op(act, lambda e: e.activation(out=otb[0:A_, :, :], in_=ot[0:A_, :, :], func=AF.Copy), [bot], [botb])
                cx.dma(stq, ybv[:, bc * 8:(bc + 1) * 8, 1536:2048], otb[0:A_, :, :], reads=[botb], writes=blks)

    rel_bias = dram("rel_bias", [128, 128], F32, "ExternalInput")
    oh8_d = dram("oh8", [128, 512], F32, "ExternalInput")
    mneg_d = dram("mneg", [8, 512], F32, "ExternalInput")
    att_sink = dram("att_sink", [depth, 8], F32, "ExternalInput")
    bd_d = dram("bias_vec", [8, 512], F32)
    biasT = T(nc, "biasT", [128, 8, 3, 128], BF16)
    def setup_bias():
        with ExitStack() as es:
            rb = es.enter_context(sbt("rb", [128, 128], F32))
            oh8 = es.enter_context(sbt("oh8s", [128, 512], F32))
            mneg = es.enter_context(sbt("mnegs", [8, 512], F32))
            antid = es.enter_context(sbt("antids", [128, 128], F32))
            bv = es.enter_context(sbt("bv", [8, 512], F32))
            hk = es.enter_context(sbt("hk", [128, 8, 3, 128], F32))
            brb, boh, bmn, ban, bbv, bhk, bbd = (Buf() for _ in range(7))
            cx.dma(sp, rb[:], rel_bias[:, :], writes=[brb])
            cx.dma(sp, oh8[:], oh8_d[:, :], writes=[boh])
            cx.dma(sp, mneg[:], mneg_d[:, :], writes=[bmn])
            cx.dma(sp, antid[:], antid_d[:, :], writes=[ban])
            rbh = es.enter_context(sbt("rbh", [128, 128], BF16))
            rbl = es.enter_context(sbt("rbl", [128, 128], BF16))
            rbr = es.enter_context(sbt("rbr", [128, 128], F32))
            oh8b = es.enter_context(sbt("oh8b", [128, 512], BF16))
            antb = es.enter_context(sbt("antb", [128, 128], BF16))
            hkh = es.enter_context(sbt("hkh", [128, 3072], BF16))
            hkl = es.enter_context(sbt("hkl", [128, 3072], BF16))
            hkr = es.enter_context(sbt("hkr", [128, 3072], F32))
            bhl = Buf()
            cx.op(dve, lambda e: e.tensor_copy(out=rbh[:], in_=rb[:]), [brb], [bhl])
            cx.op(dve, lambda e: e.tensor_tensor(out=rbr[:], in0=rb[:], in1=rbh[:], op=ALU.subtract), [brb, bhl], [bhl])
            cx.op(dve, lambda e: e.tensor_copy(out=rbl[:], in_=rbr[:]), [bhl], [bhl])
            cx.op(dve, lambda e: e.tensor_copy(out=oh8b[:], in_=oh8[:]), [boh], [boh])
            cx.op(dve, lambda e: e.tensor_copy(out=antb[:], in_=antid[:]), [ban], [ban])
            bk = banks[2]
            cx.op(pe, lambda e: e.matmul(bk[:, :], rbh[:, :], oh8b[:, :], start=True, stop=False), [bhl, boh], [bk.b], inc=False)
            cx.op(pe, lambda e: e.matmul(bk[:, :], rbl[:, :], oh8b[:, :], start=False, stop=True), [bhl, boh], [bk.b])
            cx.op(dve, lambda e: e.tensor_tensor(out=bv[:], in0=bk[0:8, :], in1=mneg[:], op=ALU.add), [bk.b, bmn], [bbv])
            cx.dma(sp, bd_d[:, :], bv[:], reads=[bbv], writes=[bbd])
            for h in range(8 if stop > 0 else 0):
                for j in range(3):
                    src = bass.AP(bd_d.tensor, h * 512 + (2 - j) * 128, [[1, 128], [1, 128]])
                    cx.dma(sp, hk[:, h, j, :], src, reads=[bbd], writes=[bhk])
            hkf = hk[:].rearrange("p h j q -> p (h j q)")
            btf = biasT[:].rearrange("p h j q -> p (h j q)")
            if stop > 1:
                cx.op(dve, lambda e: e.tensor_copy(out=hkh[:], in_=hkf), [bhk], [bhl])
                cx.op(dve, lambda e: e.tensor_tensor(out=hkr[:], in0=hkf, in1=hkh[:], op=ALU.subtract), [bhk, bhl], [bhl])
                cx.op(dve, lambda e: e.tensor_copy(out=hkl[:], in_=hkr[:]), [bhl], [bhl])
            for c in range(6 if stop > 1 else 0):
                bk = banks[3 + c % 2]
                cx.op(pe, lambda e: e.matmul(bk[:, :], antb[:, :], hkh[:, c * 512:(c + 1) * 512], start=True, stop=False),
                      [ban, bhl], [bk.b], inc=False)
                cx.op(pe, lambda e: e.matmul(bk[:, :], antb[:, :], hkl[:, c * 512:(c + 1) * 512], start=False, stop=True),
                      [ban, bhl], [bk.b])
                cx.op(dve, lambda e: e.tensor_copy(out=btf[:, c * 512:(c + 1) * 512], in_=bk[:, :]), [bk.b], [biasT.b])
        cx.barrier()

    def phase_att(layer, s, L):
        nb = L // 128
        stq = sp if os.environ.get("KNOSTQ") else pool
        with ExitStack() as es:
            qT = es.enter_context(sbt("qT", [128, 4, L], BF16))
            kT = es.enter_context(sbt("kT", [128, 2, L], BF16))
            vb = es.enter_context(sbt("vb", [128, nb, 2, 65], BF16))
            esk = es.enter_context(sbt("esk", [128, 8], F32))
            pT0 = es.enter_context(sbt("pT0", [128, 512], BF16))
            pT1 = es.enter_context(sbt("pT1", [128, 512], BF16))
            pT2 = es.enter_context(sbt("pT2", [128, 512], BF16))
            den = es.enter_context(sbt("den", [128, 16], F32))
            ao0 = es.enter_context(sbt("ao0", [128, 512], BF16))
            ao1 = es.enter_context(sbt("ao1", [128, 512], BF16))
            bq, bk_, bv_, bes, bden = (Buf() for _ in range(5))
            pT3 = es.enter_context(sbt("pT3", [128, 512], BF16))
            pTs = [(pT0, Buf()), (pT1, Buf()), (pT2, Buf()), (pT3, Buf())]
            aos = [(ao0, Buf()), (ao1, Buf())]
            for r in range(4):
                cx.dma(sp, qT[:, r, :], qk_[s][r * 128:(r + 1) * 128, :], reads=qkb_[s], writes=[bq])
            cx.op(dve, lambda e: e.memset(kT[:], 0.0), [], [bk_])
            for g in range(2):
                cx.dma(sp, kT[g * 64:(g + 1) * 64, g, :], qk_[s][512 + g * 64:512 + (g + 1) * 64, :], reads=qkb_[s], writes=[bk_])
            cx.op(dve, lambda e: e.memset(vb[:], 1.0), [], [bv_])
            for i in range(nb):
                cx.dma(sp, vb[:, i, :, 0:64], u_[s][PAD + i * 128:PAD + (i + 1) * 128, C_V:C_V + 128]
                       .rearrange("p (g d) -> p g d", g=2), reads=[ub_[s][i + 1]], writes=[bv_])
            cx.dma(sp, esk[:], att_sink[layer:layer + 1, :].broadcast_to([128, 8]), writes=[bes],
                   allow_slow_non_contiguous=True)
            cx.op(act, lambda e: e.activation(out=esk[:], in_=esk[:], func=AF.Exp), [bes], [bes])
            zt = es.enter_context(sbt("zt", [128, 1024], BF16))
            bz = Buf()
            cx.op(dve, lambda e: e.memset(zt[:], 0.0), [], [bz])
            for i in range(nb):
                if "ssd" not in phases:
                    cx.dma(sp, ybr_[s][i * 128:(i + 1) * 128, 0:1024], zt[:, :], reads=[bz], writes=[ybb_[s][i][0]])
                if "hy" not in phases:
                    cx.dma(sp, ybr_[s][i * 128:(i + 1) * 128, 1536:2048], zt[:, 0:512], reads=[bz], writes=[ybb_[s][i][2]])
            bi = 0
            pi = 0
            for i in range(nb if stop > 2 else 0):
                obanks = [banks[6], banks[7]]
                pairs = [(g, j) for g in range(2) for j in range(3) if 0 <= i + j - 1 < nb]
                firsts = {0: True, 1: True}
                slots = {}

                def att_front(q):
                    nonlocal bi, pi
                    g, j = pairs[q]
                    kb = i + j - 1
                    sb_ = banks[2 + bi % 4]
                    bi += 1
                    cx.op(pe, lambda e: e.matmul(sb_[:, :], kT[:, g, kb * 128:(kb + 1) * 128], qT[:, :, i * 128:(i + 1) * 128],
                                                 start=True, stop=False), [bk_, bq], [sb_.b], inc=False)
                    cx.op(pe, lambda e: e.matmul(sb_[:, :], ident_b[:, :], biasT[:, g * 4:(g + 1) * 4, j, :],
                                                 start=False, stop=True), [ident_b.b, biasT.b], [sb_.b])
                    pT, pb = pTs[pi % 4]
                    pi += 1
                    cx.op(act, lambda e: e.activation(out=pT[:, :], in_=sb_[:, :], func=AF.Exp, scale=0.125), [sb_.b], [pb])
                    slots[q] = (pT, pb)

                def att_back(q):
                    g, j = pairs[q]
                    kb = i + j - 1
                    ob = obanks[g]
                    pT, pb = slots[q]
                    for r in range(4):
                        cx.op(pe, lambda e: e.matmul(ob[:, r * 65:(r + 1) * 65], pT[:, r * 128:(r + 1) * 128],
                                                     vb[:, kb, g, :], start=firsts[g], stop=True),
                              [pb, bv_], [ob.b], inc=(r == 3))
                        firsts[g] = False

                npair = len(pairs)
                for q in range(npair + 2):
                    if q < npair:
                        att_front(q)
                    if q >= 2:
                        att_back(q - 2)
                ao, aob = aos[i % 2]
                for g in range(2 if stop > 4 else 0):
                    ob = obanks[g]
                    ov = ob[:, 0:260].rearrange("p (r c) -> p r c", r=4)
                    cx.op(dve, lambda e: e.tensor_tensor(out=den[:, g * 4:(g + 1) * 4], in0=ov[:, :, 64],
                                                         in1=esk[:, g * 4:(g + 1) * 4], op=ALU.add), [ob.b, bes], [bden])
                    cx.op(dve, lambda e: e.reciprocal(out=den[:, 8 + g * 4:8 + (g + 1) * 4], in_=den[:, g * 4:(g + 1) * 4]),
                          [bden], [bden])
                    cx.op(dve, lambda e: e.tensor_tensor(
                        out=ao[:, g * 256:(g + 1) * 256].rearrange("p (r d) -> p r d", r=4), in0=ov[:, :, 0:64],
                        in1=den[:, 8 + g * 4:8 + (g + 1) * 4].unsqueeze(2).broadcast_to([128, 4, 64]), op=ALU.mult),
                          [ob.b, bden], [aob])
                cx.dma(stq, ybr_[s][i * 128:(i + 1) * 128, 1024:1536], ao[:, :], reads=[aob], writes=[ybb_[s][i][1]])

    setup_done = []
    for layer in range(depth):
        for s, L in enumerate(seqs):
            if layer == 0:
                xsrc, xsrc_b = xin[s], [Buf() for _ in range(L // 128)]
            else:
                xsrc, xsrc_b = xs_[s][(layer - 1) % 2], xb_[s][(layer - 1) % 2]
            xdst, xdst_b = xs_[s][layer % 2], xb_[s][layer % 2]
            if "inproj" in phases:
                phase_inproj(layer, s, L, xsrc, xsrc_b)
                cx.barrier()
            if "ssd" in phases:
                phase_ssd(layer, s, L)
                cx.barrier()
            if "hy" in phases:
                phase_hy(layer, s, L)
                cx.barrier()
            if "att" in phases:
                if not setup_done:
                    setup_bias()
                    setup_done.append(1)
                phase_att(layer, s, L)
                cx.barrier()
            if "final" in phases:
                phase_final(layer, s, L, xsrc, xsrc_b, xdst, xdst_b, last=(layer == depth - 1))
                cx.barrier()
    cx.finish()
    return nc


_IN_SIZES = [1024, 1536, 32, 512, 128, 128, 512, 1536, 512]


def _perm_cols():
    o, off = {}, 0
    for n, sz in zip(["z", "xbc", "dt", "q", "k", "v", "ga", "hy", "gh"], _IN_SIZES):
        o[n] = np.arange(off, off + sz)
        off += sz
    q = o["q"].reshape(8, 64)
    qp = np.concatenate([np.concatenate([q[i], q[4 + i]]) for i in range(4)])
    return np.concatenate([o["z"], o["xbc"], o["v"], o["ga"], o["hy"], o["gh"], o["dt"], qp, o["k"]])


def layer_params(p, depth):
    f = lambda a: np.ascontiguousarray(np.asarray(a, np.float32))
    return {
        "ssd_conv_w": f(p["ssd_conv_w"][:depth]),
        "ssd_conv_b": f(p["ssd_conv_b"][:depth])[:, None, :],
        "ssd_dt_bias": f(p["ssd_dt_bias"][:depth]).reshape(depth, 1, 32),
        "ssd_a_log": f(p["ssd_a_log"][:depth]).reshape(depth, 1, 32),
        "ssd_d": f(p["ssd_d"][:depth])[:, None, :],
        "hy_conv_w": f(p["hy_conv_w"][:depth]),
        "hy_conv_b": f(p["hy_conv_b"][:depth])[:, None, :],
        "hy_w123": np.ascontiguousarray(np.stack([
            np.pad(f(p["hy_w1"][:depth]), ((0, 0), (0, 95), (0, 64))),
            np.pad(f(p["hy_w2"][:depth]), ((0, 0), (0, 64), (0, 64))),
            np.pad(f(p["hy_w3"][:depth]), ((0, 0), (0, 64), (0, 64)))], axis=1)),
        "hy_w4": np.pad(f(p["hy_w4"][:depth]), ((0, 0), (0, 64), (0, 0))),
        "hy_fb": np.ascontiguousarray(np.pad(np.stack([f(p["hy_freq"][:depth]), f(p["hy_b1"][:depth]), f(p["hy_b2"][:depth]),
                                                       f(p["hy_b3"][:depth])], axis=-1), ((0, 0), (0, 64), (0, 0)))),
        "hy_d": f(p["hy_d"][:depth])[:, None, :],
    }


def kernel(x_prompt, x_sample, rel_bias, norm_g, w_in, ssd_conv_w, ssd_conv_b, ssd_dt_bias, ssd_a_log,
           ssd_d, ssd_norm_g, att_sink, att_norm_g, hy_conv_w, hy_conv_b, hy_w1, hy_b1, hy_w2, hy_b2,
           hy_w3, hy_b3, hy_w4, hy_freq, hy_d, hy_norm_g, w_out, final_norm_g):
    x_prompt, x_sample = np.asarray(x_prompt, np.float32), np.asarray(x_sample, np.float32)
    n = 8
    Lp, Ls = x_prompt.shape[1], x_sample.shape[1]
    depth = int(np.asarray(w_in).shape[0])
    nc = build([Lp, Ls], depth=depth)
    p = dict(ssd_conv_w=ssd_conv_w, ssd_conv_b=ssd_conv_b, ssd_dt_bias=ssd_dt_bias, ssd_a_log=ssd_a_log, ssd_d=ssd_d,
             hy_conv_w=hy_conv_w, hy_conv_b=hy_conv_b, hy_w1=hy_w1, hy_b1=hy_b1, hy_w2=hy_w2, hy_b2=hy_b2,
             hy_w3=hy_w3, hy_b3=hy_b3, hy_w4=hy_w4, hy_freq=hy_freq, hy_d=hy_d)
    p = {k: np.asarray(v, np.float32) for k, v in p.items()}
    shared = {
        "w_in": np.ascontiguousarray(np.asarray(w_in, np.float32)[:, :, _perm_cols()]),
        "w_out": np.asarray(w_out, np.float32),
        "norm_g": np.asarray(norm_g, np.float32),
        "brg": np.ascontiguousarray(np.concatenate([np.asarray(ssd_norm_g), np.asarray(att_norm_g),
                                                    np.asarray(hy_norm_g)], axis=1).astype(np.float32)),
        "fin_g": np.asarray(final_norm_g, np.float32)[None],
        "rel_bias": np.pad(np.asarray(rel_bias, np.float32), ((0, 96), (0, 120))),
        "att_sink": np.asarray(att_sink, np.float32),
    }
    shared.update(host_consts())
    shared.update(layer_params(p, depth))
    for L in sorted({Lp, Ls}):
        shared.update(hy_consts(L))
    in_maps = [dict(shared, x0=np.ascontiguousarray(x_prompt[c]), x1=np.ascontiguousarray(x_sample[c])) for c in range(n)]
    res = run_bass_kernel_spmd(nc, in_maps, core_ids=list(range(n)))
    yp = np.stack([np.asarray(r["y0"]) for r in res.results], 0)
    ys = np.stack([np.asarray(r["y1"]) for r in res.results], 0)
    return (yp.astype(np.float32), ys.astype(np.float32))
```

```python
import math
import os
from contextlib import ExitStack
import numpy as np
import concourse.bass as bass
import concourse.mybir as mybir
from concourse.bass_utils import run_bass_kernel_spmd

F32 = mybir.dt.float32
BF16 = mybir.dt.bfloat16
I32 = mybir.dt.int32
AF = mybir.ActivationFunctionType
ALU = mybir.AluOpType
AX = mybir.AxisListType

D = 2048
DEPTH = 4
NCOL = 5920
EPS = 1e-6
C_Z, C_XBC, C_V, C_GA, C_HY, C_GH = 0, 1024, 2560, 2688, 3200, 4736
NTOK = 5248
C_DT = 5248
C_QK = 5280
UW = NTOK


class Buf:
    __slots__ = ("w", "r", "name")

    def __init__(self, name=""):
        self.w = {}
        self.r = {}
        self.name = name


class Eng:
    def __init__(self, ctx, name, e, sync_self=True):
        self.ctx, self.name, self.e = ctx, name, e
        self.sem = ctx.nc.alloc_semaphore("s_" + name)
        self.count = 0
        self.waited = {}
        self.pend_r, self.pend_w = [], []
        self.sync_self = sync_self

    def wait(self, sem, val):
        if sem is self.sem and not self.sync_self:
            return
        k = id(sem)
        if self.waited.get(k, 0) >= val:
            return
        self.waited[k] = val
        self.e.wait_ge(sem, val)


class Ctx:
    def __init__(self, nc, n_dma_sems=40):
        self.nc = nc
        self.pe = Eng(self, "pe", nc.tensor, sync_self=False)
        self.act = Eng(self, "act", nc.scalar)
        self.dve = Eng(self, "dve", nc.vector)
        self.pool = Eng(self, "pool", nc.gpsimd)
        self.sp = Eng(self, "sp", nc.sync)
        self.dsem = [[nc.alloc_semaphore(f"dma{i}"), 0] for i in range(n_dma_sems)]
        self.dnext = 0
        self.out_tokens = []

    def _deps(self, reads, writes):
        deps = {}
        for b in reads:
            for k, (s, v) in b.w.items():
                if deps.get(k, (None, 0))[1] < v:
                    deps[k] = (s, v)
        for b in writes:
            for dd in (b.w, b.r):
                for k, (s, v) in dd.items():
                    if deps.get(k, (None, 0))[1] < v:
                        deps[k] = (s, v)
        return deps

    @staticmethod
    def _publish(tok, reads, writes):
        k = id(tok[0])
        for b in reads:
            if b.r.get(k, (None, 0))[1] < tok[1]:
                b.r[k] = tok
        for b in writes:
            b.w = {k: tok}
            b.r = {}

    def op(self, eng, fn, reads=(), writes=(), inc=True):
        for s, v in self._deps(reads, writes).values():
            eng.wait(s, v)
        ins = fn(eng.e)
        if not inc:
            eng.pend_r += list(reads)
            eng.pend_w += list(writes)
            return ins
        eng.count += 1
        ins.then_inc(eng.sem, 1)
        tok = (eng.sem, eng.count)
        self._publish(tok, list(reads) + eng.pend_r, list(writes) + eng.pend_w)
        eng.pend_r, eng.pend_w = [], []
        return ins

    def dma(self, q, out, in_, reads=(), writes=(), final=False, **kw):
        slot = self.dsem[self.dnext]
        self.dnext = (self.dnext + 1) % len(self.dsem)
        sem = slot[0]
        for s, v in self._deps(reads, writes).values():
            q.wait(s, v)
        if slot[1]:
            q.wait(sem, slot[1])
        slot[1] += 16
        q.e.dma_start(out=out, in_=in_, **kw).then_inc(sem, 16)
        tok = (sem, slot[1])
        self._publish(tok, reads, writes)
        if final:
            self.out_tokens.append(tok)

    def barrier(self):
        import os
        if os.environ.get('NOBAR'):
            return
        engs = [self.pe, self.act, self.dve, self.pool, self.sp]
        toks = [(e.sem, e.count) for e in engs if e.count] + [(s[0], s[1]) for s in self.dsem if s[1]]
        for e in engs:
            for s, v in toks:
                if s is not e.sem:
                    e.wait(s, v)

    def finish(self):
        for s, v in self.out_tokens:
            self.sp.wait(s, v)
        for slot in self.dsem:
            if slot[1]:
                self.sp.wait(slot[0], slot[1])


class T:
    def __init__(self, nc, name, shape, dtype, psum=False):
        self.t = (nc.alloc_psum_tensor if psum else nc.alloc_sbuf_tensor)(name, shape, dtype)
        self.b = Buf(name)

    def __getitem__(self, k):
        return self.t[k]


def _t5_buckets(rel):
    nb = 16
    max_exact = 8
    ret = (rel > 0).astype(np.int32) * nb
    n = np.abs(rel)
    large = max_exact + (np.log(np.maximum(n, 1) / max_exact) / math.log(128 / max_exact)
                         * (nb - max_exact)).astype(np.int32)
    large = np.minimum(large, nb - 1)
    return ret + np.where(n < max_exact, n, large)


def host_consts():
    c = {}
    c["ident"] = np.eye(128, dtype=np.float32)
    m = np.arange(512)
    rel = 255 - m
    bk = _t5_buckets(rel)
    valid = (np.abs(rel) <= 128) & (m < 511)
    oh8 = np.zeros((128, 512), np.float32)
    oh8[bk[valid], m[valid]] = 8.0
    c["oh8"] = oh8
    c["mneg"] = np.ascontiguousarray(np.where(valid, 0.0, -30000.0).astype(np.float32)[None].repeat(8, 0))
    c["antid"] = np.ascontiguousarray(np.eye(128, dtype=np.float32)[::-1])
    j = np.arange(128)[:, None]
    l = np.arange(128)[None, :]
    c["ssd_masks"] = np.ascontiguousarray(np.stack([j <= l, j >= l, j > l, j < l, np.ones((128, 128), bool)], 1).astype(np.float32))
    return c


def hy_consts(L):
    import ml_dtypes
    bf = ml_dtypes.bfloat16
    A = L // 64
    N = 2 * L
    KA = 2 * A
    c = {}
    a = np.arange(KA)[:, None]
    ka = np.arange(KA)[None, :]
    th = 2 * np.pi * a * ka / KA
    FA = np.zeros((128, 2, 128))
    FA[:KA, 0, :KA] = np.cos(th)
    FA[:KA, 1, :KA] = -np.sin(th)
    b = np.arange(64)[:, None]
    kb = np.arange(64)[None, :]
    GB = np.zeros((KA, 128, 128))
    GBsw = np.zeros((KA, 128, 128))
    GP1 = np.zeros((KA, 128, 128))
    GP2 = np.zeros((KA, 128, 128))
    for k in range(KA):
        t = 2 * np.pi * (b * k / N + b * kb / 64)
        Gr, Gi = np.cos(t), -np.sin(t)
        GB[k] = np.block([[Gr, Gi], [-Gi, Gr]])
        GBsw[k] = np.block([[Gi, Gr], [Gr, -Gi]])
        Pr, Pi = np.cos(t.T), np.sin(t.T)
        GP1[k] = np.block([[Pr, Pi], [-Pr, -Pi]])
        GP2[k] = np.block([[-Pi, Pr], [-Pi, Pr]])
    th2 = 2 * np.pi * np.arange(A)[None, :] * np.arange(KA)[:, None] / KA
    wk = np.full((KA, 1), 2.0)
    wk[0] = 1.0
    wk[A] = 1.0
    wk[A + 1:] = 0.0
    FP = np.zeros((128, 2, 128))
    FP[:KA, 0, :A] = wk * np.cos(th2) / N
    FP[:KA, 1, :A] = -wk * np.sin(th2) / N
    for nm, v in (("FA", FA), ("FP", FP), ("GB", GB), ("GBsw", GBsw), ("GP1", GP1), ("GP2", GP2)):
        c[f"hy_{nm}{L}"] = np.ascontiguousarray(v.astype(np.float32).astype(bf))
    tt = np.linspace(0.0, 1.0, L, dtype=np.float32)[:, None]
    pos = np.arange(L, dtype=np.float32)[:, None]
    bands = np.linspace(1e-4, 15, 16, dtype=np.float32)[None, :]
    ang = (2.0 * math.pi * pos * bands / L).astype(np.float32)
    zemb = np.concatenate([tt, np.cos(ang), -np.sin(ang)], axis=-1)
    zp = np.zeros((128, L), np.float32)
    zp[:33] = zemb.T
    c[f"hy_zemb{L}"] = np.ascontiguousarray(zp.astype(bf))
    max_decay = math.log(1e-2) / 0.3
    min_decay = math.log(1e-2) / 1.5
    deltas = np.linspace(min_decay, max_decay, 512, dtype=np.float32)
    c[f"hy_decay{L}"] = np.ascontiguousarray(np.exp(-tt * np.abs(deltas)[None, :]).astype(np.float32))
    return c


def build(seqs, depth=DEPTH, dbg_out=(), dbg_in=(), phases=("inproj", "ssd", "att", "hy", "final"), stop=99):
    nc = bass.Bass("TRN2", target_bir_lowering=False)
    cx = Ctx(nc)
    pe, act, dve, pool, sp = cx.pe, cx.act, cx.dve, cx.pool, cx.sp

    def dram(name, shape, dt, kind="Internal"):
        if name in dbg_out:
            kind = "ExternalOutput"
        if name in dbg_in:
            kind = "ExternalInput"
        return nc.dram_tensor(name, list(shape), dt, kind=kind).ap()

    nseq = len(seqs)
    uid = [0]

    def sbt(name, shape, dt):
        uid[0] += 1
        return nc.sbuf_tensor(f"{name}_{uid[0]}", shape, dt)

    xin = [dram(f"x{s}", [L, D], F32, "ExternalInput") for s, L in enumerate(seqs)]
    yout = [dram(f"y{s}", [L, D], F32, "ExternalOutput") for s, L in enumerate(seqs)]
    w_in = dram("w_in", [depth, D, NCOL], F32, "ExternalInput")
    w_out = dram("w_out", [depth, D, D], F32, "ExternalInput")
    norm_g = dram("norm_g", [depth, D], F32, "ExternalInput")
    brg = dram("brg", [depth, D], F32, "ExternalInput")
    fin_g = dram("fin_g", [1, D], F32, "ExternalInput")
    ident_d = dram("ident", [128, 128], F32, "ExternalInput")

    PAD = 2
    xs_ = [[dram(f"xs{s}_{i}", [L, D], F32) for i in range(2)] for s, L in enumerate(seqs)]
    u_ = [dram(f"u{s}", [L + 2 * PAD, UW], BF16) for s, L in enumerate(seqs)]
    dtr_ = [dram(f"dtr{s}", [L, 32], F32) for s, L in enumerate(seqs)]
    qk_ = [dram(f"qk{s}", [640, L], BF16) for s, L in enumerate(seqs)]
    ybr_ = [dram(f"ybr{s}", [L, D], BF16) for s, L in enumerate(seqs)]
    xb_ = [[[Buf() for _ in range(L // 128)] for i in range(2)] for L in seqs]
    ub_ = [[Buf() for _ in range(L // 128 + 2)] for L in seqs]
    dtb_ = [[Buf() for _ in range(L // 128)] for L in seqs]
    qkb_ = [[Buf() for _ in range(L // 512)] for L in seqs]
    ybb_ = [[[Buf() for _ in range(3)] for _ in range(L // 128)] for L in seqs]

    dtall = [T(nc, f"dtall{s}", [128, L // 128, 32], F32) for s, L in enumerate(seqs)]
    ident_f = T(nc, "ident_f", [128, 128], F32)
    ident_b = T(nc, "ident_b", [128, 128], BF16)
    cx.dma(sp, ident_f[:], ident_d[:, :], writes=[ident_f.b])
    cx.op(dve, lambda e: e.tensor_copy(out=ident_b[:], in_=ident_f[:]), [ident_f.b], [ident_b.b])
    antid_d = dram("antid", [128, 128], F32, "ExternalInput")
    anti_f = T(nc, "anti_f", [128, 128], F32)
    anti_b = T(nc, "anti_b", [128, 128], BF16)
    cx.dma(sp, anti_f[:], antid_d[:, :], writes=[anti_f.b])
    cx.op(dve, lambda e: e.tensor_copy(out=anti_b[:], in_=anti_f[:]), [anti_f.b], [anti_b.b])
    zrow = T(nc, "zrow", [128, 82], BF16)
    cx.op(dve, lambda e: e.memset(zrow[:], 0.0), [], [zrow.b])
    for s, L in enumerate(seqs):
        cx.dma(sp, u_[s][0:PAD, :].rearrange("a (p f) -> (a p) f", f=82), zrow[:], reads=[zrow.b], writes=[ub_[s][0]])
        cx.dma(sp, u_[s][L + PAD:L + 2 * PAD, :].rearrange("a (p f) -> (a p) f", f=82), zrow[:], reads=[zrow.b],
               writes=[ub_[s][-1]])

    banks = [T(nc, f"bank{i}", [128, 512], F32, psum=True) for i in range(8)]

    def phase_inproj(layer, s, L, xsrc, xsrc_b):
        MG = 2048 if L % 2048 == 0 else L
        ntile = MG // 128
        with ExitStack() as es:
            hT_t = es.enter_context(sbt("hT", [128, 16, MG], BF16))
            wst0 = es.enter_context(sbt("wst0", [128, 8, 512], F32))
            wst1 = es.enter_context(sbt("wst1", [128, 8, 512], F32))
            wb0 = es.enter_context(sbt("wb0", [128, 16, 512], BF16))
            wb1 = es.enter_context(sbt("wb1", [128, 16, 512], BF16))
            xt0 = es.enter_context(sbt("xt0", [128, D], F32))
            xt1 = es.enter_context(sbt("xt1", [128, D], F32))
            hb_t = es.enter_context(sbt("hb", [128, D], BF16))
            gcol_t = es.enter_context(sbt("gcol", [128, 16], F32))
            st_t = es.enter_context(sbt("st4", [128, 4], F32))
            junk_t = es.enter_context(sbt("junk", [128, D], BF16))
            ot0 = es.enter_context(sbt("ot0", [128, 512], BF16))
            ot1 = es.enter_context(sbt("ot1", [128, 512], BF16))
            ot2 = es.enter_context(sbt("ot2", [128, 512], BF16))
            ot3 = es.enter_context(sbt("ot3", [128, 512], BF16))
            dto0 = es.enter_context(sbt("dto0", [128, 32], F32))
            dto1 = es.enter_context(sbt("dto1", [128, 32], F32))
            hT, gcol, st, hb, junk = hT_t, gcol_t, st_t, hb_t, junk_t
            bhT = [Buf() for _ in range(ntile)]
            bg, bst, bhb, bjunk = Buf(), Buf(), Buf(), Buf()
            wst = [(wst0, Buf()), (wst1, Buf())]
            wb = [(wb0, Buf()), (wb1, Buf())]
            xt = [(xt0, Buf()), (xt1, Buf())]
            ot = [(o, Buf()) for o in (ot0, ot1, ot2, ot3)]
            dto = [(dto0, Buf()), (dto1, Buf())]
            cx.dma(sp, gcol[:], norm_g[layer].rearrange("(c p) -> p c", p=128), writes=[bg],
                   allow_slow_non_contiguous=True)
            wv = w_in[layer].rearrange("(c p) n -> p c n", p=128)
            cnt = {"ot": 0, "bank": 0, "ev": 0, "dto": 0, "wst": 0}
            blocks = [(c0, 512) for c0 in range(0, 5120, 512)] + [(5120, 160)] + \
                     [(C_QK + 128 * i, 128) for i in range(5)]
            items = [(mg, bi) for mg in range(L // MG) for bi in range(len(blocks))]

            def prep_mg(mg):
                t0 = mg * MG
                for ti in range(ntile):
                    gt = (t0 // 128) + ti
                    xtile, xbuf = xt[ti % 2]
                    cx.dma(sp, xtile[:], xsrc[gt * 128:(gt + 1) * 128, :], reads=[xsrc_b[gt]], writes=[xbuf])
                    cx.op(act, lambda e: e.activation(out=junk[:], in_=xtile[:], func=AF.Square,
                                                      accum_out=st[:, 0:1]), [xbuf], [bjunk, bst])
                    cx.op(dve, lambda e: e.tensor_scalar(out=st[:, 1:2], in0=st[:, 0:1], scalar1=1.0 / D,
                                                         scalar2=EPS, op0=ALU.mult, op1=ALU.add), [bst], [bst])
                    cx.op(act, lambda e: e.activation(out=st[:, 2:3], in_=st[:, 1:2], func=AF.Sqrt), [bst], [bst])
                    cx.op(dve, lambda e: e.reciprocal(out=st[:, 3:4], in_=st[:, 2:3]), [bst], [bst])
                    cx.op(act, lambda e: e.activation(out=hb[:], in_=xtile[:], func=AF.Copy, scale=st[:, 3:4]),
                          [xbuf, bst], [bhb])
                    for q4 in range(4):
                        bk = banks[2 + cnt["bank"] % 6]
                        cnt["bank"] += 1
                        for j in range(4):
                            c = q4 * 4 + j
                            cx.op(pe, lambda e: e.matmul(bk[:, j * 128:(j + 1) * 128], hb[:, c * 128:(c + 1) * 128],
                                                         ident_b[:, :], start=(j == 0), stop=(j == 3)),
                                  [bhb, ident_b.b], [bk.b], inc=(j == 3))
                        src = bk[:, :].rearrange("p (c t) -> p c t", c=4)
                        dst = hT[:, q4 * 4:(q4 + 1) * 4, ti * 128:(ti + 1) * 128]
                        if q4 % 2 == 0:
                            cx.op(act, lambda e: e.activation(out=dst, in_=src, func=AF.Copy), [bk.b], [bhT[ti]])
                        else:
                            cx.op(dve, lambda e: e.tensor_copy(out=dst, in_=src), [bk.b], [bhT[ti]])

            def load_block(k):
                mg, bi = items[k]
                c0, cw = blocks[bi]
                wbt, wbb = wb[k % 2]
                for hlf in range(2):
                    wt, wtb = wst[cnt["wst"] % 2]
                    cnt["wst"] += 1
                    cx.dma(sp, wt[:, :, 0:cw], wv[:, hlf * 8:(hlf + 1) * 8, c0:c0 + cw], writes=[wtb])
                    for c in range(8):
                        cc = hlf * 8 + c
                        if c % 8 < 5:
                            cx.op(dve, lambda e: e.tensor_scalar(out=wbt[:, cc, 0:cw], in0=wt[:, c, 0:cw],
                                                                 scalar1=gcol[:, cc:cc + 1], scalar2=None, op0=ALU.mult),
                                  [wtb, bg], [wbb])
                        else:
                            cx.op(pool, lambda e: e.tensor_scalar(out=wbt[:, cc, 0:cw], in0=wt[:, c, 0:cw],
                                                                  scalar1=gcol[:, cc:cc + 1], scalar2=0.0, op0=ALU.mult,
                                                                  op1=ALU.add), [wtb, bg], [wbb])

            def compute_block(k):
                mg, bi = items[k]
                t0 = mg * MG
                c0, cw = blocks[bi]
                wbt, wbb = wb[k % 2]
                if c0 < C_QK:
                    for ti in range(ntile):
                        gt = (t0 // 128) + ti
                        bk = banks[2 + cnt["bank"] % 6]
                        cnt["bank"] += 1
                        for c in range(16):
                            cx.op(pe, lambda e: e.matmul(bk[:, 0:cw], hT[:, c, ti * 128:(ti + 1) * 128],
                                                         wbt[:, c, 0:cw], start=(c == 0), stop=(c == 15)),
                                  [bhT[ti], wbb], [bk.b], inc=(c == 15))
                        otile, ob = ot[cnt["ot"] % 4]
                        cnt["ot"] += 1
                        nb16 = min(cw, 512) if c0 < 5120 else 128
                        cx.op(act, lambda e: e.activation(out=otile[:, 0:nb16], in_=bk[:, 0:nb16], func=AF.Copy),
                              [bk.b], [ob])
                        cx.dma(sp, u_[s][PAD + gt * 128:PAD + (gt + 1) * 128, c0:c0 + nb16], otile[:, 0:nb16],
                               reads=[ob], writes=[ub_[s][gt + 1]])
                        if c0 == 5120:
                            cx.op(dve, lambda e: e.tensor_copy(out=dtall[s][:, gt, :], in_=bk[:, 128:160]),
                                  [bk.b], [dtall[s].b])
                else:
                    qi = (c0 - C_QK) // 128
                    for tg in range(MG // 512):
                        bk = banks[2 + cnt["bank"] % 6]
                        cnt["bank"] += 1
                        tiles = [bhT[tg * 4 + j] for j in range(4)]
                        for c in range(16):
                            cx.op(pe, lambda e: e.matmul(bk[:, :], wbt[:, c, 0:128], hT[:, c, tg * 512:(tg + 1) * 512],
                                                         start=(c == 0), stop=(c == 15)),
                                  tiles + [wbb], [bk.b], inc=(c == 15))
                        otile, ob = ot[cnt["ot"] % 4]
                        cnt["ot"] += 1
                        cx.op(act, lambda e: e.activation(out=otile[:, :], in_=bk[:, :], func=AF.Copy), [bk.b], [ob])
                        g512 = (t0 + tg * 512) // 512
                        cx.dma(sp, qk_[s][qi * 128:(qi + 1) * 128, t0 + tg * 512:t0 + (tg + 1) * 512], otile[:, :],
                               reads=[ob], writes=[qkb_[s][g512]])

            load_block(0)
            for k in range(len(items)):
                if items[k][1] == 0:
                    prep_mg(items[k][0])
                if k + 1 < len(items):
                    load_block(k + 1)
                compute_block(k)

    def phase_final(layer, seqargs, last):
        with ExitStack() as es:
            wo = es.enter_context(sbt("wo", [128, 16, D], BF16))
            wos0 = es.enter_context(sbt("wos0", [128, 1, D], F32))
            wos1 = es.enter_context(sbt("wos1", [128, 1, D], F32))
            gbc = es.enter_context(sbt("gbc", [128, D], F32))
            fgbc = es.enter_context(sbt("fgbc", [128, D], F32))
            yb0 = es.enter_context(sbt("yb0", [128, D], BF16))
            yb1 = es.enter_context(sbt("yb1", [128, D], BF16))
            gt0 = es.enter_context(sbt("gt0", [128, D], BF16))
            gt1 = es.enter_context(sbt("gt1", [128, D], BF16))
            sg = es.enter_context(sbt("sg", [128, D], F32))
            yg = es.enter_context(sbt("yg", [128, D], F32))
            ynb = es.enter_context(sbt("ynb", [128, D], BF16))
            yT = es.enter_context(sbt("yT", [128, 16, 128], BF16))
            yT2 = es.enter_context(sbt("yT2", [128, 16, 128], BF16))
            ynb2 = es.enter_context(sbt("ynb2", [128, D], BF16))
            fst2 = es.enter_context(sbt("fst2", [128, 4], F32))
            bfst2 = Buf()
            xr0 = es.enter_context(sbt("xr0", [128, D], F32))
            xr1 = es.enter_context(sbt("xr1", [128, D], F32))
            xo0 = es.enter_context(sbt("xo0", [128, D], F32))
            fst = es.enter_context(sbt("fst", [128, 16], F32))
            junkf = es.enter_context(sbt("junkf", [128, D], BF16))
            bwo, bgbc, bfg, bsg, byg, bynb, byT, bfst, bjk = (Buf() for _ in range(9))
            wos = [(wos0, Buf()), (wos1, Buf())]
            ybt = [(yb0, Buf()), (yb1, Buf())]
            gtt = [(gt0, Buf()), (gt1, Buf())]
            xrt = [(xr0, Buf()), (xr1, Buf())]
            xot = [(xo0, Buf()), (xo0, None)]
            wv = w_out[layer].rearrange("(c p) n -> p c n", p=128)
            for c2 in range(16):
                wt, wtb = wos[c2 % 2]
                cx.dma(sp, wt[:], wv[:, c2:c2 + 1, :], writes=[wtb])
                if c2 % 2 == 0:
                    cx.op(dve, lambda e: e.tensor_copy(out=wo[:, c2:c2 + 1, :], in_=wt[:]), [wtb], [bwo])
                else:
                    cx.op(act, lambda e: e.activation(out=wo[:, c2:c2 + 1, :], in_=wt[:], func=AF.Copy), [wtb], [bwo])
            cx.dma(sp, gbc[:], brg[layer:layer + 1, :].broadcast_to([128, D]), writes=[bgbc], allow_slow_non_contiguous=True)
            if last:
                cx.dma(sp, fgbc[:], fin_g[0:1, :].broadcast_to([128, D]), writes=[bfg], allow_slow_non_contiguous=True)
            bynb2, byT2 = Buf(), Buf()

            def run(s, L, xsrc, xsrc_b, xdst, xdst_b):
                segs = [(0, 1024), (1024, 1536), (1536, 2048)]
                gcols = [(C_Z, 1024, 0), (C_GA, 512, 1024), (C_GH, 512, 1536)]
                bank_i = 0
                def fin_loads(ti):
                    r0, r1 = ti * 128, (ti + 1) * 128
                    yt, yb = ybt[ti % 2]
                    gt, gb = gtt[ti % 2]
                    xr, xrb = xrt[ti % 2]
                    q2 = pool if os.environ.get("KQ2") else sp
                    cx.dma(q2, yt[:], ybr_[s][r0:r1, :], reads=ybb_[s][ti], writes=[yb])
                    for (cs, cw, do) in gcols:
                        cx.dma(sp, gt[:, do:do + cw], u_[s][PAD + r0:PAD + r1, cs:cs + cw], reads=[ub_[s][ti + 1]], writes=[gb])
                    cx.dma(q2, xr[:], xsrc[r0:r1, :], reads=[xsrc_b[ti]], writes=[xrb])

                ynbs = [(ynb, bynb), (ynb2, bynb2)]
                yTs = [(yT, byT), (yT2, byT2)]
                bcnt = [0]

                def nbk():
                    bcnt[0] += 1
                    return banks[bcnt[0] % 8]

                def fa_elem(ti):
                    yt, yb = ybt[ti % 2]
                    gt, gb = gtt[ti % 2]
                    ynb_, bynb_ = ynbs[ti % 2]
                    cx.op(act, lambda e: e.activation(out=sg[:], in_=gt[:], func=AF.Silu), [gb], [bsg])
                    cx.op(dve, lambda e: e.tensor_tensor(out=yg[:], in0=yt[:], in1=sg[:], op=ALU.mult), [yb, bsg], [byg])
                    for k, (a, b) in enumerate(segs):
                        cx.op(act, lambda e: e.activation(out=junkf[:, a:b], in_=yg[:, a:b], func=AF.Square,
                                                          accum_out=fst[:, k:k + 1]), [byg], [bjk, bfst])
                        cx.op(dve, lambda e: e.tensor_scalar(out=fst[:, 4 + k:5 + k], in0=fst[:, k:k + 1],
                                                             scalar1=1.0 / (b - a), scalar2=EPS, op0=ALU.mult, op1=ALU.add),
                              [bfst], [bfst])
                    cx.op(act, lambda e: e.activation(out=fst[:, 8:11], in_=fst[:, 4:7], func=AF.Sqrt), [bfst], [bfst])
                    cx.op(dve, lambda e: e.reciprocal(out=fst[:, 12:15], in_=fst[:, 8:11]), [bfst], [bfst])
                    for k, (a, b) in enumerate(segs):
                        cx.op(dve, lambda e: e.scalar_tensor_tensor(out=ynb_[:, a:b], in0=yg[:, a:b], scalar=fst[:, 12 + k:13 + k],
                                                                    in1=gbc[:, a:b], op0=ALU.mult, op1=ALU.mult),
                              [byg, bfst, bgbc], [bynb_])

                def fa_tr(ti):
                    ynb_, bynb_ = ynbs[ti % 2]
                    yT_, byT_ = yTs[ti % 2]
                    for q4 in range(4):
                        bk = nbk()
                        for j in range(4):
                            c = q4 * 4 + j
                            cx.op(pe, lambda e: e.matmul(bk[:, j * 128:(j + 1) * 128], ynb_[:, c * 128:(c + 1) * 128],
                                                         ident_b[:, :], start=(j == 0), stop=(j == 3)),
                                  [bynb_, ident_b.b], [bk.b], inc=(j == 3))
                        src = bk[:, :].rearrange("p (c t) -> p c t", c=4)
                        if q4 % 2 == 0:
                            cx.op(act, lambda e: e.activation(out=yT_[:, q4 * 4:(q4 + 1) * 4, :], in_=src, func=AF.Copy), [bk.b], [byT_])
                        else:
                            cx.op(dve, lambda e: e.tensor_copy(out=yT_[:, q4 * 4:(q4 + 1) * 4, :], in_=src), [bk.b], [byT_])

                def fb_mm(ti):
                    yT_, byT_ = yTs[ti % 2]
                    bks = []
                    for nb in range(4):
                        bk = nbk()
                        bks.append(bk)
                        for c in range(16):
                            cx.op(pe, lambda e: e.matmul(bk[:, :], yT_[:, c, :], wo[:, c, nb * 512:(nb + 1) * 512],
                                                         start=(c == 0), stop=(c == 15)), [byT_, bwo], [bk.b], inc=(c == 15))
                    return bks

                def fb_post(ti, bks):
                    r0, r1 = ti * 128, (ti + 1) * 128
                    xr, xrb = xrt[ti % 2]
                    xo, xob = xot[0]
                    for nb in range(4):
                        bk = bks[nb]
                        cx.op(dve, lambda e: e.tensor_tensor(out=xo[:, nb * 512:(nb + 1) * 512], in0=bk[:, :],
                                                             in1=xr[:, nb * 512:(nb + 1) * 512], op=ALU.add), [bk.b, xrb], [xob])
                    if last:
                        cx.op(act, lambda e: e.activation(out=junkf[:], in_=xo[:], func=AF.Square, accum_out=fst2[:, 0:1]),
                              [xob], [bjk, bfst2])
                        cx.op(dve, lambda e: e.tensor_scalar(out=fst2[:, 1:2], in0=fst2[:, 0:1], scalar1=1.0 / D, scalar2=EPS,
                                                             op0=ALU.mult, op1=ALU.add), [bfst2], [bfst2])
                        cx.op(act, lambda e: e.activation(out=fst2[:, 2:3], in_=fst2[:, 1:2], func=AF.Sqrt), [bfst2], [bfst2])
                        cx.op(dve, lambda e: e.reciprocal(out=fst2[:, 3:4], in_=fst2[:, 2:3]), [bfst2], [bfst2])
                        cx.op(dve, lambda e: e.scalar_tensor_tensor(out=xo[:], in0=xo[:], scalar=fst2[:, 3:4], in1=fgbc[:],
                                                                    op0=ALU.mult, op1=ALU.mult), [xob, bfst2, bfg], [xob])
                        cx.dma(stq, yout[s][r0:r1, :], xo[:], reads=[xob], writes=[], final=True)
                    else:
                        cx.dma(stq, xdst[r0:r1, :], xo[:], reads=[xob], writes=[xdst_b[ti]])

                nt = L // 128
                stq = sp if os.environ.get("KNOSTQ") else pool
                fin_loads(0)
                fa_elem(0)
                fa_tr(0)
                for ti in range(nt):
                    if ti + 1 < nt:
                        fin_loads(ti + 1)
                        fa_elem(ti + 1)
                    bks = fb_mm(ti)
                    if ti + 1 < nt:
                        fa_tr(ti + 1)
                    fb_post(ti, bks)

            for args in seqargs:
                run(*args)

    masks_d = dram("ssd_masks", [128, 5, 128], F32, "ExternalInput")
    ssd_cw = dram("ssd_conv_w", [depth, 5, 1536], F32, "ExternalInput")
    ssd_cb = dram("ssd_conv_b", [depth, 1, 1536], F32, "ExternalInput")
    ssd_dtb = dram("ssd_dt_bias", [depth, 1, 32], F32, "ExternalInput")
    ssd_al = dram("ssd_a_log", [depth, 1, 32], F32, "ExternalInput")
    ssd_dd = dram("ssd_d", [depth, 1, 16], F32, "ExternalInput")
    xact_ = [dram(f"xact{s}", [L, 1536], BF16) for s, L in enumerate(seqs)]
    hbs_ = [dram(f"hbs{s}", [L // 128, 128, 1024], BF16) for s, L in enumerate(seqs)]
    xab_ = [[Buf() for _ in range(L // 128)] for L in seqs]
    hbb_ = [[Buf() for _ in range(L // 128)] for L in seqs]
    mk_f = T(nc, "mk_f", [128, 5, 128], F32)
    mk_b = T(nc, "mk_b", [128, 5, 128], BF16)
    cx.dma(sp, mk_f[:], masks_d[:, :, :], writes=[mk_f.b])
    cx.op(dve, lambda e: e.tensor_copy(out=mk_b[:], in_=mk_f[:]), [mk_f.b], [mk_b.b])

    def phase_ssd(layer, s, L):
        nb = L // 128
        with ExitStack() as es:
            A = lambda name, shape, dt: es.enter_context(sbt(name, shape, dt))
            wbc = A("wbc", [128, 5, 1536], F32)
            bbc = A("bbc", [128, 1536], F32)
            dtbb = A("dtbb", [128, 32], F32)
            abc = A("abc", [128, 32], F32)
            d16 = A("d16", [128, 16], F32)
            dbc = A("dbc", [128, 16, 64], F32)
            bconst = Buf()
            for k in range(5):
                cx.dma(sp, wbc[:, k, :], ssd_cw[layer, k:k + 1, :].broadcast_to([128, 1536]), writes=[bconst],
                       allow_slow_non_contiguous=True)
            cx.dma(sp, bbc[:], ssd_cb[layer, 0:1, :].broadcast_to([128, 1536]), writes=[bconst], allow_slow_non_contiguous=True)
            cx.dma(sp, dtbb[:], ssd_dtb[layer, 0:1, :].broadcast_to([128, 32]), writes=[bconst], allow_slow_non_contiguous=True)
            cx.dma(sp, abc[:], ssd_al[layer, 0:1, :].broadcast_to([128, 32]), writes=[bconst], allow_slow_non_contiguous=True)
            cx.dma(sp, d16[:], ssd_dd[layer, 0:1, :].broadcast_to([128, 16]), writes=[bconst], allow_slow_non_contiguous=True)
            cx.op(act, lambda e: e.activation(out=abc[:], in_=abc[:], func=AF.Exp), [bconst], [bconst])
            cx.op(dve, lambda e: e.tensor_scalar(out=abc[:], in0=abc[:], scalar1=-1.0, scalar2=None, op0=ALU.mult),
                  [bconst], [bconst])
            cx.op(dve, lambda e: e.tensor_copy(out=dbc[:], in_=d16[:].unsqueeze(2).broadcast_to([128, 16, 64])),
                  [bconst], [bconst])
            xk = [[(A(f"xk{k}_{p}", [128, 1536], BF16), Buf()) for k in range(5)] for p in range(2)]
            accs_ = [A(f"acc{p}", [128, 8], F32) for p in range(2)]
            tmps = [A(f"tmpc{k}", [128, 1536], BF16) for k in range(5)]
            btmps = [Buf() for _ in range(5)]
            bias_bf = A("bias_bf", [1, 1536], BF16)
            cx.op(dve, lambda e: e.tensor_copy(out=bias_bf[:], in_=bbc[0:1, :]), [bconst], [bconst])
            xa2 = [(A(f"xa{p}", [128, 1536], BF16), Buf()) for p in range(2)]
            dtss = [A(f"dts{p}", [128, 4, 32], F32) for p in range(2)]
            dhls = [A(f"dhl{p}", [128, 2, 32], BF16) for p in range(2)]
            exs = [A(f"ex{p}", [128, 96], F32) for p in range(2)]
            bmcms = [A(f"bmcm{p}", [128, 4, 128], BF16) for p in range(2)]
            cbms = [A(f"cbm{p}", [128, 2, 2, 128], F32) for p in range(2)]
            rhl = [A(f"rhl{p}", [128, 512], F32) for p in range(2)]
            et = [A(f"et{p}", [128, 512], F32) for p in range(3)]
            mts = [A(f"mt{p}", [128, 2, 16, 128], BF16) for p in range(2)]
            xdts = [A(f"xdt{p}", [128, 2, 1024], BF16) for p in range(2)]
            xdds = [A(f"xdd{p}", [128, 2, 1024], BF16) for p in range(2)]
            hst = A("hst", [128, 2, 2, 512], F32)
            hbf = A("hbf", [128, 2, 2, 512], BF16)
            hld = [A(f"hld{p}", [128, 1024], BF16) for p in range(2)]
            yt1 = A("yt1", [128, 512], F32)
            yo = [A(f"yo{p}", [128, 1024], F32) for p in range(2)]
            yob = [A(f"yob{p}", [128, 1024], BF16) for p in range(2)]
            byob = [Buf(), Buf()]
            btmp, bhst, bhbf, byt1 = (Buf() for _ in range(4))
            pb = [{n: Buf() for n in ("acc", "dts", "dhl", "ex", "bmcm", "cbm", "mt", "xdt", "xdd")} for _ in range(2)]
            acc = dts = dhl = ex = bmcm = cbm = mt = xdt = xdd = None
            bacc = bdts = bdhl = bex = bbmcm = bcbm = bmt = bxdt = bxdd = None

            def bind(par):
                nonlocal acc, dts, dhl, ex, bmcm, cbm, mt, xdt, xdd, bacc, bdts, bdhl, bex, bbmcm, bcbm, bmt, bxdt, bxdd
                acc, dts, dhl, ex, bmcm, cbm, mt, xdt, xdd = (accs_[par], dtss[par], dhls[par], exs[par], bmcms[par],
                                                             cbms[par], mts[par], xdts[par], xdds[par])
                d = pb[par]
                bacc, bdts, bdhl, bex, bbmcm, bcbm, bmt, bxdt, bxdd = (d["acc"], d["dts"], d["dhl"], d["ex"], d["bmcm"],
                                                                      d["cbm"], d["mt"], d["xdt"], d["xdd"])

            brhl = [Buf(), Buf()]
            bet = [Buf(), Buf(), Buf()]
            bhld = [Buf(), Buf()]
            byo = [Buf(), Buf()]
            cx.op(dve, lambda e: e.memset(hst[:], 0.0), [], [bhst])
            cx.op(dve, lambda e: e.memset(hbf[:], 0.0), [], [bhbf])
            bank_n = [0]

            def nbank():
                bank_n[0] += 1
                return banks[2 + bank_n[0] % 6]

            def dt_part(i):
                cx.op(dve, lambda e: e.tensor_tensor(out=dts[:, 2, :], in0=dtall[s][:, i, :], in1=dtbb[:], op=ALU.add),
                      [dtall[s].b, bconst], [bdts])
                cx.op(act, lambda e: e.activation(out=dts[:, 2, :], in_=dts[:, 2, :], func=AF.Exp), [bdts], [bdts])
                cx.op(act, lambda e: e.activation(out=dts[:, 0, :], in_=dts[:, 2, :], func=AF.Ln, bias=1.0), [bdts], [bdts])
                cx.op(dve, lambda e: e.tensor_tensor(out=dts[:, 1, :], in0=dts[:, 0, :], in1=abc[:], op=ALU.mult),
                      [bdts, bconst], [bdts])
                cx.op(dve, lambda e: e.tensor_copy(out=dhl[:, 0, :], in_=dts[:, 1, :]), [bdts], [bdhl])
                cx.op(dve, lambda e: e.tensor_tensor(out=dts[:, 2, :], in0=dts[:, 1, :], in1=dhl[:, 0, :], op=ALU.subtract),
                      [bdts, bdhl], [bdts])
                cx.op(dve, lambda e: e.tensor_copy(out=dhl[:, 1, :], in_=dts[:, 2, :]), [bdts], [bdhl])
                bk = nbank()
                plan = [(0, 0, 0), (1, 16, 16), (2, 32, 0), (3, 48, 16), (4, 64, 0), (4, 80, 16)]
                first = True
                for (mi, oc, ic) in plan:
                    for hl in range(2):
                        last = (mi, oc) == (4, 80) and hl == 1
                        cx.op(pe, lambda e: e.matmul(bk[:, oc:oc + 16], mk_b[:, mi, :], dhl[:, hl, ic:ic + 16],
                                                     start=first, stop=last), [mk_b.b, bdhl], [bk.b], inc=last)
                        first = False
                cx.op(act, lambda e: e.activation(out=ex[:], in_=bk[:, 0:96], func=AF.Exp), [bk.b], [bex])
                cx.op(dve, lambda e: e.tensor_tensor(out=dts[:, 3, :], in0=dts[:, 0, :], in1=ex[:, 32:64], op=ALU.mult),
                      [bdts, bex], [bdts])

            def a_loads(i):
                par = i % 2
                for k in range(5):
                    t, b = xk[par][k]
                    r0 = PAD + i * 128 + k - 2
                    rd = [ub_[s][i + 1]] + ([ub_[s][i]] if k < 2 else []) + ([ub_[s][i + 2]] if k > 2 else [])
                    cx.dma(sp, t[:], u_[s][r0:r0 + 128, C_XBC:C_XBC + 1536], reads=rd, writes=[b])

            a_loads(nb - 1)
            for i in range(nb - 1, -1, -1):
                par = i % 2
                bind(par)
                if i - 1 >= 0:
                    a_loads(i - 1)
                t0, b0 = xk[par][0]
                for k in range(3, 5):
                    tk, bk_ = xk[par][k]
                    cx.op(pool, lambda e: e.tensor_tensor(out=tmps[k][:], in0=tk[:], in1=wbc[:, k, :], op=ALU.mult),
                          [bk_, bconst], [btmps[k]])
                for k in range(0, 3):
                    tk, bk_ = xk[par][k]
                    cx.op(dve, lambda e: e.tensor_tensor(out=tmps[k][:], in0=tk[:], in1=wbc[:, k, :], op=ALU.mult),
                          [bk_, bconst], [btmps[k]])
                xa, xab = xa2[par]
                for cb in range(3):
                    bk = nbank()
                    for k in (0, 1, 2, 3, 4):
                        cx.op(pe, lambda e: e.matmul(bk[:, :], ident_b[:, :], tmps[k][:, cb * 512:(cb + 1) * 512],
                                                     start=(k == 0), stop=False), [ident_b.b, btmps[k]], [bk.b], inc=False)
                    cx.op(pe, lambda e: e.matmul(bk[:, :], mk_b[0:1, 4, :], bias_bf[0:1, cb * 512:(cb + 1) * 512],
                                                 start=False, stop=True), [mk_b.b, bconst], [bk.b])
                    cx.op(act, lambda e: e.activation(out=xa[:, cb * 512:(cb + 1) * 512], in_=bk[:, :], func=AF.Silu),
                          [bk.b], [xab])
                cx.dma(sp, xact_[s][i * 128:(i + 1) * 128, :], xa[:], reads=[xab], writes=[xab_[s][i]])
                cx.dma(sp, hbs_[s][i].rearrange("p (g c) -> p g c", g=2), hbf[:, 1, :, :], reads=[bhbf], writes=[hbb_[s][i]])
                if i == 0:
                    break
                dt_part(i)
                cx.op(pool, lambda e: e.tensor_tensor(
                    out=xdd[:, 1, :].rearrange("p (h c) -> p h c", h=16), in0=xa[:, 0:1024].rearrange("p (h c) -> p h c", h=16),
                    in1=dts[:, 3, 16:32].unsqueeze(2).broadcast_to([128, 16, 64]), op=ALU.mult), [xab, bdts], [bxdd])
                for g in range(2):
                    bk = nbank()
                    cx.op(pe, lambda e: e.matmul(bk[:, :], xa[:, 1024 + g * 128:1024 + (g + 1) * 128],
                                                 xdd[:, 1, g * 512:(g + 1) * 512], start=True, stop=True), [xab, bxdd], [bk.b])
                    cx.op(dve, lambda e: e.tensor_tensor(
                        out=hst[:, 1, g, :].rearrange("p (h c) -> p h c", h=8), in0=hst[:, 1, g, :].rearrange("p (h c) -> p h c", h=8),
                        in1=ex[:, 80 + g * 8:88 + g * 8].unsqueeze(2).broadcast_to([128, 8, 64]), op=ALU.mult), [bhst, bex], [bhst])
                    cx.op(dve, lambda e: e.tensor_tensor(out=hst[:, 1, g, :], in0=hst[:, 1, g, :], in1=bk[:, :], op=ALU.add),
                          [bhst, bk.b], [bhst])
                    cx.op(act, lambda e: e.activation(out=hbf[:, 1, g, :], in_=hst[:, 1, g, :], func=AF.Copy), [bhst], [bhbf])

            def b_loads(i):
                par = i % 2
                xa, xab = xa2[par]
                cx.dma(sp, xa[:], xact_[s][i * 128:(i + 1) * 128, :], reads=[xab_[s][i]], writes=[xab])
                if i < nb - 1:
                    cx.dma(sp, hld[par][:], hbs_[s][i], reads=[hbb_[s][i]], writes=[bhld[par]])

            def st1(i):
                par = i % 2
                bind(par)
                xa, xab = xa2[par]
                dt_part(i)
                bk = nbank()
                for j in range(4):
                    cx.op(pe, lambda e: e.matmul(bk[:, j * 128:(j + 1) * 128], xa[:, 1024 + j * 128:1024 + (j + 1) * 128],
                                                 ident_b[:, :], start=(j == 0), stop=(j == 3)), [xab, ident_b.b], [bk.b], inc=(j == 3))
                cx.op(act, lambda e: e.activation(out=bmcm[:].rearrange("p a b -> p (a b)"), in_=bk[:, :], func=AF.Copy), [bk.b], [bbmcm])
                bk = nbank()
                for g in range(2):
                    cx.op(pe, lambda e: e.matmul(bk[:, g * 128:(g + 1) * 128], bmcm[:, g, :], bmcm[:, 2 + g, :],
                                                 start=(g == 0), stop=(g == 1)), [bbmcm], [bk.b], inc=(g == 1))
                for d in range(2):
                    cx.op(dve, lambda e: e.tensor_tensor(
                        out=cbm[:, d, :, :], in0=bk[:, 0:256].rearrange("p (g l) -> p g l", g=2),
                        in1=mk_f[:, d, :].unsqueeze(1).broadcast_to([128, 2, 128]), op=ALU.mult), [bk.b, mk_f.b], [bcbm])
                for d in range(2):
                    cx.op(pool, lambda e: e.tensor_tensor(
                        out=xdt[:, d, :].rearrange("p (h c) -> p h c", h=16), in0=xa[:, 0:1024].rearrange("p (h c) -> p h c", h=16),
                        in1=dts[:, 0, d * 16:(d + 1) * 16].unsqueeze(2).broadcast_to([128, 16, 64]), op=ALU.mult), [xab, bdts], [bxdt])
                cx.op(pool, lambda e: e.tensor_tensor(
                    out=xdd[:, 0, :].rearrange("p (h c) -> p h c", h=16), in0=xa[:, 0:1024].rearrange("p (h c) -> p h c", h=16),
                    in1=dts[:, 3, 0:16].unsqueeze(2).broadcast_to([128, 16, 64]), op=ALU.mult), [xab, bdts], [bxdd])
                specs = [(d, h4) for d in range(2) for h4 in range(4)]

                def seg_front(q):
                    d, h4 = specs[q]
                    lm, rm = (2, 0) if d == 0 else (3, 1)
                    r_t, r_b = rhl[q % 2], brhl[q % 2]
                    e_t, e_b = et[q % 3], bet[q % 3]
                    cx.op(dve, lambda e: e.tensor_tensor(
                        out=r_t[:, :].rearrange("p (h l) -> p h l", h=4),
                        in0=mk_f[:, rm, :].unsqueeze(1).broadcast_to([128, 4, 128]),
                        in1=dts[:, 1, d * 16 + h4 * 4:d * 16 + h4 * 4 + 4].unsqueeze(2).broadcast_to([128, 4, 128]),
                        op=ALU.mult), [mk_f.b, bdts], [r_b])
                    bk = nbank()
                    cx.op(pe, lambda e: e.matmul(bk[:, :], mk_f[:, lm, :], r_t[:, :], start=True, stop=True),
                          [mk_f.b, r_b], [bk.b])
                    cx.op(act, lambda e: e.activation(out=e_t[:], in_=bk[:, :], func=AF.Exp), [bk.b], [e_b])

                def seg_back(q):
                    d, h4 = specs[q]
                    e_t, e_b = et[q % 3], bet[q % 3]
                    g = h4 // 2
                    cx.op(dve, lambda e: e.tensor_tensor(
                        out=mt[:, d, h4 * 4:(h4 + 1) * 4, :], in0=e_t[:].rearrange("p (h l) -> p h l", h=4),
                        in1=cbm[:, d, g, :].unsqueeze(1).broadcast_to([128, 4, 128]), op=ALU.mult), [e_b, bcbm], [bmt])

                for q in range(8 + 2):
                    if q < 8:
                        seg_front(q)
                    if q >= 2:
                        seg_back(q - 2)

            def st2(i):
                par = i % 2
                bind(par)
                xa, xab = xa2[par]
                hl_t, hl_b = hld[par], bhld[par]
                ybk = [banks[0], banks[1]]
                for g in range(2):
                    first = True
                    for d in range(2):
                        for h in range(8):
                            hh = g * 8 + h
                            last = (d == 1 and h == 7)
                            cx.op(pe, lambda e: e.matmul(ybk[g][:, h * 64:(h + 1) * 64], mt[:, d, hh, :],
                                                         xdt[:, d, hh * 64:(hh + 1) * 64], start=first, stop=last),
                                  [bmt, bxdt], [ybk[g].b], inc=last)
                            first = False
                yo_t, yo_b = yo[par], byo[par]
                for g in range(2):
                    cx.op(dve, lambda e: e.tensor_tensor(out=yo_t[:, g * 512:(g + 1) * 512], in0=xa[:, g * 512:(g + 1) * 512],
                                                         in1=dbc[:, g * 8:(g + 1) * 8, :].rearrange("p h c -> p (h c)"), op=ALU.mult),
                          [xab, bconst], [yo_b])
                    cx.op(dve, lambda e: e.tensor_tensor(out=yo_t[:, g * 512:(g + 1) * 512], in0=yo_t[:, g * 512:(g + 1) * 512],
                                                         in1=ybk[g][:, :], op=ALU.add), [yo_b, ybk[g].b], [yo_b])
                    for d in range(2):
                        if (d == 0 and i == 0) or (d == 1 and i == nb - 1):
                            continue
                        bk = nbank()
                        rhs = hbf[:, 0, g, :] if d == 0 else hl_t[:, g * 512:(g + 1) * 512]
                        rb_ = bhbf if d == 0 else hl_b
                        cx.op(pe, lambda e: e.matmul(bk[:, :], bmcm[:, 2 + g, :], rhs, start=True, stop=True), [bbmcm, rb_], [bk.b])
                        cx.op(dve, lambda e: e.tensor_tensor(
                            out=yt1[:].rearrange("p (h c) -> p h c", h=8), in0=bk[:, :].rearrange("p (h c) -> p h c", h=8),
                            in1=ex[:, d * 16 + g * 8:d * 16 + g * 8 + 8].unsqueeze(2).broadcast_to([128, 8, 64]), op=ALU.mult),
                              [bk.b, bex], [byt1])
                        cx.op(dve, lambda e: e.tensor_tensor(out=yo_t[:, g * 512:(g + 1) * 512], in0=yo_t[:, g * 512:(g + 1) * 512],
                                                             in1=yt1[:], op=ALU.add), [yo_b, byt1], [yo_b])
                cx.op(act, lambda e: e.activation(out=yob[par][:], in_=yo_t[:], func=AF.Copy), [yo_b], [byob[par]])
                cx.dma(sp, ybr_[s][i * 128:(i + 1) * 128, 0:1024], yob[par][:], reads=[byob[par]], writes=[ybb_[s][i][0]])
                if i < nb - 1:
                    for g in range(2):
                        bk = nbank()
                        cx.op(pe, lambda e: e.matmul(bk[:, :], xa[:, 1024 + g * 128:1024 + (g + 1) * 128],
                                                     xdd[:, 0, g * 512:(g + 1) * 512], start=True, stop=True), [xab, bxdd], [bk.b])
                        cx.op(dve, lambda e: e.tensor_tensor(
                            out=hst[:, 0, g, :].rearrange("p (h c) -> p h c", h=8), in0=hst[:, 0, g, :].rearrange("p (h c) -> p h c", h=8),
                            in1=ex[:, 64 + g * 8:72 + g * 8].unsqueeze(2).broadcast_to([128, 8, 64]), op=ALU.mult), [bhst, bex], [bhst])
                        cx.op(dve, lambda e: e.tensor_tensor(out=hst[:, 0, g, :], in0=hst[:, 0, g, :], in1=bk[:, :], op=ALU.add),
                              [bhst, bk.b], [bhst])
                        cx.op(act, lambda e: e.activation(out=hbf[:, 0, g, :], in_=hst[:, 0, g, :], func=AF.Copy), [bhst], [bhbf])

            b_loads(0)
            st1(0)
            for i in range(nb):
                if i + 1 < nb:
                    b_loads(i + 1)
                    st1(i + 1)
                st2(i)

    hy_cw = dram("hy_conv_w", [depth, 3, 1536], F32, "ExternalInput")
    hy_cb = dram("hy_conv_b", [depth, 1, 1536], F32, "ExternalInput")
    hy_w123 = dram("hy_w123", [depth, 3, 128, 128], F32, "ExternalInput")
    hy_w4 = dram("hy_w4", [depth, 128, 1024], F32, "ExternalInput")
    hy_fb = dram("hy_fb", [depth, 128, 4], F32, "ExternalInput")
    hy_dd = dram("hy_d", [depth, 1, 512], F32, "ExternalInput")
    uniqL = sorted(set(seqs))
    hyc = {}
    for L in uniqL:
        KA = L // 32
        hyc[L] = dict(
            zemb=dram(f"hy_zemb{L}", [128, L], BF16, "ExternalInput"),
            decay=dram(f"hy_decay{L}", [L, 512], F32, "ExternalInput"),
            FA=dram(f"hy_FA{L}", [128, 2, 128], BF16, "ExternalInput"),
            FP=dram(f"hy_FP{L}", [128, 2, 128], BF16, "ExternalInput"),
            GB=dram(f"hy_GB{L}", [KA, 128, 128], BF16, "ExternalInput"),
            GBsw=dram(f"hy_GBsw{L}", [KA, 128, 128], BF16, "ExternalInput"),
            GP1=dram(f"hy_GP1{L}", [KA, 128, 128], BF16, "ExternalInput"),
            GP2=dram(f"hy_GP2{L}", [KA, 128, 128], BF16, "ExternalInput"),
        )
    hxa_ = [dram(f"hxa{s}", [L, 512], F32) for s, L in enumerate(seqs)]
    huu_ = [dram(f"huu{s}", [L, 512], F32) for s, L in enumerate(seqs)]
    huub_ = [dram(f"huub{s}", [L, 512], BF16) for s, L in enumerate(seqs)]
    kern_ = [dram(f"kern{s}", [2 * L, 512], BF16) for s, L in enumerate(seqs)]
    yu_ = [dram(f"hyu{s}", [2, 64, L // 32, 512], BF16) for s, L in enumerate(seqs)]
    yk_ = [dram(f"hyk{s}", [2, 64, L // 32, 512], BF16) for s, L in enumerate(seqs)]
    khd_ = [dram(f"hkh{s}", [128, L // 32, 512], BF16) for s, L in enumerate(seqs)]
    kswd_ = [dram(f"hksw{s}", [128, L // 32, 512], BF16) for s, L in enumerate(seqs)]
    zd_ = [dram(f"hzd{s}", [2, L // 32, 64, 512], BF16) for s, L in enumerate(seqs)]
    PI = math.pi

    def phase_hy(layer, s, L):
        nb = L // 128
        A_ = L // 64
        KA = 2 * A_
        tb = hyc[L]
        NK = A_ + 1
        stq = sp if os.environ.get("KNOSTQ") else pool
        bn = [0]

        def nbank():
            bn[0] += 1
            return banks[2 + bn[0] % 6]

        with ExitStack() as es:
            A = lambda name, shape, dt: es.enter_context(sbt(name, shape, dt))
            wbc = A("hwbc", [128, 3, 1536], F32)
            bbc = A("hbbc", [128, 1536], F32)
            bconst = Buf()
            for k in range(3):
                cx.dma(sp, wbc[:, k, :], hy_cw[layer, k:k + 1, :].broadcast_to([128, 1536]), writes=[bconst],
                       allow_slow_non_contiguous=True)
            cx.dma(sp, bbc[:], hy_cb[layer, 0:1, :].broadcast_to([128, 1536]), writes=[bconst], allow_slow_non_contiguous=True)
            xk = [[(A(f"hxk{k}_{p}", [128, 1536], BF16), Buf()) for k in range(3)] for p in range(2)]
            accs = [(A(f"hacc{p}", [128, 1024], F32), Buf()) for p in range(2)]
            tmps = [A(f"htmp{k}", [128, 1536], BF16) for k in range(3)]
            btmps = [Buf(), Buf(), Buf()]
            bias_bf = A("hbias_bf", [1, 1536], BF16)
            cx.op(dve, lambda e: e.tensor_copy(out=bias_bf[:], in_=bbc[0:1, :]), [bconst], [bconst])
            uus = [(A(f"huu32_{p}", [128, 512], F32), Buf()) for p in range(2)]
            uubs = [(A(f"huub_{p}", [128, 512], BF16), Buf()) for p in range(2)]
            def h1_loads(i):
                par = i % 2
                for k in range(3):
                    t, b = xk[par][k]
                    r0 = PAD + i * 128 + k - 1
                    rd = [ub_[s][i + 1]] + ([ub_[s][i]] if k < 1 else []) + ([ub_[s][i + 2]] if k > 1 else [])
                    cx.dma(sp, t[:], u_[s][r0:r0 + 128, C_HY:C_HY + 1536], reads=rd, writes=[b])

            h1_loads(0)
            for i in range(nb):
                par = i % 2
                if i + 1 < nb:
                    h1_loads(i + 1)
                acc, bacc = accs[par]
                for k in (2,):
                    tk, bk_ = xk[par][k]
                    cx.op(pool, lambda e: e.tensor_tensor(out=tmps[k][:], in0=tk[:], in1=wbc[:, k, :], op=ALU.mult),
                          [bk_, bconst], [btmps[k]])
                for k in (0, 1):
                    tk, bk_ = xk[par][k]
                    cx.op(dve, lambda e: e.tensor_tensor(out=tmps[k][:], in0=tk[:], in1=wbc[:, k, :], op=ALU.mult),
                          [bk_, bconst], [btmps[k]])
                cbk = []
                for cb in range(3):
                    bk = nbank()
                    cbk.append(bk)
                    for k in range(3):
                        cx.op(pe, lambda e: e.matmul(bk[:, :], ident_b[:, :], tmps[k][:, cb * 512:(cb + 1) * 512],
                                                     start=(k == 0), stop=False), [ident_b.b, btmps[k]], [bk.b], inc=False)
                    cx.op(pe, lambda e: e.matmul(bk[:, :], mk_b[0:1, 4, :], bias_bf[0:1, cb * 512:(cb + 1) * 512],
                                                 start=False, stop=True), [mk_b.b, bconst], [bk.b])
                cx.op(act, lambda e: e.activation(out=acc[:, 0:512], in_=cbk[0][:, :], func=AF.Copy), [cbk[0].b], [bacc])
                cx.op(act, lambda e: e.activation(out=acc[:, 512:1024], in_=cbk[2][:, :], func=AF.Copy), [cbk[2].b], [bacc])
                uu, buu = uus[par]
                uub, buub = uubs[par]
                cx.op(dve, lambda e: e.tensor_tensor(out=uu[:], in0=cbk[1][:, :], in1=acc[:, 512:1024], op=ALU.mult),
                      [cbk[1].b, bacc], [buu])
                cx.op(act, lambda e: e.activation(out=uub[:], in_=uu[:], func=AF.Copy), [buu], [buub])
                cx.dma(sp, hxa_[s][i * 128:(i + 1) * 128, :], acc[:, 0:512], reads=[bacc], writes=[])
                cx.dma(sp, huu_[s][i * 128:(i + 1) * 128, :], uu[:], reads=[buu], writes=[])
                cx.dma(sp, huub_[s][i * 128:(i + 1) * 128, :], uub[:], reads=[buub], writes=[])
        cx.barrier()

        with ExitStack() as es:
            A = lambda name, shape, dt: es.enter_context(sbt(name, shape, dt))
            wf = A("hwf", [128, 3, 128], F32)
            wbq = A("hwb", [128, 3, 128], BF16)
            w4f = A("hw4f", [128, 1024], F32)
            w4b = A("hw4b", [128, 1024], BF16)
            fb = A("hfb", [128, 8], F32)
            bw = Buf()
            for k in range(3):
                cx.dma(sp, wf[:, k, :], hy_w123[layer, k], writes=[bw])
            cx.dma(sp, w4f[:], hy_w4[layer], writes=[bw])
            cx.dma(sp, fb[:, 0:4], hy_fb[layer], writes=[bw])
            cx.op(dve, lambda e: e.tensor_copy(out=wbq[:], in_=wf[:]), [bw], [bw])
            cx.op(dve, lambda e: e.tensor_copy(out=w4b[:], in_=w4f[:]), [bw], [bw])
            cx.op(dve, lambda e: e.tensor_scalar(out=fb[:, 4:7], in0=fb[:, 1:4], scalar1=fb[:, 0:1], scalar2=None, op0=ALU.mult),
                  [bw], [bw])
            zts = [(A(f"hzt{p}", [128, 512], BF16), Buf()) for p in range(2)]
            arg = A("harg", [128, 512], F32)
            m1 = A("hm1", [128, 512], F32)
            hs = [(A(f"hh{p}", [128, 512], BF16), Buf()) for p in range(3)]
            dcs = [(A(f"hdc{p}", [128, 512], F32), Buf()) for p in range(2)]
            kfs = [(A(f"hkf{p}", [128, 512], BF16), Buf()) for p in range(2)]
            kbs = [(A(f"hkb{p}", [128, 512], BF16), Buf()) for p in range(2)]
            krs = [(A(f"hkr{p}", [128, 512], BF16), Buf()) for p in range(2)]
            barg, bm1 = Buf(), Buf()
            cx.dma(sp, kern_[s][L:L + 1, :].rearrange("a (p f) -> (a p) f", f=4), zrow[:, 0:4], reads=[zrow.b], writes=[])
            for tg in range(L // 512):
                zt, bzt = zts[tg % 2]
                cx.dma(sp, zt[:], tb["zemb"][:, tg * 512:(tg + 1) * 512], writes=[bzt])
                hin, bhin = zt, bzt
                for li in range(3):
                    bk = nbank()
                    cx.op(pe, lambda e: e.matmul(bk[:, :], wbq[:, li, :], hin[:, :], start=True, stop=True), [bw, bhin], [bk.b])
                    cx.op(act, lambda e: e.activation(out=arg[:], in_=bk[:, :], func=AF.Identity, scale=fb[:, 0:1],
                                                      bias=fb[:, 4 + li:5 + li]), [bk.b, bw], [barg])
                    cx.op(dve, lambda e: e.tensor_scalar(out=m1[:], in0=arg[:], scalar1=PI, scalar2=None, op0=ALU.is_gt),
                          [barg], [bm1])
                    cx.op(dve, lambda e: e.scalar_tensor_tensor(out=arg[:], in0=m1[:], scalar=-2.0 * PI, in1=arg[:],
                                                                op0=ALU.mult, op1=ALU.add), [bm1, barg], [barg])
                    cx.op(dve, lambda e: e.tensor_scalar(out=m1[:], in0=arg[:], scalar1=-PI, scalar2=None, op0=ALU.is_lt),
                          [barg], [bm1])
                    cx.op(dve, lambda e: e.scalar_tensor_tensor(out=arg[:], in0=m1[:], scalar=2.0 * PI, in1=arg[:],
                                                                op0=ALU.mult, op1=ALU.add), [bm1, barg], [barg])
                    ho, bho = hs[li]
                    cx.op(act, lambda e: e.activation(out=ho[:], in_=arg[:], func=AF.Sin), [barg], [bho])
                    hin, bhin = ho, bho
                for tt in range(4):
                    j = tg * 4 + tt
                    dc, bdc = dcs[j % 2]
                    cx.dma(sp, dc[:], tb["decay"][j * 128:(j + 1) * 128, :], writes=[bdc])
                    bk = nbank()
                    cx.op(pe, lambda e: e.matmul(bk[:, :], hin[:, tt * 128:(tt + 1) * 128], w4b[:, 0:512], start=True, stop=True),
                          [bhin, bw], [bk.b])
                    kf, bkf = kfs[j % 2]
                    cx.op(dve, lambda e: e.tensor_tensor(out=kf[:], in0=bk[:, :], in1=dc[:], op=ALU.mult), [bk.b, bdc], [bkf])
                    cx.dma(sp, kern_[s][j * 128:(j + 1) * 128, :], kf[:], reads=[bkf], writes=[])
                    bk = nbank()
                    cx.op(pe, lambda e: e.matmul(bk[:, :], hin[:, tt * 128:(tt + 1) * 128], w4b[:, 512:1024], start=True, stop=True),
                          [bhin, bw], [bk.b])
                    kb_, bkb = kbs[j % 2]
                    cx.op(dve, lambda e: e.tensor_tensor(out=kb_[:], in0=bk[:, :], in1=dc[:], op=ALU.mult), [bk.b, bdc], [bkb])
                    bk2 = nbank()
                    cx.op(pe, lambda e: e.matmul(bk2[:, :], anti_b[:, :], kb_[:, :], start=True, stop=True), [anti_b.b, bkb], [bk2.b])
                    kr, bkr = krs[j % 2]
                    cx.op(act, lambda e: e.activation(out=kr[:], in_=bk2[:, :], func=AF.Copy), [bk2.b], [bkr])
                    base = 2 * L - 128 * j - 127
                    nrow = 127 if j == 0 else 128
                    cx.dma(sp, kern_[s][base:base + nrow, :], kr[0:nrow, :], reads=[bkr], writes=[])
        cx.barrier()

        with ExitStack() as es:
            A = lambda name, shape, dt: es.enter_context(sbt(name, shape, dt))
            fa = A("hfa", [128, 2, 128], BF16)
            bfa = Buf()
            cx.dma(sp, fa[:], tb["FA"][:, :, :], writes=[bfa])
            xins = [(A(f"hxin{p}", [128, 8, 512], BF16), Buf()) for p in range(2)]
            yts = [(A(f"hyt{p}", [128, 2, 8, 512], BF16), Buf()) for p in range(2)]
            for p in range(2):
                cx.op(dve, lambda e: e.memset(xins[p][0][:], 0.0), [], [xins[p][1]])
            q = 0
            ev = 0
            for (srcd, kin, dst) in ((huub_[s], A_, yu_[s]), (kern_[s], KA, yk_[s])):
                sv = srcd.rearrange("(a b) c -> a b c", b=64)
                def h3_load(bc, q):
                    xin, bxin = xins[q % 2]
                    cx.dma(sp, xin[0:kin, :, :], sv[:, bc * 8:(bc + 1) * 8, :], writes=[bxin])

                h3_load(0, q)
                for bc in range(8):
                    xin, bxin = xins[q % 2]
                    yt, byt = yts[q % 2]
                    q += 1
                    if bc + 1 < 8:
                        h3_load(bc + 1, q)
                    for j in range(8):
                        for ri in range(2):
                            bk = nbank()
                            cx.op(pe, lambda e: e.matmul(bk[:, :], fa[:, ri, :], xin[:, j, :], start=True, stop=True),
                                  [bfa, bxin], [bk.b])
                            ev += 1
                            if ev % 2:
                                cx.op(act, lambda e: e.activation(out=yt[:, ri, j, :], in_=bk[:, :], func=AF.Copy), [bk.b], [byt])
                            else:
                                cx.op(dve, lambda e: e.tensor_copy(out=yt[:, ri, j, :], in_=bk[:, :]), [bk.b], [byt])
                    for ri in range(2):
                        cx.dma(stq, dst[ri, bc * 8:(bc + 1) * 8, 0:NK, :].rearrange("b k c -> k b c"), yt[0:NK, ri, :, :],
                               reads=[byt], writes=[])
        cx.barrier()

        nkc = (NK + 7) // 8
        nks = [min(8, NK - kc * 8) for kc in range(nkc)]
        with ExitStack() as es:
            A = lambda name, shape, dt: es.enter_context(sbt(name, shape, dt))
            ycs = [(A(f"hyc{p}", [128, 8, 512], BF16), Buf()) for p in range(2)]
            gbs = [(A(f"hgb{p}", [128, 2, 8, 128], BF16), Buf()) for p in range(2)]
            kts = [(A(f"hkt{p}", [128, 2, 8, 512], BF16), Buf()) for p in range(2)]
            ykv = yk_[s].rearrange("r b k c -> (r b) k c")
            ev = 0
            def h4_loads(kc):
                yc, byc = ycs[kc % 2]
                gb, bgb = gbs[kc % 2]
                nk = nks[kc]
                cx.dma(sp, yc[:, 0:nk, :], ykv[:, kc * 8:kc * 8 + nk, :], writes=[byc])
                cx.dma(sp, gb[:, 0, 0:nk, :], tb["GB"][kc * 8:kc * 8 + nk].rearrange("k p m -> p k m"), writes=[bgb])
                cx.dma(sp, gb[:, 1, 0:nk, :], tb["GBsw"][kc * 8:kc * 8 + nk].rearrange("k p m -> p k m"), writes=[bgb])

            h4_loads(0)
            for kc in range(nkc):
                yc, byc = ycs[kc % 2]
                gb, bgb = gbs[kc % 2]
                kt, bkt = kts[kc % 2]
                if kc + 1 < nkc:
                    h4_loads(kc + 1)
                nk = nks[kc]
                for j in range(nk):
                    for w in range(2):
                        bk = nbank()
                        cx.op(pe, lambda e: e.matmul(bk[:, :], gb[:, w, j, :], yc[:, j, :], start=True, stop=True), [bgb, byc], [bk.b])
                        ev += 1
                        if ev % 2:
                            cx.op(act, lambda e: e.activation(out=kt[:, w, j, :], in_=bk[:, :], func=AF.Copy), [bk.b], [bkt])
                        else:
                            cx.op(dve, lambda e: e.tensor_copy(out=kt[:, w, j, :], in_=bk[:, :]), [bk.b], [bkt])
                cx.dma(stq, khd_[s][:, kc * 8:kc * 8 + nk, :], kt[:, 0, 0:nk, :], reads=[bkt], writes=[])
                cx.dma(stq, kswd_[s][:, kc * 8:kc * 8 + nk, :], kt[:, 1, 0:nk, :], reads=[bkt], writes=[])
        cx.barrier()

        with ExitStack() as es:
            A = lambda name, shape, dt: es.enter_context(sbt(name, shape, dt))
            ycs = [(A(f"hyc5{p}", [128, 8, 512], BF16), Buf()) for p in range(2)]
            gbs = [(A(f"hgb5{p}", [128, 3, 8, 128], BF16), Buf()) for p in range(2)]
            khs = [(A(f"hkh5{p}", [128, 2, 8, 512], BF16), Buf()) for p in range(2)]
            zts = [(A(f"hzt5{p}", [128, 8, 512], BF16), Buf()) for p in range(2)]
            p1s = [(A(f"hp1{p}", [128, 512], BF16), Buf()) for p in range(4)]
            p2s = [(A(f"hp2{p}", [128, 512], BF16), Buf()) for p in range(4)]
            yuv = yu_[s].rearrange("r b k c -> (r b) k c")
            n = 0
            def h5_loads(kc):
                yc, byc = ycs[kc % 2]
                gb, bgb = gbs[kc % 2]
                kh, bkh = khs[kc % 2]
                nk = nks[kc]
                cx.dma(sp, yc[:, 0:nk, :], yuv[:, kc * 8:kc * 8 + nk, :], writes=[byc])
                for w, nm in enumerate(("GB", "GP1", "GP2")):
                    cx.dma(sp, gb[:, w, 0:nk, :], tb[nm][kc * 8:kc * 8 + nk].rearrange("k p m -> p k m"), writes=[bgb])
                cx.dma(sp, kh[:, 0, 0:nk, :], khd_[s][:, kc * 8:kc * 8 + nk, :], writes=[bkh])
                cx.dma(sp, kh[:, 1, 0:nk, :], kswd_[s][:, kc * 8:kc * 8 + nk, :], writes=[bkh])

            h5_loads(0)
            for kc in range(nkc):
                yc, byc = ycs[kc % 2]
                gb, bgb = gbs[kc % 2]
                kh, bkh = khs[kc % 2]
                zt, bzt = zts[kc % 2]
                if kc + 1 < nkc:
                    h5_loads(kc + 1)
                nk = nks[kc]
                pslots = {}

                def h5_front(j):
                    nonlocal n
                    bk = nbank()
                    cx.op(pe, lambda e: e.matmul(bk[:, :], gb[:, 0, j, :], yc[:, j, :], start=True, stop=True), [bgb, byc], [bk.b])
                    p1, bp1 = p1s[n % 4]
                    p2, bp2 = p2s[n % 4]
                    n += 1
                    cx.op(dve, lambda e: e.tensor_tensor(out=p1[:], in0=bk[:, :], in1=kh[:, 0, j, :], op=ALU.mult), [bk.b, bkh], [bp1])
                    cx.op(dve, lambda e: e.tensor_tensor(out=p2[:], in0=bk[:, :], in1=kh[:, 1, j, :], op=ALU.mult), [bk.b, bkh], [bp2])
                    pslots[j] = (p1, bp1, p2, bp2)

                def h5_back(j):
                    p1, bp1, p2, bp2 = pslots[j]
                    bz = nbank()
                    cx.op(pe, lambda e: e.matmul(bz[:, :], gb[:, 1, j, :], p1[:, :], start=True, stop=False), [bgb, bp1], [bz.b], inc=False)
                    cx.op(pe, lambda e: e.matmul(bz[:, :], gb[:, 2, j, :], p2[:, :], start=False, stop=True), [bgb, bp2], [bz.b])
                    cx.op(act, lambda e: e.activation(out=zt[:, j, :], in_=bz[:, :], func=AF.Copy), [bz.b], [bzt])

                for j in range(nk + 2):
                    if j < nk:
                        h5_front(j)
                    if j >= 2:
                        h5_back(j - 2)
                for ri in range(2):
                    cx.dma(stq, zd_[s][ri, kc * 8:kc * 8 + nk, :, :].rearrange("k b c -> b k c"), zt[ri * 64:(ri + 1) * 64, 0:nk, :],
                           reads=[bzt], writes=[])
        cx.barrier()

        with ExitStack() as es:
            A = lambda name, shape, dt: es.enter_context(sbt(name, shape, dt))
            fp = A("hfp", [128, 2, 128], BF16)
            dbc = A("hdbc", [128, 512], F32)
            bfp = Buf()
            cx.dma(sp, fp[:], tb["FP"][:, :, :], writes=[bfp])
            cx.dma(sp, dbc[:], hy_dd[layer, 0:1, :].broadcast_to([128, 512]), writes=[bfp], allow_slow_non_contiguous=True)
            zrs = [(A(f"hzr{p}", [128, 2, 8, 512], BF16), Buf()) for p in range(2)]
            xas = [(A(f"hxa6{p}", [128, 8, 512], F32), Buf()) for p in range(2)]
            uu6 = [(A(f"huu6{p}", [128, 8, 512], F32), Buf()) for p in range(2)]
            ots = [(A(f"hot{p}", [128, 8, 512], F32), Buf()) for p in range(2)]
            otbs = [(A(f"hotb{p}", [128, 8, 512], BF16), Buf()) for p in range(2)]
            for p in range(2):
                cx.op(dve, lambda e: e.memset(zrs[p][0][:], 0.0), [], [zrs[p][1]])
            xav = hxa_[s].rearrange("(a b) c -> a b c", b=64)
            uuv = huu_[s].rearrange("(a b) c -> a b c", b=64)
            ybv = ybr_[s].rearrange("(a b) c -> a b c", b=64)
            def h6_loads(bc):
                zr, bzr = zrs[bc % 2]
                xa, bxa = xas[bc % 2]
                uu, buu = uu6[bc % 2]
                for ri in range(2):
                    cx.dma(sp, zr[0:NK, ri, :, :], zd_[s][ri, 0:NK, bc * 8:(bc + 1) * 8, :], writes=[bzr])
                cx.dma(sp, xa[0:A_, :, :], xav[:, bc * 8:(bc + 1) * 8, :], writes=[bxa])
                cx.dma(sp, uu[0:A_, :, :], uuv[:, bc * 8:(bc + 1) * 8, :], writes=[buu])

            h6_loads(0)
            for bc in range(8):
                zr, bzr = zrs[bc % 2]
                xa, bxa = xas[bc % 2]
                uu, buu = uu6[bc % 2]
                ot, bot = ots[bc % 2]
                if bc + 1 < 8:
                    h6_loads(bc + 1)
                for j in range(8):
                    bk = nbank()
                    cx.op(pe, lambda e: e.matmul(bk[:, :], fp[:, 0, :], zr[:, 0, j, :], start=True, stop=False), [bfp, bzr], [bk.b], inc=False)
                    cx.op(pe, lambda e: e.matmul(bk[:, :], fp[:, 1, :], zr[:, 1, j, :], start=False, stop=True), [bfp, bzr], [bk.b])
                    cx.op(dve, lambda e: e.tensor_tensor(out=ot[0:A_, j, :], in0=uu[0:A_, j, :], in1=dbc[0:A_, :], op=ALU.mult),
                          [buu, bfp], [bot])
                    cx.op(dve, lambda e: e.tensor_tensor(out=ot[0:A_, j, :], in0=ot[0:A_, j, :], in1=bk[0:A_, :], op=ALU.add),
                          [bot, bk.b], [bot])
                    cx.op(dve, lambda e: e.tensor_tensor(out=ot[0:A_, j, :], in0=ot[0:A_, j, :], in1=xa[0:A_, j, :], op=ALU.mult),
                          [bot, bxa], [bot])
                blks = [ybb_[s][t][2] for t in range(nb)]
                otb, botb = otbs[bc % 2]
                cx.op(act, lambda e: e.activation(out=otb[0:A_, :, :], in_=ot[0:A_, :, :], func=AF.Copy), [bot], [botb])
                cx.dma(stq, ybv[:, bc * 8:(bc + 1) * 8, 1536:2048], otb[0:A_, :, :], reads=[botb], writes=blks)

    rel_bias = dram("rel_bias", [128, 128], F32, "ExternalInput")
    oh8_d = dram("oh8", [128, 512], F32, "ExternalInput")
    mneg_d = dram("mneg", [8, 512], F32, "ExternalInput")
    att_sink = dram("att_sink", [depth, 8], F32, "ExternalInput")
    bd_d = dram("bias_vec", [8, 512], F32)
    biasT = T(nc, "biasT", [128, 8, 3, 128], BF16)
    def setup_bias():
        with ExitStack() as es:
            rb = es.enter_context(sbt("rb", [128, 128], F32))
            oh8 = es.enter_context(sbt("oh8s", [128, 512], F32))
            mneg = es.enter_context(sbt("mnegs", [8, 512], F32))
            antid = es.enter_context(sbt("antids", [128, 128], F32))
            bv = es.enter_context(sbt("bv", [8, 512], F32))
            hk = es.enter_context(sbt("hk", [128, 8, 3, 128], F32))
            brb, boh, bmn, ban, bbv, bhk, bbd = (Buf() for _ in range(7))
            cx.dma(sp, rb[:], rel_bias[:, :], writes=[brb])
            cx.dma(sp, oh8[:], oh8_d[:, :], writes=[boh])
            cx.dma(sp, mneg[:], mneg_d[:, :], writes=[bmn])
            cx.dma(sp, antid[:], antid_d[:, :], writes=[ban])
            rbh = es.enter_context(sbt("rbh", [128, 128], BF16))
            rbl = es.enter_context(sbt("rbl", [128, 128], BF16))
            rbr = es.enter_context(sbt("rbr", [128, 128], F32))
            oh8b = es.enter_context(sbt("oh8b", [128, 512], BF16))
            antb = es.enter_context(sbt("antb", [128, 128], BF16))
            hkh = es.enter_context(sbt("hkh", [128, 3072], BF16))
            hkl = es.enter_context(sbt("hkl", [128, 3072], BF16))
            hkr = es.enter_context(sbt("hkr", [128, 3072], F32))
            bhl = Buf()
            cx.op(dve, lambda e: e.tensor_copy(out=rbh[:], in_=rb[:]), [brb], [bhl])
            cx.op(dve, lambda e: e.tensor_tensor(out=rbr[:], in0=rb[:], in1=rbh[:], op=ALU.subtract), [brb, bhl], [bhl])
            cx.op(dve, lambda e: e.tensor_copy(out=rbl[:], in_=rbr[:]), [bhl], [bhl])
            cx.op(dve, lambda e: e.tensor_copy(out=oh8b[:], in_=oh8[:]), [boh], [boh])
            cx.op(dve, lambda e: e.tensor_copy(out=antb[:], in_=antid[:]), [ban], [ban])
            bk = banks[2]
            cx.op(pe, lambda e: e.matmul(bk[:, :], rbh[:, :], oh8b[:, :], start=True, stop=False), [bhl, boh], [bk.b], inc=False)
            cx.op(pe, lambda e: e.matmul(bk[:, :], rbl[:, :], oh8b[:, :], start=False, stop=True), [bhl, boh], [bk.b])
            cx.op(dve, lambda e: e.tensor_tensor(out=bv[:], in0=bk[0:8, :], in1=mneg[:], op=ALU.add), [bk.b, bmn], [bbv])
            cx.dma(sp, bd_d[:, :], bv[:], reads=[bbv], writes=[bbd])
            for h in range(8 if stop > 0 else 0):
                for j in range(3):
                    src = bass.AP(bd_d.tensor, h * 512 + (2 - j) * 128, [[1, 128], [1, 128]])
                    cx.dma(sp, hk[:, h, j, :], src, reads=[bbd], writes=[bhk])
            hkf = hk[:].rearrange("p h j q -> p (h j q)")
            btf = biasT[:].rearrange("p h j q -> p (h j q)")
            if stop > 1:
                cx.op(dve, lambda e: e.tensor_copy(out=hkh[:], in_=hkf), [bhk], [bhl])
                cx.op(dve, lambda e: e.tensor_tensor(out=hkr[:], in0=hkf, in1=hkh[:], op=ALU.subtract), [bhk, bhl], [bhl])
                cx.op(dve, lambda e: e.tensor_copy(out=hkl[:], in_=hkr[:]), [bhl], [bhl])
            for c in range(6 if stop > 1 else 0):
                bk = banks[3 + c % 2]
                cx.op(pe, lambda e: e.matmul(bk[:, :], antb[:, :], hkh[:, c * 512:(c + 1) * 512], start=True, stop=False),
                      [ban, bhl], [bk.b], inc=False)
                cx.op(pe, lambda e: e.matmul(bk[:, :], antb[:, :], hkl[:, c * 512:(c + 1) * 512], start=False, stop=True),
                      [ban, bhl], [bk.b])
                cx.op(dve, lambda e: e.tensor_copy(out=btf[:, c * 512:(c + 1) * 512], in_=bk[:, :]), [bk.b], [biasT.b])
        cx.barrier()

    def phase_att(layer, s, L):
        nb = L // 128
        stq = sp if os.environ.get("KNOSTQ") else pool
        with ExitStack() as es:
            qT = es.enter_context(sbt("qT", [128, 4, L], BF16))
            kT = es.enter_context(sbt("kT", [128, 2, L], BF16))
            vb = es.enter_context(sbt("vb", [128, nb, 2, 65], BF16))
            esk = es.enter_context(sbt("esk", [128, 8], F32))
            pT0 = es.enter_context(sbt("pT0", [128, 512], BF16))
            pT1 = es.enter_context(sbt("pT1", [128, 512], BF16))
            pT2 = es.enter_context(sbt("pT2", [128, 512], BF16))
            den = es.enter_context(sbt("den", [128, 16], F32))
            ao0 = es.enter_context(sbt("ao0", [128, 512], BF16))
            ao1 = es.enter_context(sbt("ao1", [128, 512], BF16))
            bq, bk_, bv_, bes, bden = (Buf() for _ in range(5))
            pT3 = es.enter_context(sbt("pT3", [128, 512], BF16))
            pTs = [(pT0, Buf()), (pT1, Buf()), (pT2, Buf()), (pT3, Buf())]
            aos = [(ao0, Buf()), (ao1, Buf())]
            for r in range(4):
                cx.dma(sp, qT[:, r, :], qk_[s][r * 128:(r + 1) * 128, :], reads=qkb_[s], writes=[bq])
            cx.op(dve, lambda e: e.memset(kT[:], 0.0), [], [bk_])
            for g in range(2):
                cx.dma(sp, kT[g * 64:(g + 1) * 64, g, :], qk_[s][512 + g * 64:512 + (g + 1) * 64, :], reads=qkb_[s], writes=[bk_])
            cx.op(dve, lambda e: e.memset(vb[:], 1.0), [], [bv_])
            for i in range(nb):
                cx.dma(sp, vb[:, i, :, 0:64], u_[s][PAD + i * 128:PAD + (i + 1) * 128, C_V:C_V + 128]
                       .rearrange("p (g d) -> p g d", g=2), reads=[ub_[s][i + 1]], writes=[bv_])
            cx.dma(sp, esk[:], att_sink[layer:layer + 1, :].broadcast_to([128, 8]), writes=[bes],
                   allow_slow_non_contiguous=True)
            cx.op(act, lambda e: e.activation(out=esk[:], in_=esk[:], func=AF.Exp), [bes], [bes])
            zt = es.enter_context(sbt("zt", [128, 1024], BF16))
            bz = Buf()
            cx.op(dve, lambda e: e.memset(zt[:], 0.0), [], [bz])
            for i in range(nb):
                if "ssd" not in phases:
                    cx.dma(sp, ybr_[s][i * 128:(i + 1) * 128, 0:1024], zt[:, :], reads=[bz], writes=[ybb_[s][i][0]])
                if "hy" not in phases:
                    cx.dma(sp, ybr_[s][i * 128:(i + 1) * 128, 1536:2048], zt[:, 0:512], reads=[bz], writes=[ybb_[s][i][2]])
            bi = 0
            pi = 0
            for i in range(nb if stop > 2 else 0):
                obanks = [banks[6], banks[7]]
                pairs = [(g, j) for g in range(2) for j in range(3) if 0 <= i + j - 1 < nb]
                firsts = {0: True, 1: True}
                slots = {}

                def att_front(q):
                    nonlocal bi, pi
                    g, j = pairs[q]
                    kb = i + j - 1
                    sb_ = banks[2 + bi % 4]
                    bi += 1
                    cx.op(pe, lambda e: e.matmul(sb_[:, :], kT[:, g, kb * 128:(kb + 1) * 128], qT[:, :, i * 128:(i + 1) * 128],
                                                 start=True, stop=False), [bk_, bq], [sb_.b], inc=False)
                    cx.op(pe, lambda e: e.matmul(sb_[:, :], ident_b[:, :], biasT[:, g * 4:(g + 1) * 4, j, :],
                                                 start=False, stop=True), [ident_b.b, biasT.b], [sb_.b])
                    pT, pb = pTs[pi % 4]
                    pi += 1
                    cx.op(act, lambda e: e.activation(out=pT[:, :], in_=sb_[:, :], func=AF.Exp, scale=0.125), [sb_.b], [pb])
                    slots[q] = (pT, pb)

                def att_back(q):
                    g, j = pairs[q]
                    kb = i + j - 1
                    ob = obanks[g]
                    pT, pb = slots[q]
                    for r in range(4):
                        cx.op(pe, lambda e: e.matmul(ob[:, r * 65:(r + 1) * 65], pT[:, r * 128:(r + 1) * 128],
                                                     vb[:, kb, g, :], start=firsts[g], stop=True),
                              [pb, bv_], [ob.b], inc=(r == 3))
                        firsts[g] = False

                npair = len(pairs)
                for q in range(npair + 2):
                    if q < npair:
                        att_front(q)
                    if q >= 2:
                        att_back(q - 2)
                ao, aob = aos[i % 2]
                for g in range(2 if stop > 4 else 0):
                    ob = obanks[g]
                    ov = ob[:, 0:260].rearrange("p (r c) -> p r c", r=4)
                    cx.op(dve, lambda e: e.tensor_tensor(out=den[:, g * 4:(g + 1) * 4], in0=ov[:, :, 64],
                                                         in1=esk[:, g * 4:(g + 1) * 4], op=ALU.add), [ob.b, bes], [bden])
                    cx.op(dve, lambda e: e.reciprocal(out=den[:, 8 + g * 4:8 + (g + 1) * 4], in_=den[:, g * 4:(g + 1) * 4]),
                          [bden], [bden])
                    cx.op(dve, lambda e: e.tensor_tensor(
                        out=ao[:, g * 256:(g + 1) * 256].rearrange("p (r d) -> p r d", r=4), in0=ov[:, :, 0:64],
                        in1=den[:, 8 + g * 4:8 + (g + 1) * 4].unsqueeze(2).broadcast_to([128, 4, 64]), op=ALU.mult),
                          [ob.b, bden], [aob])
                cx.dma(stq, ybr_[s][i * 128:(i + 1) * 128, 1024:1536], ao[:, :], reads=[aob], writes=[ybb_[s][i][1]])

    setup_done = []
    for layer in range(depth):
        fin_args = []
        for s, L in enumerate(seqs):
            if layer == 0:
                xsrc, xsrc_b = xin[s], [Buf() for _ in range(L // 128)]
            else:
                xsrc, xsrc_b = xs_[s][(layer - 1) % 2], xb_[s][(layer - 1) % 2]
            xdst, xdst_b = xs_[s][layer % 2], xb_[s][layer % 2]
            if "inproj" in phases:
                phase_inproj(layer, s, L, xsrc, xsrc_b)
                cx.barrier()
            if "ssd" in phases:
                phase_ssd(layer, s, L)
                cx.barrier()
            if "hy" in phases:
                phase_hy(layer, s, L)
                cx.barrier()
            if "att" in phases:
                if not setup_done:
                    setup_bias()
                    setup_done.append(1)
                phase_att(layer, s, L)
                cx.barrier()
            fin_args.append((s, L, xsrc, xsrc_b, xdst, xdst_b))
        if "final" in phases:
            phase_final(layer, fin_args, last=(layer == depth - 1))
            cx.barrier()
    cx.finish()
    return nc


_IN_SIZES = [1024, 1536, 32, 512, 128, 128, 512, 1536, 512]


def _perm_cols():
    o, off = {}, 0
    for n, sz in zip(["z", "xbc", "dt", "q", "k", "v", "ga", "hy", "gh"], _IN_SIZES):
        o[n] = np.arange(off, off + sz)
        off += sz
    q = o["q"].reshape(8, 64)
    qp = np.concatenate([np.concatenate([q[i], q[4 + i]]) for i in range(4)])
    return np.concatenate([o["z"], o["xbc"], o["v"], o["ga"], o["hy"], o["gh"], o["dt"], qp, o["k"]])


def layer_params(p, depth):
    f = lambda a: np.ascontiguousarray(np.asarray(a, np.float32))
    return {
        "ssd_conv_w": f(p["ssd_conv_w"][:depth]),
        "ssd_conv_b": f(p["ssd_conv_b"][:depth])[:, None, :],
        "ssd_dt_bias": f(p["ssd_dt_bias"][:depth]).reshape(depth, 1, 32),
        "ssd_a_log": f(p["ssd_a_log"][:depth]).reshape(depth, 1, 32),
        "ssd_d": f(p["ssd_d"][:depth])[:, None, :],
        "hy_conv_w": f(p["hy_conv_w"][:depth]),
        "hy_conv_b": f(p["hy_conv_b"][:depth])[:, None, :],
        "hy_w123": np.ascontiguousarray(np.stack([
            np.pad(f(p["hy_w1"][:depth]), ((0, 0), (0, 95), (0, 64))),
            np.pad(f(p["hy_w2"][:depth]), ((0, 0), (0, 64), (0, 64))),
            np.pad(f(p["hy_w3"][:depth]), ((0, 0), (0, 64), (0, 64)))], axis=1)),
        "hy_w4": np.pad(f(p["hy_w4"][:depth]), ((0, 0), (0, 64), (0, 0))),
        "hy_fb": np.ascontiguousarray(np.pad(np.stack([f(p["hy_freq"][:depth]), f(p["hy_b1"][:depth]), f(p["hy_b2"][:depth]),
                                                       f(p["hy_b3"][:depth])], axis=-1), ((0, 0), (0, 64), (0, 0)))),
        "hy_d": f(p["hy_d"][:depth])[:, None, :],
    }


def kernel(x_prompt, x_sample, rel_bias, norm_g, w_in, ssd_conv_w, ssd_conv_b, ssd_dt_bias, ssd_a_log,
           ssd_d, ssd_norm_g, att_sink, att_norm_g, hy_conv_w, hy_conv_b, hy_w1, hy_b1, hy_w2, hy_b2,
           hy_w3, hy_b3, hy_w4, hy_freq, hy_d, hy_norm_g, w_out, final_norm_g):
    x_prompt, x_sample = np.asarray(x_prompt, np.float32), np.asarray(x_sample, np.float32)
    n = 8
    Lp, Ls = x_prompt.shape[1], x_sample.shape[1]
    depth = int(np.asarray(w_in).shape[0])
    nc = build([Lp, Ls], depth=depth)
    p = dict(ssd_conv_w=ssd_conv_w, ssd_conv_b=ssd_conv_b, ssd_dt_bias=ssd_dt_bias, ssd_a_log=ssd_a_log, ssd_d=ssd_d,
             hy_conv_w=hy_conv_w, hy_conv_b=hy_conv_b, hy_w1=hy_w1, hy_b1=hy_b1, hy_w2=hy_w2, hy_b2=hy_b2,
             hy_w3=hy_w3, hy_b3=hy_b3, hy_w4=hy_w4, hy_freq=hy_freq, hy_d=hy_d)
    p = {k: np.asarray(v, np.float32) for k, v in p.items()}
    shared = {
        "w_in": np.ascontiguousarray(np.asarray(w_in, np.float32)[:, :, _perm_cols()]),
        "w_out": np.asarray(w_out, np.float32),
        "norm_g": np.asarray(norm_g, np.float32),
        "brg": np.ascontiguousarray(np.concatenate([np.asarray(ssd_norm_g), np.asarray(att_norm_g),
                                                    np.asarray(hy_norm_g)], axis=1).astype(np.float32)),
        "fin_g": np.asarray(final_norm_g, np.float32)[None],
        "rel_bias": np.pad(np.asarray(rel_bias, np.float32), ((0, 96), (0, 120))),
        "att_sink": np.asarray(att_sink, np.float32),
    }
    shared.update(host_consts())
    shared.update(layer_params(p, depth))
    for L in sorted({Lp, Ls}):
        shared.update(hy_consts(L))
    in_maps = [dict(shared, x0=np.ascontiguousarray(x_prompt[c]), x1=np.ascontiguousarray(x_sample[c])) for c in range(n)]
    res = run_bass_kernel_spmd(nc, in_maps, core_ids=list(range(n)))
    yp = np.stack([np.asarray(r["y0"]) for r in res.results], 0)
    ys = np.stack([np.asarray(r["y1"]) for r in res.results], 0)
    return (yp.astype(np.float32), ys.astype(np.float32))
```
